# Optimizing a Trainium2 kernel written in Bass

```python
import jax, jax.numpy as jnp
from jax import lax
import numpy as np

D_MODEL = 1024
BATCH = 8
SEQ = 8192
DEPTH = 1
DEC_BATCH = 128
DEC_SEQ = 1
PAST_LEN = 8192
PAGE_SIZE = 128

ATT_GROUPS = ((128, 1), (512, 4), (2048, 16))
N_ATT_GROUPS = 3
ATT_HEADS = 8
ATT_HEAD_DIM = 64
ATT_W = ATT_HEADS * ATT_HEAD_DIM
ATT_BLOCK = 128
HG_DK = 128
HG_HEADS = D_MODEL // HG_DK
HG_DV = D_MODEL // HG_HEADS
HG_W = HG_HEADS * HG_DK
HG_VW = HG_HEADS * HG_DV
HG_CHUNK = 64
NORM_EPS = 1e-6
NEG_INF = -1e30
OFF_ATT_GATE = N_ATT_GROUPS * 3 * ATT_W
OFF_HG_Q = OFF_ATT_GATE + ATT_W
OFF_HG_F = OFF_HG_Q + HG_W
OFF_HG_I = OFF_HG_F + HG_W
OFF_HG_GATE = OFF_HG_I + HG_VW
OFF_MERGE_A = OFF_HG_GATE + HG_VW
OFF_MERGE_B = OFF_MERGE_A + D_MODEL
IN_W = OFF_MERGE_B + D_MODEL

kernel_name = 'hybrid_dilated_attn_hgrn2_step'


def rms_norm(x, w):
    xf = x.astype(jnp.float32)
    y = xf * lax.rsqrt(jnp.mean(xf * xf, axis=-1, keepdims=True) + NORM_EPS)
    return y * w.astype(jnp.float32)


def alibi_slopes():
    n = N_ATT_GROUPS * ATT_HEADS
    s = 2.0 ** (-8.0 * np.arange(1, n + 1) / n)
    return s.astype(np.float32).reshape(N_ATT_GROUPS, ATT_HEADS)


def cols(h, w, lo, width):
    return jnp.einsum('ntd,de->nte', h, w[:, lo:lo + width])


def dilated_attn_prompt(q, k, v, window, dil, slopes):
    n_seq, t_len, n_h, e = q.shape
    n_back = window // dil
    blk = ATT_BLOCK
    span = dil * blk
    t_pad = -(-t_len // span) * span
    nb = t_pad // span

    def to_blocks(a):
        a = jnp.pad(a, ((0, 0), (0, t_pad - t_len), (0, 0), (0, 0)))
        return a.reshape(n_seq, nb, blk, dil, n_h, e)

    def with_prev(a):
        prev = jnp.pad(a[:, :-1], ((0, 0), (1, 0), (0, 0), (0, 0), (0, 0), (0, 0)))
        return jnp.concatenate([prev, a], axis=2)

    qb = to_blocks(q)
    kc = with_prev(to_blocks(k))
    vc = with_prev(to_blocks(v))
    steps = np.arange(blk)[:, None] + blk - np.arange(2 * blk)[None, :]
    band = (steps >= 0) & (steps <= n_back)
    exists = (np.arange(nb)[:, None] * blk + np.arange(2 * blk)[None, :] - blk) >= 0
    mask = (band[None] & exists[:, None])[None, :, None, None]
    bias = jnp.asarray(-slopes[:, None, None] * (steps * dil).astype(np.float32))
    s = jnp.einsum('nbqrhe,nbkrhe->nbrhqk', qb, kc).astype(jnp.float32) * (ATT_HEAD_DIM ** -0.5) + bias
    s = jnp.where(mask, s, NEG_INF)
    mx = jnp.max(s, axis=-1, keepdims=True)
    p = jnp.exp(s - mx)
    den = jnp.sum(p, axis=-1)
    lse = mx[..., 0] + jnp.log(den)
    o = jnp.einsum('nbrhqk,nbkrhe->nbrhqe', p, vc.astype(jnp.float32)) / den[..., None]
    o = o.transpose(0, 1, 4, 2, 3, 5).reshape(n_seq, t_pad, n_h, e)[:, :t_len]
    lse = lse.transpose(0, 1, 4, 2, 3).reshape(n_seq, t_pad, n_h)[:, :t_len]
    return o, lse


def dilated_attn_sample(q, k, v, kv_cache, window, dil, slopes):
    n_seq, s_len, n_h, e = q.shape
    rows = kv_cache.shape[1]
    n_back = window // dil
    keys = jnp.concatenate([kv_cache[:, :, 0], k], axis=1)
    vals = jnp.concatenate([kv_cache[:, :, 1], v], axis=1)
    steps = np.arange(n_back + 1)
    idx = (rows + np.arange(s_len))[:, None] - dil * steps[None, :]
    valid = idx >= 0
    idx = np.maximum(idx, 0)
    kg = keys[:, idx]
    vg = vals[:, idx]
    bias = jnp.asarray(-slopes[:, None] * (steps * dil).astype(np.float32)[None, :])
    s = jnp.einsum('nshe,nsjhe->nshj', q, kg).astype(jnp.float32) * (ATT_HEAD_DIM ** -0.5) + bias
    s = jnp.where(valid[None, :, None, :], s, NEG_INF)
    mx = jnp.max(s, axis=-1, keepdims=True)
    p = jnp.exp(s - mx)
    den = jnp.sum(p, axis=-1)
    lse = mx[..., 0] + jnp.log(den)
    o = jnp.einsum('nshj,nsjhe->nshe', p, vg.astype(jnp.float32)) / den[..., None]
    return o, lse


def hgrn2_chunked(q, k, v, logf, s0):
    n_seq, t_len, n_h, dk = q.shape
    dv = v.shape[-1]
    c = min(HG_CHUNK, t_len)
    t_pad = -(-t_len // c) * c
    nc = t_pad // c
    pad = ((0, 0), (0, t_pad - t_len), (0, 0), (0, 0))

    def chunks(a):
        a = jnp.pad(a, pad)
        return jnp.moveaxis(a.reshape(n_seq, nc, c, n_h, a.shape[-1]), 1, 0)

    causal = jnp.asarray(np.tril(np.ones((c, c), dtype=bool)))[None, :, :, None, None]

    def step(state, inp):
        qc, kc, vc, gc = inp
        b = jnp.cumsum(gc, axis=1)
        o_inter = jnp.einsum('nthk,nhkv->nthv', qc * jnp.exp(b), state)
        diff = b[:, :, None] - b[:, None, :]
        decay = jnp.where(causal, jnp.exp(jnp.minimum(diff, 0.0)), 0.0)
        a = jnp.einsum('nthk,nshk,ntshk->ntsh', qc, kc, decay)
        o_intra = jnp.einsum('ntsh,nshv->nthv', a, vc)
        b_end = b[:, -1]
        new_state = jnp.exp(b_end)[..., None] * state + jnp.einsum(
            'nshk,nshv->nhkv', kc * jnp.exp(b_end[:, None] - b), vc)
        return new_state, o_inter + o_intra

    s_fin, o = lax.scan(step, s0, (chunks(q), chunks(k), chunks(v), chunks(logf)))
    o = jnp.moveaxis(o, 0, 1).reshape(n_seq, t_pad, n_h, dv)[:, :t_len]
    return o, s_fin


def mixer_layer(x, kv_caches, hg_state, norm_w, w_in, w_att_proj, w_hg_proj, w_out, hg_norm_w, lb):
    n_seq, t_len, _ = x.shape
    dt = x.dtype
    f32 = jnp.float32
    h = rms_norm(x, norm_w).astype(dt)
    slopes = alibi_slopes()
    outs, lses, new_kv = [], [], []
    for g, (win, dil) in enumerate(ATT_GROUPS):
        base = g * 3 * ATT_W
        q, k, v = [cols(h, w_in, base + j * ATT_W, ATT_W).reshape(n_seq, t_len, ATT_HEADS, ATT_HEAD_DIM)
                   for j in range(3)]
        if kv_caches is None:
            o, lse = dilated_attn_prompt(q, k, v, win, dil, slopes[g])
            keep = min(win, t_len)
            new_kv.append(jnp.stack([k[:, t_len - keep:], v[:, t_len - keep:]], axis=2))
        else:
            o, lse = dilated_attn_sample(q, k, v, kv_caches[g], win, dil, slopes[g])
            new_kv.append(jnp.stack([k, v], axis=2))
        outs.append(o)
        lses.append(lse)
    mix = jax.nn.softmax(jnp.stack(lses, axis=0), axis=0)
    att = jnp.einsum('gnth,gnthe->nthe', mix, jnp.stack(outs, axis=0)).reshape(n_seq, t_len, ATT_W)
    att = att * jax.nn.silu(cols(h, w_in, OFF_ATT_GATE, ATT_W).astype(f32))
    hq = jax.nn.silu(cols(h, w_in, OFF_HG_Q, HG_W).astype(f32)).reshape(
        n_seq, t_len, HG_HEADS, HG_DK) * (HG_DK ** -0.5)
    fl = cols(h, w_in, OFF_HG_F, HG_W).astype(f32).reshape(n_seq, t_len, HG_HEADS, HG_DK)
    lbh = lb.reshape(HG_HEADS, HG_DK)
    logf = jnp.log(lbh + (1.0 - lbh) * jax.nn.sigmoid(fl))
    hk = (1.0 - lbh) * jax.nn.sigmoid(-fl)
    hi = cols(h, w_in, OFF_HG_I, HG_VW).astype(f32).reshape(n_seq, t_len, HG_HEADS, HG_DV)
    if hg_state is None:
        s0 = jnp.zeros((n_seq, HG_HEADS, HG_DK, HG_DV), f32)
    else:
        s0 = hg_state.astype(f32)
    ho, s_new = hgrn2_chunked(hq, hk, hi, logf, s0)
    ho = rms_norm(ho, hg_norm_w).reshape(n_seq, t_len, HG_VW)
    ho = ho * jax.nn.silu(cols(h, w_in, OFF_HG_GATE, HG_VW).astype(f32))
    ga = jax.nn.sigmoid(cols(h, w_in, OFF_MERGE_A, D_MODEL).astype(f32))
    gb = jax.nn.sigmoid(cols(h, w_in, OFF_MERGE_B, D_MODEL).astype(f32))
    merged = ga * jnp.einsum('nta,ad->ntd', att, w_att_proj.astype(f32)) + \
        gb * jnp.einsum('ntv,vd->ntd', ho, w_hg_proj.astype(f32))
    y = jnp.einsum('ntd,de->nte', merged.astype(dt), w_out)
    return x + y.astype(dt), new_kv, s_new.astype(dt)


def setup_inputs(seed: int = 0) -> dict:
    key = jax.random.key(seed)
    ks = jax.random.split(key, 16)
    f32 = jnp.float32

    def nrm(k, shape, scale=1.0):
        return jax.random.normal(k, shape, f32) * scale

    rows = [min(w, PAST_LEN) for w, _ in ATT_GROUPS]
    kv_shape = lambda r: (DEPTH, DEC_BATCH, r, 2, ATT_HEADS, ATT_HEAD_DIM)
    return {
        'x_prompt': nrm(ks[0], (BATCH, SEQ, D_MODEL)),
        'x_sample': nrm(ks[1], (DEC_BATCH, DEC_SEQ, D_MODEL)),
        'cache_kv_w128': nrm(ks[2], kv_shape(rows[0])),
        'cache_kv_w512': nrm(ks[3], kv_shape(rows[1])),
        'cache_kv_w2048': nrm(ks[4], kv_shape(rows[2])),
        'state_hgrn': nrm(ks[5], (DEPTH, DEC_BATCH, HG_HEADS, HG_DK, HG_DV), 0.5),
        'norm_w': 1.0 + nrm(ks[6], (DEPTH, D_MODEL), 0.02),
        'w_in': nrm(ks[7], (DEPTH, D_MODEL, IN_W), D_MODEL ** -0.5),
        'w_att_proj': nrm(ks[8], (DEPTH, ATT_W, D_MODEL), ATT_W ** -0.5),
        'w_hg_proj': nrm(ks[9], (DEPTH, HG_VW, D_MODEL), HG_VW ** -0.5),
        'w_out': nrm(ks[10], (DEPTH, D_MODEL, D_MODEL), D_MODEL ** -0.5),
        'hg_norm_w': 1.0 + nrm(ks[11], (DEPTH, HG_DV), 0.02),
        'hg_lb_logits': nrm(ks[12], (DEPTH + 1, HG_W), 0.1),
        'final_norm_w': 1.0 + nrm(ks[13], (D_MODEL,), 0.02),
    }


def reference(x_prompt, x_sample, cache_kv_w128, cache_kv_w512, cache_kv_w2048, state_hgrn,
              norm_w, w_in, w_att_proj, w_hg_proj, w_out, hg_norm_w, hg_lb_logits, final_norm_w):
    lb_all = jnp.cumsum(jax.nn.softmax(hg_lb_logits.astype(jnp.float32), axis=0), axis=0)
    xp, xs = x_prompt, x_sample
    kv_p = [[], [], []]
    kv_s = [[], [], []]
    st_p, st_s = [], []
    for layer in range(DEPTH):
        wts = (norm_w[layer], w_in[layer], w_att_proj[layer], w_hg_proj[layer], w_out[layer],
               hg_norm_w[layer], lb_all[layer])
        xp, new_kv_p, sp = mixer_layer(xp, None, None, *wts)
        caches = (cache_kv_w128[layer], cache_kv_w512[layer], cache_kv_w2048[layer])
        xs, new_kv_s, ss = mixer_layer(xs, caches, state_hgrn[layer], *wts)
        for g in range(N_ATT_GROUPS):
            kv_p[g].append(new_kv_p[g])
            kv_s[g].append(new_kv_s[g])
        st_p.append(sp)
        st_s.append(ss)
    y_prompt = rms_norm(xp, final_norm_w).astype(x_prompt.dtype)
    y_sample = rms_norm(xs, final_norm_w).astype(x_sample.dtype)
    kv_w128_prompt = jnp.stack(kv_p[0], axis=0)
    kv_w512_prompt = jnp.stack(kv_p[1], axis=0)
    kv_w2048_prompt = jnp.stack(kv_p[2], axis=0)
    state_hgrn_prompt = jnp.stack(st_p, axis=0)
    kv_w128_sample = jnp.stack(kv_s[0], axis=0)
    kv_w512_sample = jnp.stack(kv_s[1], axis=0)
    kv_w2048_sample = jnp.stack(kv_s[2], axis=0)
    state_hgrn_sample = jnp.stack(st_s, axis=0)
    return (y_prompt, y_sample, kv_w128_prompt, kv_w512_prompt, kv_w2048_prompt, state_hgrn_prompt,
            kv_w128_sample, kv_w512_sample, kv_w2048_sample, state_hgrn_sample)
```

```python
import numpy as np
import concourse.bass as bass
import concourse.mybir as mybir
from concourse.bass_utils import run_bass_kernel_spmd

F32 = mybir.dt.float32
BF16 = mybir.dt.bfloat16
AF = mybir.ActivationFunctionType
ALU = mybir.AluOpType
AX = mybir.AxisListType

ENGS = ['pe', 'act', 'dve', 'pool', 'sp']


class Buf:
    __slots__ = ('name', 't', 'lw', 'rd', 'dsem')

    def __init__(self, name, t):
        self.name = name
        self.t = t
        self.lw = None
        self.rd = {}
        self.dsem = None

    def __getitem__(self, idx):
        return self.t[idx]


class DSem:
    __slots__ = ('key', 'val', 'h')

    def __init__(self, key, h):
        self.key = key
        self.val = 0
        self.h = h


class Prog:
    def __init__(self, nc, stack):
        self.nc = nc
        self.stack = stack
        self.sem_stack = stack
        self.lists = {e: [] for e in ENGS}
        self.cnt = {e: 0 for e in ENGS}
        self.seen = {e: {} for e in ENGS}
        self.semh = {}
        for e in ENGS:
            self.semh[e] = stack.enter_context(nc.semaphore('s_' + e))
        self.ndsem = 0
        self.alldsem = []
        self.nbuf = 0

    def sb(self, name, shape, dtype):
        t = self.stack.enter_context(self.nc.sbuf_tensor("sb_" + name, list(shape), dtype))
        return Buf(name, t)

    def ps(self, name, shape, dtype=F32):
        t = self.stack.enter_context(self.nc.psum_tensor("ps_" + name, list(shape), dtype))
        return Buf(name, t)

    def wrap(self, name, t):
        return Buf(name, t)

    def _dsem(self, b):
        if b.dsem is None:
            h = self.sem_stack.enter_context(self.nc.semaphore('d%d' % self.ndsem))
            key = ('dma', self.ndsem)
            self.ndsem += 1
            self.semh[key] = h
            b.dsem = DSem(key, h)
            self.alldsem.append(b.dsem)
        return b.dsem

    def emit(self, eng, fn, reads=(), writes=(), dma=None):
        deps = {}

        def add(dep):
            if dep is None:
                return
            k, v = dep
            if deps.get(k, 0) < v:
                deps[k] = v

        for b in reads:
            add(b.lw)
        for b in writes:
            add(b.lw)
            for k, v in b.rd.items():
                add((k, v))
        ds = None
        if dma is not None:
            ds = self._dsem(dma)
            if ds.val:
                add((ds.key, ds.val))
        waits = []
        seen = self.seen[eng]
        for k, v in deps.items():
            if k == eng and eng == 'pe':
                continue
            if seen.get(k, 0) >= v:
                continue
            seen[k] = v
            waits.append((k, v))
        if ds is None:
            self.cnt[eng] += 1
            tok = (eng, self.cnt[eng])
        else:
            ds.val += 16
            tok = (ds.key, ds.val)
        self.lists[eng].append((waits, fn, ds))
        for b in reads:
            if b.rd.get(tok[0], 0) < tok[1]:
                b.rd[tok[0]] = tok[1]
        for b in writes:
            b.lw = tok
            b.rd = {}
        return tok

    def pe(self, fn, r=(), w=()):
        return self.emit('pe', fn, r, w)

    def act(self, fn, r=(), w=()):
        return self.emit('act', fn, r, w)

    def dve(self, fn, r=(), w=()):
        return self.emit('dve', fn, r, w)

    def pool(self, fn, r=(), w=()):
        return self.emit('pool', fn, r, w)

    def dma(self, q, out_ap, in_ap, r=(), w=(), sem_on=None, **kw):
        if sem_on is None:
            sem_on = (list(w) + list(r))[0]
        return self.emit(q, lambda e: e.dma_start(out=out_ap, in_=in_ap, **kw), r, w, dma=sem_on)

    def final_wait(self, eng, bufs):
        deps = {}
        for b in bufs:
            for dep in [b.lw] + list(b.rd.items()):
                if dep is None:
                    continue
                k, v = dep
                if deps.get(k, 0) < v:
                    deps[k] = v
        waits = [(k, v) for k, v in deps.items() if self.seen[eng].get(k, 0) < v]
        for k, v in waits:
            self.seen[eng][k] = v
        self.lists[eng].append((waits, None, None))

    def barrier(self):
        toks = [(e, self.cnt[e]) for e in ENGS if self.cnt[e] > 0]
        toks += [(ds.key, ds.val) for ds in self.alldsem if ds.val > 0]
        for e in ENGS:
            waits = []
            for k, v in toks:
                if k == e and e == 'pe':
                    continue
                if self.seen[e].get(k, 0) < v:
                    self.seen[e][k] = v
                    waits.append((k, v))
            self.lists[e].append((waits, None, None))

    def build(self):
        nc = self.nc
        semh = self.semh
        lists = self.lists
        needed = {e: set() for e in ENGS}
        for e in ENGS:
            for waits, fn, ds in lists[e]:
                for k, v in waits:
                    if k in needed:
                        needed[k].add(v)
        rank = {}
        for e in ENGS:
            rank[e] = {v: i + 1 for i, v in enumerate(sorted(needed[e]))}

        def run(engname):
            def body(e):
                own = semh[engname]
                seq = 0
                myrank = rank[engname]
                for waits, fn, ds in lists[engname]:
                    for k, v in waits:
                        if k in rank:
                            e.wait_ge(semh[k], rank[k][v])
                        else:
                            e.wait_ge(semh[k], v)
                    if fn is None:
                        continue
                    ins = fn(e)
                    if ds is None:
                        seq += 1
                        if seq in myrank:
                            ins.then_inc(own, 1)
                    else:
                        ins.then_inc(ds.h, 16)
            return body

        with nc.Block() as block:
            block.tensor(run('pe'))
            block.scalar(run('act'))
            block.vector(run('dve'))
            block.gpsimd(run('pool'))
            block.sync(run('sp'))


class View:
    def __init__(self, base, ap):
        self.base = base
        self.t = ap
        self.name = base.name

    def __getitem__(self, idx):
        return self.t[idx]

    lw = property(lambda s: s.base.lw, lambda s, v: setattr(s.base, 'lw', v))
    rd = property(lambda s: s.base.rd, lambda s, v: setattr(s.base, 'rd', v))
    dsem = property(lambda s: s.base.dsem, lambda s, v: setattr(s.base, 'dsem', v))

from contextlib import ExitStack

T = 8192
D = 1024
ST = 2048
NST = T // ST
SUB = 512
EPS = 1e-6
GROUPS = ((128, 1), (512, 4), (2048, 16))
NS = 16
NSL = 22


def unit_tokens(g, u):
    d = GROUPS[g][1]
    if d == 1:
        return slice(128 * u, 128 * u + 128)
    if d == 4:
        b, r = divmod(u, 4)
        return slice(512 * b + r, 512 * b + 512, 4)
    return slice(u, 2048, 16)


def build_program(do_prompt=True, do_sample=True, nst=NST):
    nc = bass.Bass("TRN2", target_bir_lowering=False)

    def din(name, shape):
        return nc.dram_tensor(name, list(shape), F32, kind="ExternalInput").ap()

    def dout(name, shape):
        return nc.dram_tensor(name, list(shape), F32, kind="ExternalOutput").ap()

    x_d = din("x", [T, D])
    xs_d = din("xs", [NS, D])
    c_d = [din("c128", [NS, 128, 1024]), din("c512", [NS, 512, 1024]), din("c2048", [NS, 2048, 1024])]
    sh_d = din("sh", [NS, 8, 128, 128])
    win_d = din("w_in", [D, 11264])
    wap_d = din("wap", [512, D])
    whp_d = din("whp", [D, D])
    wout_d = din("wout", [D, D])
    nw_d = din("nw", [128, 8])
    hnw_d = din("hnw", [128, 1])
    lbl_d = din("lbl", [128, 16])
    fnw_d = din("fnw", [128, D])
    nwr_d = din("nwr", [128, D])
    ident_d = din("ident", [128, 128])
    eb_d = din("eb", [128, 12 * 512])
    hmask_d = din("hmask", [128, 64])
    rsm_d = din("rsm", [128, 512])
    sel_d = din("sel", [128, 64])
    lblr_d = din("lblr", [NS, 2048])
    hnwr_d = din("hnwr", [NS, 1024])
    sbias_d = din("sbias", [128, 24])
    onehot_d = din("onehot", [NS, NS * 128])
    bdm_d = din("bdm", [8, 520])
    en_d = din("en", [8, NS * NS])

    y_d = dout("y", [T, D])
    ys_d = dout("ys", [NS, D])
    kvp_d = [dout("kv128", [128, 1024]), dout("kv512", [512, 1024]), dout("kv2048", [2048, 1024])]
    stp_d = dout("stp", [8, 128, 128])
    kvs_d = [dout("kvs128", [NS, 1024]), dout("kvs512", [NS, 1024]), dout("kvs2048", [NS, 1024])]
    sts_d = dout("sts", [NS, 8, 128, 128])

    wib_t = nc.dram_tensor("wib", [NSL, 128, 8, 512], BF16).ap()
    wapb_t = nc.dram_tensor("wapb", [8, 128, 4, 128], BF16).ap()
    whpb_t = nc.dram_tensor("whpb", [8, 128, 8, 128], BF16).ap()
    woutb_t = nc.dram_tensor("woutb", [2, 128, 8, 512], BF16).ap()
    NU = (1, 4, 16)
    hist_t = [[(nc.dram_tensor("histk%d_%d" % (g, hp), [128, NU[g] * 128], BF16).ap(),
                nc.dram_tensor("histv%d_%d" % (g, hp), [128, NU[g] * 132], BF16).ap())
               for hp in range(4)] for g in range(3)]

    with ExitStack() as stack:
        P = Prog(nc, stack)
        out_bufs = []

        wib = [P.wrap("wib%d" % k, wib_t) for k in range(3)]
        wsrc = win_d.rearrange("(c p) (s n) -> s p c n", p=128, n=512)
        for k, (s0, s1) in enumerate(((0, 8), (8, 16), (16, 22))):
            for s in range(s0, s1):
                P.dma('pool', wib_t[s], wsrc[s], w=[wib[k]], sem_on=wib[k])

        def wib_buf(s):
            return wib[0 if s < 8 else (1 if s < 16 else 2)]

        wapb = P.wrap("wapb", wapb_t)
        whpb = P.wrap("whpb", whpb_t)
        woutb = P.wrap("woutb", woutb_t)
        s_ap = wap_d.rearrange("(hp p) (j n) -> j p hp n", p=128, n=128)
        s_hp = whp_d.rearrange("(h p) (j n) -> j p h n", p=128, n=128)
        for j in range(8):
            P.dma('pool', wapb_t[j], s_ap[j], w=[wapb], sem_on=wapb)
            P.dma('pool', whpb_t[j], s_hp[j], w=[whpb], sem_on=whpb)
        s_wo = wout_d.rearrange("(c p) (s n) -> s p c n", p=128, n=512)
        for s in range(2):
            P.dma('pool', woutb_t[s], s_wo[s], w=[woutb], sem_on=woutb)

        ident_f = P.sb("ident_f", [128, 128], F32)
        ident = P.sb("ident", [128, 128], BF16)
        eb = P.sb("eb", [128, 12, 512], BF16)
        hmask = P.sb("hmask", [128, 64], F32)
        rsm = P.sb("rsm", [128, 512], F32)
        sel = P.sb("sel", [128, 64], F32)
        nw = P.sb("nw", [128, 8], F32)
        hnw = P.sb("hnw", [128, 1], F32)
        lbl = P.sb("lbl", [128, 16], F32)
        fnw = P.sb("fnw", [128, D], F32)
        epsc = P.sb("epsc", [128, 1], F32)
        ones_f = P.sb("ones_f", [128, 128], F32)
        lbc = P.sb("lbc", [128, 32], F32)
        P.dma('sp', ident_f[:], ident_d, w=[ident_f])
        for i in range(12):
            P.dma('pool', eb[:, i, :], eb_d[:, i * 512:(i + 1) * 512], w=[eb], sem_on=eb)
        P.dma('sp', hmask[:], hmask_d, w=[hmask])
        P.dma('sp', rsm[:], rsm_d, w=[rsm])
        P.dma('sp', sel[:], sel_d, w=[sel])
        P.dma('sp', nw[:], nw_d, w=[nw])
        P.dma('sp', hnw[:], hnw_d, w=[hnw])
        P.dma('sp', lbl[:], lbl_d, w=[lbl])
        P.dma('sp', fnw[:], fnw_d, w=[fnw])
        P.dve(lambda e: e.tensor_copy(out=ident[:], in_=ident_f[:]), r=[ident_f], w=[ident])
        P.pool(lambda e: e.memset(epsc[:], EPS), w=[epsc])
        P.pool(lambda e: e.memset(ones_f[:], 1.0), w=[ones_f])
        P.dve(lambda e: e.tensor_tensor(out=lbc[:, 24:32], in0=lbl[:, 0:8], in1=lbl[:, 8:16], op=ALU.subtract), r=[lbl], w=[lbc])
        P.act(lambda e: e.activation(out=lbc[:, 0:8], in_=lbc[:, 24:32], func=AF.Sigmoid), r=[lbc], w=[lbc])
        P.dve(lambda e: e.tensor_scalar(out=lbc[:, 8:16], in0=lbc[:, 0:8], scalar1=-1.0, scalar2=1.0, op0=ALU.mult, op1=ALU.add), r=[lbc], w=[lbc])
        P.dve(lambda e: e.tensor_scalar(out=lbc[:, 16:24], in0=lbc[:, 0:8], scalar1=1.0, scalar2=-1.0, op0=ALU.mult, op1=ALU.add), r=[lbc], w=[lbc])

        pb = [P.ps("pb%d" % i, [128, 512], F32) for i in range(7)]
        pbf = P.ps("pbf", [128, 1024], BF16)

        wsl = [P.sb("wsl%d" % i, [128, 8, 512], BF16) for i in range(3)]
        wsl_i = [0]

        def load_slice(s):
            b = wsl[wsl_i[0] % 3]
            wsl_i[0] += 1
            P.dma('sp', b[:], wib_t[s], r=[wib_buf(s)], w=[b], sem_on=b)
            return b

        def rstd_from_ss(dst, ss, scale, n):
            P.act(lambda e: e.activation(out=dst[:, 0:n], in_=ss[:, 0:n], func=AF.Ln, bias=epsc[:, 0:1], scale=scale), r=[ss, epsc], w=[dst])
            P.act(lambda e: e.activation(out=dst[:, 0:n], in_=dst[:, 0:n], func=AF.Exp, scale=-0.5), r=[dst], w=[dst])

        Lc = locals()
        if do_sample:
            with ExitStack() as sstack:
                P.stack = sstack
                sample_path(P, Lc)
                P.barrier()
            P.stack = stack
        if do_prompt:
            prompt_path(P, Lc)
        P.final_wait('sp', out_bufs)
        P.build()
    return nc


def prompt_path(P, L):
    nc = L['nc']; pb = L['pb']; pbf = L['pbf']; out_bufs = L['out_bufs']
    x_d = L['x_d']; y_d = L['y_d']; kvp_d = L['kvp_d']; stp_d = L['stp_d']
    ident = L['ident']; eb = L['eb']; hmask = L['hmask']; rsm = L['rsm']; sel = L['sel']
    hnw = L['hnw']; fnw = L['fnw']; lbc = L['lbc']; epsc = L['epsc']; ones_f = L['ones_f']
    load_slice = L['load_slice']; rstd_from_ss = L['rstd_from_ss']
    hist_t = L['hist_t']; NU = L['NU']; nst = L['nst']
    wapb_t = L['wapb_t']; whpb_t = L['whpb_t']; woutb_t = L['woutb_t']
    wapb = L['wapb']; whpb = L['whpb']; woutb = L['woutb']
    nwr_d = L['nwr_d']

    nwr = P.sb("nwr", [128, D], F32)
    P.dma('sp', nwr[:], nwr_d, w=[nwr])
    hT = P.sb("hT", [128, 8, ST], BF16)
    xt = [P.sb("xt%d" % i, [128, D], F32) for i in range(2)]
    xn = P.sb("xn", [128, D], BF16)
    ssn = P.sb("ssn", [128, 4], F32)
    rsn = P.sb("rsn", [128, 4], F32)
    qU = P.sb("qU", [128, 16, 128], BF16)
    kU = P.sb("kU", [128, 16, 128], BF16)
    vU = P.sb("vU", [128, 16, 2, 66], BF16)
    kprev = P.sb("kprev", [128, 16, 128], BF16)
    vprev = P.sb("vprev", [128, 16, 2, 66], BF16)
    big = P.sb("big", [128, 9 * 512], F32)
    acc = Buf("acc", big.t[0:65, 0:4096].rearrange("p (h t) -> p h t", h=2))
    tmps = [Buf("tmp%d" % i, big.t[:, i * 512:(i + 1) * 512]) for i in range(9)]
    esb = [[P.sb("esb%d_%d" % (i, hh), [128, 512], BF16) for hh in range(2)] for i in range(2)]
    pTb = [[[P.sb("pTb%d_%d_%d" % (i, hh, uu), [128, 256], BF16) for uu in range(2)] for hh in range(2)] for i in range(2)]
    attT = P.sb("attT", [128, 4, ST], BF16)
    att1 = P.sb("att1", [64, ST], BF16)
    gsA = P.sb("gsA", [64, 512], F32)
    tmpA = P.sb("tmpA", [64, 512], F32)
    rcpA = tmps[8]
    Sck = View(kU, kU.t[:, 0:8, :])
    aTm8 = View(qU, qU.t[:, 0:4, :].rearrange("p a (b t) -> p (a b) t", t=64))
    aTm = P.sb("aTm", [128, 64], BF16)
    histb = [[P.wrap("hist%d_%d" % (g, hp), hist_t[g][hp][0]) for hp in range(4)] for g in range(3)]
    vI = P.sb("vI", [128, 4, 1024], BF16)
    S32 = P.sb("S32", [128, 8, 128], F32)
    Sbf = P.sb("Sbf", [128, 8, 128], BF16)
    sig = tmps[0]
    logf = tmps[1]
    hk = tmps[2]
    bcs = tmps[3]
    Ep = tmps[4]
    En = tmps[5]
    hq = tmps[6]
    kt32 = tmps[7]
    qtl = P.sb("qtl", [128, 512], BF16)
    ktl = P.sb("ktl", [128, 512], BF16)
    kend = P.sb("kend", [128, 512], BF16)
    kendT = P.sb("kendT", [128, 4, 128], BF16)
    o32 = tmps[3]
    o2 = tmps[1]
    rs_h = tmps[5]
    gsH = tmps[8]
    hoT = P.sb("hoT", [128, 8, 512], BF16)
    ga = tmps[0]
    gbt = tmps[1]
    m1 = tmps[2]
    m2 = tmps[3]
    mergedT = View(vI, vI.t[:].rearrange("p a b -> p (a b)").rearrange("p (c t) -> p c t", t=512))
    wapj = [P.sb("wapj%d" % i, [128, 4, 128], BF16) for i in range(2)]
    whpj = [P.sb("whpj%d" % i, [128, 8, 128], BF16) for i in range(2)]
    xr = P.sb("xr", [128, D], F32)
    yo = [P.sb("yo%d" % i, [128, D], F32) for i in range(2)]
    kvtb = yo
    sq = xr
    ssf = P.sb("ssf", [128, 4], F32)
    rsf = P.sb("rsf", [128, 4], F32)

    P.pool(lambda e: e.memset(vU[:].rearrange("p u h e -> p (u h e)"), 1.0), w=[vU])
    P.pool(lambda e: e.memset(kprev[:].rearrange("p u i -> p (u i)"), 0.0), w=[kprev])
    P.pool(lambda e: e.memset(vprev[:].rearrange("p u h e -> p (u h e)"), 0.0), w=[vprev])
    P.pool(lambda e: e.memset(S32[:].rearrange("p h v -> p (h v)"), 0.0), w=[S32])
    P.pool(lambda e: e.memset(Sbf[:].rearrange("p h v -> p (h v)"), 0.0), w=[Sbf])

    def mm(bank_ap, lhsT, rhs, first, last, r, w):
        P.pe(lambda e: e.matmul(bank_ap, lhsT=lhsT, rhs=rhs, start=first, stop=last), r=r, w=w)

    def proj_fm(bank, wsb, off, m, tsl):
        for c in range(8):
            mm(bank[0:m, :], wsb[:, c, off:off + m], hT[:, c, tsl], c == 0, c == 7, [wsb, hT], [bank])

    def phase_n(st):
        for tt in range(16):
            t0 = st * ST + tt * 128
            xb = xt[tt % 2]
            P.dma('sp', xb[:], x_d[t0:t0 + 128, :], w=[xb])
            P.act(lambda e, xb=xb: e.activation(out=sq[:], in_=xb[:], func=AF.Square), r=[xb], w=[sq])
            P.dve(lambda e: e.tensor_reduce(out=ssn[:, 0:1], in_=sq[:], axis=AX.X, op=ALU.add), r=[sq], w=[ssn])
            rstd_from_ss(rsn, ssn, 1.0 / D, 1)
            P.dve(lambda e, xb=xb: e.scalar_tensor_tensor(out=xn[:], in0=xb[:], scalar=rsn[:, 0:1], in1=nwr[:], op0=ALU.mult, op1=ALU.mult), r=[xb, rsn, nwr], w=[xn])
            for c in range(8):
                P.pe(lambda e, c=c: e.transpose(out=pbf[:, c * 128:(c + 1) * 128], in_=xn[:, c * 128:(c + 1) * 128], identity=ident[:]), r=[xn, ident], w=[pbf])
            P.act(lambda e, tt=tt: e.copy(out=hT[:, :, tt * 128:(tt + 1) * 128], in_=pbf[:].rearrange("p (c t) -> p c t", t=128)), r=[pbf], w=[hT])

    def att_gh(st, g, hp):
        d = GROUPS[g][1]
        nu = NU[g]
        off = hp * 128
        wq = load_slice(3 * g)
        wk = load_slice(3 * g + 1)
        wv = load_slice(3 * g + 2)
        hb = histb[g][hp]
        htk, htv = hist_t[g][hp]
        if st > 0:
            P.dma('sp', kprev[:, 0:nu, :].rearrange("p u i -> p (u i)"), htk, r=[hb], w=[kprev], sem_on=kprev)
            P.dma('sp', vprev[:, 0:nu].rearrange("p u h e -> p (u h e)"), htv, r=[hb], w=[vprev], sem_on=vprev)
        for wsb, dst, eng in ((wq, qU, 'act'), (wk, kU, 'dve')):
            for n in range(4):
                bank = pb[n % 2]
                proj_fm(bank, wsb, off, 128, slice(n * 512, (n + 1) * 512))
                if d == 1:
                    o_ap = dst[:, 4 * n:4 * n + 4, :]
                    i_ap = bank[:].rearrange("p (u i) -> p u i", i=128)
                elif d == 4:
                    o_ap = dst[:, 4 * n:4 * n + 4, :]
                    i_ap = bank[:].rearrange("p (i r) -> p r i", r=4)
                else:
                    o_ap = dst[:, :, 32 * n:32 * n + 32]
                    i_ap = bank[:].rearrange("p (i r) -> p r i", r=16)
                if eng == 'act':
                    P.act(lambda e, o_ap=o_ap, i_ap=i_ap: e.copy(out=o_ap, in_=i_ap), r=[bank], w=[dst])
                else:
                    P.dve(lambda e, o_ap=o_ap, i_ap=i_ap: e.tensor_copy(out=o_ap, in_=i_ap), r=[bank], w=[dst])
        import os
        sub = int(os.environ.get("KSUB", "9"))
        if sub < 1:
            return
        for ug in range(4):
            bank = pb[2 + ug % 2]
            for uu in range(4):
                ts = unit_tokens(g, ug * 4 + uu)
                for c in range(8):
                    mm(bank[:, uu * 128:(uu + 1) * 128], hT[:, c, ts], wv[:, c, off:off + 128], c == 0, c == 7, [hT, wv], [bank])
            o_ap = vU[:, ug * 4:ug * 4 + 4, :, 0:64]
            i_ap = bank[:].rearrange("p (u h e) -> p u h e", u=4, h=2)
            if ug % 2 == 0:
                P.act(lambda e, o_ap=o_ap, i_ap=i_ap: e.copy(out=o_ap, in_=i_ap), r=[bank], w=[vU])
            else:
                P.dve(lambda e, o_ap=o_ap, i_ap=i_ap: e.tensor_copy(out=o_ap, in_=i_ap), r=[bank], w=[vU])
        if st == NST - 1 and hp == 0:
            ntile = GROUPS[g][0] // 128
            for tt in range(16 - ntile, 16):
                kvt = kvtb[tt % 2]
                for half, wsb in enumerate((wk, wv)):
                    bank = pb[half]
                    for c in range(8):
                        mm(bank[:], hT[:, c, tt * 128:(tt + 1) * 128], wsb[:, c, :], c == 0, c == 7, [hT, wsb], [bank])
                    if half == 0:
                        P.act(lambda e, kvt=kvt, bank=bank: e.copy(out=kvt[:, 0:512], in_=bank[:]), r=[bank], w=[kvt])
                    else:
                        P.dve(lambda e, kvt=kvt, bank=bank: e.tensor_copy(out=kvt[:, 512:1024], in_=bank[:]), r=[bank], w=[kvt])
                row0 = (tt - (16 - ntile)) * 128
                P.dma('sp', kvp_d[g][row0:row0 + 128, :], kvt[:], r=[kvt], sem_on=kvt)
                if kvt not in out_bufs:
                    out_bufs.append(kvt)
        if sub < 2:
            return
        def prev_of(u):
            if g == 0:
                return (kU, u - 1) if u > 0 else (kprev, 0)
            if g == 1:
                return (kU, u - 4) if u >= 4 else (kprev, u)
            return (kprev, u)

        ebi = g * 4 + hp
        sbks = ((pb[4], pb[5]), (pb[0], pb[1]))
        obs = (pb[6], pb[2])

        def scores(up):
            for hh in range(2):
                lo = hh * 64
                sbk = sbks[up % 2][hh]
                for uu in range(2):
                    u = up * 2 + uu
                    kpb, ku = prev_of(u)
                    mm(sbk[:, uu * 256:uu * 256 + 128], kU[lo:lo + 64, u, :], qU[lo:lo + 64, u, :], True, True, [kU, qU], [sbk])
                    mm(sbk[:, uu * 256 + 128:uu * 256 + 256], kpb[lo:lo + 64, ku, :], qU[lo:lo + 64, u, :], True, True, [kpb, qU], [sbk])

        def softmax(up):
            for hh in range(2):
                e_ = esb[up % 2][hh]
                sbk = sbks[up % 2][hh]
                P.act(lambda e, e_=e_, sbk=sbk: e.activation(out=e_[:], in_=sbk[:], func=AF.Exp, scale=0.125), r=[sbk], w=[e_])
                for uu in range(2):
                    p_ = pTb[up % 2][hh][uu]
                    fn = lambda e, e_=e_, p_=p_, hh=hh, uu=uu: e.tensor_tensor(out=p_[:], in0=e_[:, uu * 256:(uu + 1) * 256], in1=eb[:, ebi, hh * 256:(hh + 1) * 256], op=ALU.mult)
                    if uu == 0 or os.environ.get("KPOOL", "0") == "0":
                        P.dve(fn, r=[e_, eb], w=[p_])
                    else:
                        P.pool(fn, r=[e_, eb], w=[p_])

        def pv(up):
            for uu in range(2):
                u = up * 2 + uu
                ts = unit_tokens(g, u)
                kpb, ku = prev_of(u)
                vpb = vU if kpb is kU else vprev
                ob = obs[uu]
                for hh in range(2):
                    p_ = pTb[up % 2][hh][uu]
                    mm(ob[0:65, hh * 128:(hh + 1) * 128], vU[:, u, hh, 0:65], p_[:, 0:128], True, False, [vU, p_], [ob])
                    mm(ob[0:65, hh * 128:(hh + 1) * 128], vpb[:, ku, hh, 0:65], p_[:, 128:256], False, True, [vpb, p_], [ob])
                accv = acc[:, :, ts]
                src = ob[0:65, 0:256].rearrange("p (h q) -> p h q", h=2)
                if g == 0:
                    P.dve(lambda e, accv=accv, src=src: e.tensor_copy(out=accv, in_=src), r=[ob], w=[acc])
                else:
                    P.dve(lambda e, accv=accv, src=src: e.tensor_tensor(out=accv, in0=accv, in1=src, op=ALU.add), r=[ob, acc], w=[acc])

        scores(0)
        softmax(0)
        for up in range(8):
            if up + 1 < 8:
                scores(up + 1)
                softmax(up + 1)
            pv(up)
        if st < nst - 1:
            u0 = 16 - nu
            if g == 2:
                P.dve(lambda e: e.tensor_copy(out=kprev[:].rearrange("p u i -> p (u i)"), in_=kU[:].rearrange("p u i -> p (u i)")), r=[kU], w=[kprev])
                P.dma('sp', htk, kprev[:].rearrange("p u i -> p (u i)"), r=[kprev], w=[hb], sem_on=hb)
                if st == 0:
                    P.pool(lambda e: e.memset(kprev[:].rearrange("p u i -> p (u i)"), 0.0), w=[kprev])
            else:
                P.dma('sp', htk, kU[:, u0:16, :].rearrange("p u i -> p (u i)"), r=[kU], w=[hb], sem_on=hb)
            P.dma('sp', htv, vU[:, u0:16].rearrange("p u h e -> p (u h e)"), r=[vU], w=[hb], sem_on=hb)

    def att_final(hp):
        wg = load_slice(9)
        for hh in range(2):
            h = hp * 2 + hh
            for n in range(4):
                tsl = slice(n * 512, (n + 1) * 512)
                rb = pb[2]
                gbk = pb[3]
                mm(rb[0:64, :], sel[0:65, :], acc[0:65, hh, tsl], True, True, [sel, acc], [rb])
                proj_fm(gbk, wg, h * 64, 64, tsl)
                P.act(lambda e, gbk=gbk: e.activation(out=gsA[:], in_=gbk[0:64, :], func=AF.Silu), r=[gbk], w=[gsA])
                P.dve(lambda e, rb=rb: e.reciprocal(out=rcpA[0:64, :], in_=rb[0:64, :]), r=[rb], w=[rcpA])
                P.dve(lambda e, hh=hh, tsl=tsl: e.tensor_tensor(out=tmpA[:], in0=acc[0:64, hh, tsl], in1=rcpA[0:64, :], op=ALU.mult), r=[acc, rcpA], w=[tmpA])
                dst = attT[0:64, hp, tsl] if hh == 0 else att1[:, tsl]
                dbuf = attT if hh == 0 else att1
                P.dve(lambda e, dst=dst: e.tensor_tensor(out=dst, in0=tmpA[:], in1=gsA[:], op=ALU.mult), r=[tmpA, gsA], w=[dbuf])
            if hh == 1:
                P.dma('sp', attT[64:128, hp, :], att1[:], r=[att1], w=[attT], sem_on=att1)

    def hgrn_sub(st, j):
        c0 = j * 512
        tsl = slice(c0, c0 + 512)
        wi = [load_slice(14), load_slice(15)]
        for tt in range(4):
            for half in range(2):
                bank = pb[half]
                for c in range(8):
                    mm(bank[:], hT[:, c, c0 + tt * 128:c0 + (tt + 1) * 128], wi[half][:, c, :], c == 0, c == 7, [hT, wi[half]], [bank])
                o_ap = vI[:, tt, half * 512:(half + 1) * 512]
                if half == 0:
                    P.act(lambda e, o_ap=o_ap, bank=bank: e.copy(out=o_ap, in_=bank[:]), r=[bank], w=[vI])
                else:
                    P.dve(lambda e, o_ap=o_ap, bank=bank: e.tensor_copy(out=o_ap, in_=bank[:]), r=[bank], w=[vI])
        for h4 in range(2):
            wq_ = load_slice(10 + h4)
            wf_ = load_slice(12 + h4)
            wg_ = load_slice(16 + h4)
            for hq_ in range(4):
                h = h4 * 4 + hq_
                off = hq_ * 128
                bF = pb[0]
                proj_fm(bF, wf_, off, 128, tsl)
                P.act(lambda e, bF=bF: e.activation(out=sig[:], in_=bF[:], func=AF.Sigmoid), r=[bF], w=[sig])
                bQ = pb[1]
                proj_fm(bQ, wq_, off, 128, tsl)
                P.act(lambda e, bQ=bQ: e.activation(out=hq[:], in_=bQ[:], func=AF.Silu), r=[bQ], w=[hq])
                bG = pb[2]
                proj_fm(bG, wg_, off, 128, tsl)
                P.act(lambda e, bG=bG: e.activation(out=gsH[:], in_=bG[:], func=AF.Silu), r=[bG], w=[gsH])
                P.act(lambda e, h=h: e.activation(out=logf[:], in_=sig[:], func=AF.Ln, bias=lbc[:, h:h + 1], scale=lbc[:, 8 + h:9 + h]), r=[sig, lbc], w=[logf])
                P.dve(lambda e, h=h: e.tensor_scalar(out=hk[:], in0=sig[:], scalar1=lbc[:, 16 + h:17 + h], scalar2=lbc[:, 8 + h:9 + h], op0=ALU.mult, op1=ALU.add), r=[sig, lbc], w=[hk])
                P.dve(lambda e: e.tensor_tensor_scan(out=bcs[:], data0=rsm[:], data1=logf[:], initial=0.0, op0=ALU.mult, op1=ALU.add), r=[rsm, logf], w=[bcs])
                P.act(lambda e: e.activation(out=Ep[:], in_=bcs[:], func=AF.Exp), r=[bcs], w=[Ep])
                P.act(lambda e: e.activation(out=En[:], in_=bcs[:], func=AF.Exp, scale=-1.0), r=[bcs], w=[En])
                P.dve(lambda e: e.scalar_tensor_tensor(out=qtl[:], in0=hq[:], scalar=float(128 ** -0.5), in1=Ep[:], op0=ALU.mult, op1=ALU.mult), r=[hq, Ep], w=[qtl])
                P.dve(lambda e: e.tensor_tensor(out=kt32[:], in0=hk[:], in1=En[:], op=ALU.mult), r=[hk, En], w=[kt32])
                P.dve(lambda e: e.tensor_copy(out=ktl[:], in_=kt32[:]), r=[kt32], w=[ktl])
                for cc in range(8):
                    P.dve(lambda e, cc=cc: e.tensor_scalar(out=kend[:, cc * 64:(cc + 1) * 64], in0=kt32[:, cc * 64:(cc + 1) * 64], scalar1=Ep[:, cc * 64 + 63:cc * 64 + 64], scalar2=None, op0=ALU.mult), r=[kt32, Ep], w=[kend])
                for tt in range(4):
                    P.pe(lambda e, tt=tt: e.transpose(out=pbf[:, tt * 128:(tt + 1) * 128], in_=kend[:, tt * 128:(tt + 1) * 128], identity=ident[:]), r=[kend, ident], w=[pbf])
                P.act(lambda e: e.copy(out=kendT[:].rearrange("p a b -> p (a b)"), in_=pbf[:, 0:512]), r=[pbf], w=[kendT])
                bOs = (pb[3], pb[2])
                for cc in range(8):
                    bO = bOs[cc % 2]
                    osl = slice((cc // 2) * 64, (cc // 2) * 64 + 64)
                    tt = cc // 2
                    lo = (cc % 2) * 64
                    csl = slice(cc * 64, (cc + 1) * 64)
                    bA = pb[4]
                    mm(bA[lo:lo + 64, 0:64], ktl[:, csl], qtl[:, csl], True, True, [ktl, qtl], [bA])
                    P.dve(lambda e, lo=lo, bA=bA: e.tensor_tensor(out=aTm[lo:lo + 64, :], in0=bA[lo:lo + 64, 0:64], in1=hmask[lo:lo + 64, :], op=ALU.mult), r=[bA, hmask], w=[aTm])
                    mm(bO[:, osl], Sbf[:, h, :], qtl[:, csl], True, False, [Sbf, qtl], [bO])
                    mm(bO[:, osl], vI[lo:lo + 64, tt, h * 128:(h + 1) * 128], aTm[lo:lo + 64, :], False, True, [vI, aTm], [bO])
                    bS = pb[5 + cc % 2]
                    mm(bS[:, 0:128], kendT[lo:lo + 64, tt, :], vI[lo:lo + 64, tt, h * 128:(h + 1) * 128], True, True, [kendT, vI], [bS])
                    P.dve(lambda e, h=h, cc=cc, bS=bS: e.scalar_tensor_tensor(out=S32[:, h, :], in0=S32[:, h, :], scalar=Ep[:, cc * 64 + 63:cc * 64 + 64], in1=bS[:, 0:128], op0=ALU.mult, op1=ALU.add), r=[S32, Ep, bS], w=[S32])
                    P.act(lambda e, h=h: e.copy(out=Sbf[:, h, :], in_=S32[:, h, :]), r=[S32], w=[Sbf])
                for par in range(2):
                    bO = bOs[par]
                    o2v = o2[:].rearrange("p (a b t) -> p a b t", b=2, t=64)[:, :, par, :]
                    o32v = o32[:].rearrange("p (a b t) -> p a b t", b=2, t=64)[:, :, par, :]
                    srcv = bO[:, 0:256].rearrange("p (a t) -> p a t", t=64)
                    P.act(lambda e, o2v=o2v, srcv=srcv: e.activation(out=o2v, in_=srcv, func=AF.Square), r=[bO], w=[o2])
                    P.dve(lambda e, o32v=o32v, srcv=srcv: e.tensor_copy(out=o32v, in_=srcv), r=[bO], w=[o32])
                bN = pb[0]
                mm(bN[:], ones_f[:], o2[:], True, True, [ones_f, o2], [bN])
                P.act(lambda e, bN=bN: e.activation(out=rs_h[:], in_=bN[:], func=AF.Ln, bias=epsc[:, 0:1], scale=1.0 / 128), r=[bN, epsc], w=[rs_h])
                P.act(lambda e: e.activation(out=rs_h[:], in_=rs_h[:], func=AF.Exp, scale=-0.5), r=[rs_h], w=[rs_h])
                P.dve(lambda e: e.scalar_tensor_tensor(out=o32[:], in0=o32[:], scalar=hnw[:, 0:1], in1=rs_h[:], op0=ALU.mult, op1=ALU.mult), r=[o32, hnw, rs_h], w=[o32])
                P.dve(lambda e, h=h: e.tensor_tensor(out=hoT[:, h, :], in0=o32[:], in1=gsH[:], op=ALU.mult), r=[o32, gsH], w=[hoT])

    def out_sub(st, j):
        c0 = j * 512
        tsl = slice(c0, c0 + 512)
        wma = [None, None]
        wmb = [None, None]
        for jj in range(8):
            if jj % 4 == 0:
                wma[jj // 4] = load_slice(18 + jj // 4)
                wmb[jj // 4] = load_slice(20 + jj // 4)
            wa_ = wapj[jj % 2]
            wh_ = whpj[jj % 2]
            P.dma('sp', wa_[:], wapb_t[jj], r=[wapb], w=[wa_], sem_on=wa_)
            P.dma('sp', wh_[:], whpb_t[jj], r=[whpb], w=[wh_], sem_on=wh_)
            off = (jj % 4) * 128
            bA = pb[0]
            proj_fm(bA, wma[jj // 4], off, 128, tsl)
            P.act(lambda e, bA=bA: e.activation(out=ga[:], in_=bA[:], func=AF.Sigmoid), r=[bA], w=[ga])
            bB = pb[1]
            proj_fm(bB, wmb[jj // 4], off, 128, tsl)
            P.act(lambda e, bB=bB: e.activation(out=gbt[:], in_=bB[:], func=AF.Sigmoid), r=[bB], w=[gbt])
            b1 = pb[2]
            for hp in range(4):
                mm(b1[:], wa_[:, hp, :], attT[:, hp, tsl], hp == 0, hp == 3, [wa_, attT], [b1])
            b2 = pb[3]
            for h in range(8):
                mm(b2[:], wh_[:, h, :], hoT[:, h, :], h == 0, h == 7, [wh_, hoT], [b2])
            P.dve(lambda e, b1=b1: e.tensor_tensor(out=m1[:], in0=ga[:], in1=b1[:], op=ALU.mult), r=[ga, b1], w=[m1])
            P.dve(lambda e, b2=b2: e.tensor_tensor(out=m2[:], in0=gbt[:], in1=b2[:], op=ALU.mult), r=[gbt, b2], w=[m2])
            P.dve(lambda e, jj=jj: e.tensor_tensor(out=mergedT[:, jj, :], in0=m1[:], in1=m2[:], op=ALU.add), r=[m1, m2], w=[mergedT])
        wo = []
        for s in range(2):
            b = L['wsl'][L['wsl_i'][0] % 3]
            L['wsl_i'][0] += 1
            P.dma('sp', b[:], woutb_t[s], r=[woutb], w=[b], sem_on=b)
            wo.append(b)
        for tt in range(4):
            t0 = st * ST + c0 + tt * 128
            xb = xt[tt % 2]
            P.dma('sp', xb[:], x_d[t0:t0 + 128, :], w=[xb])
            for half in range(2):
                bank = pb[4 + half]
                for c in range(8):
                    mm(bank[:], mergedT[:, c, tt * 128:(tt + 1) * 128], wo[half][:, c, :], c == 0, c == 7, [mergedT, wo[half]], [bank])
                P.dve(lambda e, half=half, bank=bank, xb=xb: e.tensor_tensor(out=xr[:, half * 512:(half + 1) * 512], in0=bank[:], in1=xb[:, half * 512:(half + 1) * 512], op=ALU.add), r=[bank, xb], w=[xr])
            yb = yo[tt % 2]
            P.act(lambda e, yb=yb: e.activation(out=yb[:], in_=xr[:], func=AF.Square), r=[xr], w=[yb])
            P.dve(lambda e, yb=yb: e.tensor_reduce(out=ssf[:, 0:1], in_=yb[:], axis=AX.X, op=ALU.add), r=[yb], w=[ssf])
            rstd_from_ss(rsf, ssf, 1.0 / D, 1)
            P.dve(lambda e, yb=yb: e.scalar_tensor_tensor(out=yb[:], in0=xr[:], scalar=rsf[:, 0:1], in1=fnw[:], op0=ALU.mult, op1=ALU.mult), r=[xr, rsf, fnw], w=[yb])
            P.dma('sp', y_d[t0:t0 + 128, :], yb[:], r=[yb], sem_on=yb)
            if yb not in out_bufs:
                out_bufs.append(yb)

    import os
    stage0 = int(os.environ.get("KSTAGE", "9"))
    stage1 = int(os.environ.get("KSTAGE1", "9"))
    for st in range(nst):
        stage = stage0 if st == 0 else stage1
        if st > 0:
            P.barrier()
        if stage >= 1:
            phase_n(st)
        if stage >= 2:
            P.dve(lambda e: e.memset(acc[0:1, 0, 0:1], 0.0), w=tmps + [acc])
            for hp in range(4):
                for g in range(3):
                    att_gh(st, g, hp)
                if stage >= 3:
                    att_final(hp)
        if stage >= 4:
            P.barrier()
            P.dve(lambda e: e.memset(tmps[0][0:1, 0:1], 0.0), w=[acc] + tmps)
            for j in range(4):
                hgrn_sub(st, j)
                if stage >= 5:
                    out_sub(st, j)
    P.dma('sp', stp_d.rearrange("h k v -> k h v"), S32[:], r=[S32], sem_on=S32)
    out_bufs.append(S32)


def sample_path(P, L):
    nc = L['nc']; pb = L['pb']; pbf = L['pbf']; out_bufs = L['out_bufs']
    xs_d = L['xs_d']; c_d = L['c_d']; sh_d = L['sh_d']; ys_d = L['ys_d']; kvs_d = L['kvs_d']; sts_d = L['sts_d']
    ident = L['ident']; ident_f = L['ident_f']; fnw = L['fnw']; epsc = L['epsc']; ones_f = L['ones_f']
    load_slice = L['load_slice']; rstd_from_ss = L['rstd_from_ss']
    wapb_t = L['wapb_t']; whpb_t = L['whpb_t']; woutb_t = L['woutb_t']
    wapb = L['wapb']; whpb = L['whpb']; woutb = L['woutb']
    nwr_d = L['nwr_d']; lblr_d = L['lblr_d']; hnwr_d = L['hnwr_d']; sbias_d = L['sbias_d']
    onehot_d = L['onehot_d']; bdm_d = L['bdm_d']; en_d = L['en_d']
    A = AF
    OFF_G = 4608; OFF_Q = 5120; OFF_F = 6144; OFF_I = 7168; OFF_HG = 8192; OFF_MA = 9216; OFF_MB = 10240

    def sb(name, shape, dt=F32):
        return P.sb("s_" + name, shape, dt)

    class phase:
        def __enter__(self):
            self.outer = P.stack
            self.es = ExitStack()
            self.es.__enter__()
            P.stack = self.es
            return self

        def __exit__(self, *a):
            P.barrier()
            P.stack = self.outer
            return self.es.__exit__(*a)

    att_b = sb("att_b", [NS, 512], BF16)
    ho_b = sb("ho_b", [NS, 1024], BF16)

    nwrs = sb("nwrs", [NS, D]); P.dma('sp', nwrs[:], nwr_d[0:NS, :], w=[nwrs])
    lblr = sb("lblr", [NS, 2048]); P.dma('sp', lblr[:], lblr_d, w=[lblr])
    hnwr = sb("hnwr", [NS, 1024]); P.dma('sp', hnwr[:], hnwr_d, w=[hnwr])
    sbias = sb("sbias", [128, 24]); P.dma('sp', sbias[:], sbias_d, w=[sbias])
    onehot = sb("onehot", [NS, NS * 128]); P.dma('sp', onehot[:], onehot_d, w=[onehot])
    bdm = sb("bdm", [8, 520]); P.dma('sp', bdm[:], bdm_d, w=[bdm])
    en = sb("en", [8, NS * NS]); P.dma('sp', en[:], en_d, w=[en])
    xs = sb("xs", [NS, D]); P.dma('sp', xs[:], xs_d, w=[xs])
    sq = sb("sq", [NS, D])
    st1 = sb("st1", [NS, 16]); st2 = sb("st2", [NS, 16])
    xn = sb("xn", [NS, D], BF16)
    hsT = sb("hsT", [128, 8, NS], BF16)
    projs = sb("projs", [NS, 11264])

    def mm(bank_ap, lhsT, rhs, first, last, r, w):
        P.pe(lambda e: e.matmul(bank_ap, lhsT=lhsT, rhs=rhs, start=first, stop=last), r=r, w=w)

    def rms_rows(dst_bf, src, wrow):
        P.act(lambda e: e.activation(out=sq[:], in_=src[:], func=A.Square), r=[src], w=[sq])
        P.dve(lambda e: e.tensor_reduce(out=st1[:, 0:1], in_=sq[:], axis=AX.X, op=ALU.add), r=[sq], w=[st1])
        P.act(lambda e: e.activation(out=st2[:, 0:1], in_=st1[:, 0:1], func=A.Ln, bias=epsc[0:NS, 0:1], scale=1.0 / D), r=[st1, epsc], w=[st2])
        P.act(lambda e: e.activation(out=st2[:, 0:1], in_=st2[:, 0:1], func=A.Exp, scale=-0.5), r=[st2], w=[st2])
        P.dve(lambda e: e.scalar_tensor_tensor(out=dst_bf[:], in0=src[:], scalar=st2[:, 0:1], in1=wrow[:], op0=ALU.mult, op1=ALU.mult), r=[src, st2, wrow], w=[dst_bf])

    def to_fm_bf(dstT, src_bf, nch):
        for c in range(nch):
            P.pe(lambda e, c=c: e.transpose(out=pbf[:, c * NS:(c + 1) * NS], in_=src_bf[0:NS, c * 128:(c + 1) * 128], identity=ident[0:NS, 0:NS]), r=[src_bf, ident], w=[pbf])
        P.act(lambda e: e.copy(out=dstT[:].rearrange("p c n -> p (c n)"), in_=pbf[:, 0:nch * NS]), r=[pbf], w=[dstT])

    rms_rows(xn, xs, nwrs)
    to_fm_bf(hsT, xn, 8)
    for s in range(NSL):
        wsb = load_slice(s)
        bank = pb[s % 2]
        for c in range(8):
            mm(bank[0:NS, :], hsT[:, c, :], wsb[:, c, :], c == 0, c == 7, [hsT, wsb], [bank])
        if s % 2 == 0:
            P.act(lambda e, s=s, bank=bank: e.copy(out=projs[:, s * 512:(s + 1) * 512], in_=bank[0:NS, :]), r=[bank], w=[projs])
        else:
            P.dve(lambda e, s=s, bank=bank: e.tensor_copy(out=projs[:, s * 512:(s + 1) * 512], in_=bank[0:NS, :]), r=[bank], w=[projs])
    for g in range(3):
        P.dma('sp', kvs_d[g], projs[:, g * 1536 + 512:g * 1536 + 1536], r=[projs], sem_on=projs)
    out_bufs.append(projs)

    phB = phase(); phB.__enter__()
    accn = sb("accn", [NS, 512]); accd = sb("accd", [NS, 8])
    tq = sb("tq", [NS, 512]); ts_ = sb("ts", [NS, 8]); tp = sb("tp", [NS, 8])
    for g in range(3):
        b0 = g * 1536
        P.dve(lambda e, b0=b0: e.tensor_tensor(out=tq[:], in0=projs[:, b0:b0 + 512], in1=projs[:, b0 + 512:b0 + 1024], op=ALU.mult), r=[projs], w=[tq])
        P.dve(lambda e: e.tensor_reduce(out=ts_[:], in_=tq[:].rearrange("p (h e) -> p h e", e=64), axis=AX.X, op=ALU.add), r=[tq], w=[ts_])
        P.act(lambda e: e.activation(out=tp[:], in_=ts_[:], func=A.Exp, scale=0.125), r=[ts_], w=[tp])
        for h in range(8):
            hs = slice(h * 64, (h + 1) * 64)
            if g == 0:
                P.dve(lambda e, h=h, hs=hs, b0=b0: e.tensor_scalar(out=accn[:, hs], in0=projs[:, b0 + 1024 + h * 64:b0 + 1024 + (h + 1) * 64], scalar1=tp[:, h:h + 1], scalar2=None, op0=ALU.mult), r=[projs, tp], w=[accn])
            else:
                P.dve(lambda e, h=h, hs=hs, b0=b0: e.scalar_tensor_tensor(out=accn[:, hs], in0=projs[:, b0 + 1024 + h * 64:b0 + 1024 + (h + 1) * 64], scalar=tp[:, h:h + 1], in1=accn[:, hs], op0=ALU.mult, op1=ALU.add), r=[projs, tp, accn], w=[accn])
        if g == 0:
            P.dve(lambda e: e.tensor_copy(out=accd[:], in_=tp[:]), r=[tp], w=[accd])
        else:
            P.dve(lambda e: e.tensor_tensor(out=accd[:], in0=accd[:], in1=tp[:], op=ALU.add), r=[accd, tp], w=[accd])
    ck = [sb("ck%d" % i, [128, 1024]) for i in range(2)]
    prod = sb("prod", [128, 512])
    s8 = sb("s8", [128, 8]); p8 = sb("p8", [128, 8])
    msk = sb("msk", [8, 520])
    it = 0
    for g, (win, dil) in enumerate(GROUPS):
        b0 = g * 1536
        for n in range(NS):
            c_ = ck[it % 2]
            it += 1
            P.dma('sp', c_[:], c_d[g][n, 0:win:dil, :], w=[c_])
            qb = pb[2]
            mm(qb[:], onehot[:, n * 128:(n + 1) * 128], projs[:, b0:b0 + 512], True, True, [onehot, projs], [qb])
            P.dve(lambda e, c_=c_, qb=qb: e.tensor_tensor(out=prod[:], in0=c_[:, 0:512], in1=qb[:], op=ALU.mult), r=[c_, qb], w=[prod])
            P.dve(lambda e: e.tensor_reduce(out=s8[:], in_=prod[:].rearrange("p (h e) -> p h e", e=64), axis=AX.X, op=ALU.add), r=[prod], w=[s8])
            P.dve(lambda e, g=g: e.scalar_tensor_tensor(out=s8[:], in0=s8[:], scalar=0.125, in1=sbias[:, g * 8:(g + 1) * 8], op0=ALU.mult, op1=ALU.add), r=[s8, sbias], w=[s8])
            P.act(lambda e: e.activation(out=p8[:], in_=s8[:], func=A.Exp), r=[s8], w=[p8])
            nb = pb[3]; db = pb[4]
            mm(nb[0:8, :], p8[:], c_[:, 512:1024], True, True, [p8, c_], [nb])
            mm(db[0:8, 0:8], p8[:], ones_f[:, 0:8], True, True, [p8, ones_f], [db])
            P.dve(lambda e, nb=nb: e.tensor_tensor(out=msk[:, 0:512], in0=nb[0:8, :], in1=bdm[:, 0:512], op=ALU.mult), r=[nb, bdm], w=[msk])
            P.dve(lambda e, db=db: e.tensor_tensor(out=msk[:, 512:520], in0=db[0:8, 0:8], in1=bdm[:, 512:520], op=ALU.mult), r=[db, bdm], w=[msk])
            rb = pb[5]; rd = pb[6]
            mm(rb[0:NS, :], en[:, n * NS:(n + 1) * NS], msk[:, 0:512], True, True, [en, msk], [rb])
            mm(rd[0:NS, 0:8], en[:, n * NS:(n + 1) * NS], msk[:, 512:520], True, True, [en, msk], [rd])
            P.dve(lambda e, rb=rb: e.tensor_tensor(out=accn[:], in0=accn[:], in1=rb[0:NS, :], op=ALU.add), r=[accn, rb], w=[accn])
            P.dve(lambda e, rd=rd: e.tensor_tensor(out=accd[:], in0=accd[:], in1=rd[0:NS, 0:8], op=ALU.add), r=[accd, rd], w=[accd])
    att_s = sb("att_s", [NS, 512])
    gsa = sb("gsa", [NS, 512])
    P.dve(lambda e: e.reciprocal(out=accd[:], in_=accd[:]), r=[accd], w=[accd])
    for h in range(8):
        hs = slice(h * 64, (h + 1) * 64)
        P.dve(lambda e, h=h, hs=hs: e.tensor_scalar(out=att_s[:, hs], in0=accn[:, hs], scalar1=accd[:, h:h + 1], scalar2=None, op0=ALU.mult), r=[accn, accd], w=[att_s])
    P.act(lambda e: e.activation(out=gsa[:], in_=projs[:, OFF_G:OFF_G + 512], func=A.Silu), r=[projs], w=[gsa])
    P.dve(lambda e: e.tensor_tensor(out=att_b[:], in0=att_s[:], in1=gsa[:], op=ALU.mult), r=[att_s, gsa], w=[att_b])
    phB.__exit__(None, None, None)
    phC = phase(); phC.__enter__()

    lb = sb("lb", [NS, 1024]); oml = sb("oml", [NS, 1024])
    f_t = sb("f_t", [NS, 1024]); hk_t = sb("hk_t", [NS, 1024]); hq_t = sb("hq_t", [NS, 1024])
    P.dve(lambda e: e.tensor_tensor(out=lb[:], in0=lblr[:, 0:1024], in1=lblr[:, 1024:2048], op=ALU.subtract), r=[lblr], w=[lb])
    P.act(lambda e: e.activation(out=lb[:], in_=lb[:], func=A.Sigmoid), r=[lb], w=[lb])
    P.dve(lambda e: e.tensor_scalar(out=oml[:], in0=lb[:], scalar1=-1.0, scalar2=1.0, op0=ALU.mult, op1=ALU.add), r=[lb], w=[oml])
    P.act(lambda e: e.activation(out=f_t[:], in_=projs[:, OFF_F:OFF_F + 1024], func=A.Sigmoid), r=[projs], w=[f_t])
    P.dve(lambda e: e.tensor_tensor(out=f_t[:], in0=f_t[:], in1=oml[:], op=ALU.mult), r=[f_t, oml], w=[f_t])
    P.dve(lambda e: e.tensor_tensor(out=hk_t[:], in0=oml[:], in1=f_t[:], op=ALU.subtract), r=[oml, f_t], w=[hk_t])
    P.dve(lambda e: e.tensor_tensor(out=f_t[:], in0=f_t[:], in1=lb[:], op=ALU.add), r=[f_t, lb], w=[f_t])
    P.act(lambda e: e.activation(out=hq_t[:], in_=projs[:, OFF_Q:OFF_Q + 1024], func=A.Silu), r=[projs], w=[hq_t])
    P.dve(lambda e: e.tensor_scalar(out=hq_t[:], in0=hq_t[:], scalar1=float(128 ** -0.5), scalar2=None, op0=ALU.mult), r=[hq_t], w=[hq_t])
    fT = sb("fT", [128, 8, NS]); hkT = sb("hkT", [128, 8, NS]); hqT = sb("hqT", [128, 8, NS])
    for src, dst, bank in ((f_t, fT, pb[0]), (hk_t, hkT, pb[1]), (hq_t, hqT, pb[2])):
        for h in range(8):
            P.pe(lambda e, h=h, src=src, bank=bank: e.transpose(out=bank[:, h * NS:(h + 1) * NS], in_=src[0:NS, h * 128:(h + 1) * 128], identity=ident_f[0:NS, 0:NS]), r=[src, ident_f], w=[bank])
        P.dve(lambda e, dst=dst, bank=bank: e.tensor_copy(out=dst[:].rearrange("p h n -> p (h n)"), in_=bank[:, 0:8 * NS]), r=[bank], w=[dst])
    Qm = sb("Qm", [128, 8, NS, NS])
    P.pool(lambda e: e.memset(Qm[:].rearrange("p h a b -> p (h a b)"), 0.0), w=[Qm])
    for n in range(NS):
        P.dve(lambda e, n=n: e.tensor_copy(out=Qm[:, :, n, n], in_=hqT[:, :, n]), r=[hqT], w=[Qm])
    oacc = sb("oacc", [NS, 1024])
    P.pool(lambda e: e.memset(oacc[:], 0.0), w=[oacc])
    S0 = [sb("S0_%d" % i, [128, 8, 128]) for i in range(2)]
    Sn = [sb("Sn_%d" % i, [128, 8, 128]) for i in range(2)]
    tmpk = sb("tmpk", [128, 128])
    for n in range(NS):
        s0 = S0[n % 2]; sn = Sn[n % 2]
        P.dma('sp', s0[:], sh_d[n].rearrange("h k v -> k h v"), w=[s0])
        ib = (pb[3], pb[4])
        for half in range(2):
            mm(ib[half][:], onehot[:, n * 128:(n + 1) * 128], projs[:, OFF_I + half * 512:OFF_I + (half + 1) * 512], True, True, [onehot, projs], [ib[half]])
        for h in range(8):
            src = ib[h // 4][:, (h % 4) * 128:(h % 4 + 1) * 128]
            P.dve(lambda e, src=src, h=h, n=n: e.tensor_scalar(out=tmpk[:], in0=src, scalar1=hkT[:, h, n:n + 1], scalar2=None, op0=ALU.mult), r=[ib[h // 4], hkT], w=[tmpk])
            P.dve(lambda e, h=h, n=n, s0=s0, sn=sn: e.scalar_tensor_tensor(out=sn[:, h, :], in0=s0[:, h, :], scalar=fT[:, h, n:n + 1], in1=tmpk[:], op0=ALU.mult, op1=ALU.add), r=[s0, fT, tmpk], w=[sn])
        P.dma('sp', sts_d[n].rearrange("h k v -> k h v"), sn[:], r=[sn], sem_on=sn)
        if sn not in out_bufs:
            out_bufs.append(sn)
        ob = (pb[5], pb[6])
        for h in range(8):
            mm(ob[h // 4][0:NS, (h % 4) * 128:(h % 4 + 1) * 128], Qm[:, h, n, :], sn[:, h, :], True, True, [Qm, sn], [ob[h // 4]])
        for half in range(2):
            P.dve(lambda e, half=half, ob=ob: e.tensor_tensor(out=oacc[:, half * 512:(half + 1) * 512], in0=oacc[:, half * 512:(half + 1) * 512], in1=ob[half][0:NS, :], op=ALU.add), r=[oacc, ob[half]], w=[oacc])
    o2s = sb("o2s", [NS, 1024]); ssh = sb("ssh", [NS, 8]); rsh = sb("rsh", [NS, 8])
    gsh = sb("gsh", [NS, 1024])
    P.act(lambda e: e.activation(out=o2s[:], in_=oacc[:], func=A.Square), r=[oacc], w=[o2s])
    P.dve(lambda e: e.tensor_reduce(out=ssh[:], in_=o2s[:].rearrange("p (h v) -> p h v", v=128), axis=AX.X, op=ALU.add), r=[o2s], w=[ssh])
    P.act(lambda e: e.activation(out=rsh[:], in_=ssh[:], func=A.Ln, bias=epsc[0:NS, 0:1], scale=1.0 / 128), r=[ssh, epsc], w=[rsh])
    P.act(lambda e: e.activation(out=rsh[:], in_=rsh[:], func=A.Exp, scale=-0.5), r=[rsh], w=[rsh])
    P.act(lambda e: e.activation(out=gsh[:], in_=projs[:, OFF_HG:OFF_HG + 1024], func=A.Silu), r=[projs], w=[gsh])
    for h in range(8):
        hs = slice(h * 128, (h + 1) * 128)
        P.dve(lambda e, h=h, hs=hs: e.scalar_tensor_tensor(out=o2s[:, hs], in0=oacc[:, hs], scalar=rsh[:, h:h + 1], in1=hnwr[:, hs], op0=ALU.mult, op1=ALU.mult), r=[oacc, rsh, hnwr], w=[o2s])
    P.dve(lambda e: e.tensor_tensor(out=ho_b[:], in0=o2s[:], in1=gsh[:], op=ALU.mult), r=[o2s, gsh], w=[ho_b])
    phC.__exit__(None, None, None)

    attTs = sb("attTs", [128, 4, NS], BF16); hoTs = sb("hoTs", [128, 8, NS], BF16)
    to_fm_bf(attTs, att_b, 4)
    to_fm_bf(hoTs, ho_b, 8)
    gas = sb("gas", [NS, 1024]); gbs = sb("gbs", [NS, 1024])
    P.act(lambda e: e.activation(out=gas[:], in_=projs[:, OFF_MA:OFF_MA + 1024], func=A.Sigmoid), r=[projs], w=[gas])
    P.act(lambda e: e.activation(out=gbs[:], in_=projs[:, OFF_MB:OFF_MB + 1024], func=A.Sigmoid), r=[projs], w=[gbs])
    mg = sb("mg", [NS, 1024]); mg2 = sb("mg2", [NS, 1024]); mgb = sb("mgb", [NS, 1024], BF16)
    wa_s = [sb("wa_s%d" % i, [128, 4, 128], BF16) for i in range(2)]
    wh_s = [sb("wh_s%d" % i, [128, 8, 128], BF16) for i in range(2)]
    for jj in range(8):
        wa_ = wa_s[jj % 2]; wh_ = wh_s[jj % 2]
        P.dma('sp', wa_[:], wapb_t[jj], r=[wapb], w=[wa_], sem_on=wa_)
        P.dma('sp', wh_[:], whpb_t[jj], r=[whpb], w=[wh_], sem_on=wh_)
        js = slice(jj * 128, (jj + 1) * 128)
        b1 = pb[0]; b2 = pb[1]
        for hp in range(4):
            mm(b1[0:NS, 0:128], attTs[:, hp, :], wa_[:, hp, :], hp == 0, hp == 3, [attTs, wa_], [b1])
        for h in range(8):
            mm(b2[0:NS, 0:128], hoTs[:, h, :], wh_[:, h, :], h == 0, h == 7, [hoTs, wh_], [b2])
        P.dve(lambda e, js=js, b1=b1: e.tensor_tensor(out=mg[:, js], in0=gas[:, js], in1=b1[0:NS, 0:128], op=ALU.mult), r=[gas, b1], w=[mg])
        P.dve(lambda e, js=js, b2=b2: e.tensor_tensor(out=mg2[:, js], in0=gbs[:, js], in1=b2[0:NS, 0:128], op=ALU.mult), r=[gbs, b2], w=[mg2])
    P.dve(lambda e: e.tensor_tensor(out=mgb[:], in0=mg[:], in1=mg2[:], op=ALU.add), r=[mg, mg2], w=[mgb])
    mgT = sb("mgT", [128, 8, NS], BF16)
    to_fm_bf(mgT, mgb, 8)
    xr_s = sb("xr_s", [NS, D]); fnws = sb("fnws", [NS, D]); ysb = sb("ysb", [NS, D])
    P.dma('sp', fnws[:], L['fnw_d'][0:NS, :], w=[fnws])
    for half in range(2):
        wsl = L['wsl'][L['wsl_i'][0] % 3]
        L['wsl_i'][0] += 1
        P.dma('sp', wsl[:], woutb_t[half], r=[woutb], w=[wsl], sem_on=wsl)
        bank = pb[2 + half]
        for c in range(8):
            mm(bank[0:NS, :], mgT[:, c, :], wsl[:, c, :], c == 0, c == 7, [mgT, wsl], [bank])
        P.dve(lambda e, half=half, bank=bank: e.tensor_tensor(out=xr_s[:, half * 512:(half + 1) * 512], in0=bank[0:NS, :], in1=xs[:, half * 512:(half + 1) * 512], op=ALU.add), r=[bank, xs], w=[xr_s])
    rms_rows(ysb, xr_s, fnws)
    P.dma('sp', ys_d, ysb[:], r=[ysb], sem_on=ysb)
    out_bufs.append(ysb)


def _consts():
    c = {}
    c["ident"] = np.eye(128, dtype=np.float32)
    n = 24
    slopes = (2.0 ** (-8.0 * np.arange(1, n + 1) / n)).astype(np.float32).reshape(3, 8)
    k = np.arange(128)[:, None].astype(np.float32)
    q = np.arange(128)[None, :].astype(np.float32)
    eb = np.zeros((128, 12, 4, 128), np.float32)
    for g, (win, d) in enumerate(GROUPS):
        for hp in range(4):
            for hh in range(2):
                s = slopes[g, hp * 2 + hh]
                cur = np.where(q >= k, np.exp(-s * d * np.maximum(q - k, 0.0)), 0.0)
                prev = np.where(k >= q, np.exp(-s * d * (q + 128 - k)), 0.0)
                eb[:, g * 4 + hp, 2 * hh, :] = cur
                eb[:, g * 4 + hp, 2 * hh + 1, :] = prev
    c["eb"] = eb.reshape(128, 12 * 512)
    p = np.arange(128)[:, None] % 64
    t = np.arange(64)[None, :]
    c["hmask"] = (t >= p).astype(np.float32)
    rs = np.ones((128, 512), np.float32)
    rs[:, ::64] = 0.0
    c["rsm"] = rs
    sel = np.zeros((128, 64), np.float32)
    sel[64, :] = 1.0
    c["sel"] = sel
    sb = np.zeros((128, 24), np.float32)
    for g, (win, d) in enumerate(GROUPS):
        for h in range(8):
            sb[:, g * 8 + h] = -slopes[g, h] * d * (128 - np.arange(128))
    c["sbias"] = sb
    oh = np.zeros((NS, NS, 128), np.float32)
    for i in range(NS):
        oh[i, i, :] = 1.0
    c["onehot"] = oh.reshape(NS, NS * 128)
    bdm = np.zeros((8, 520), np.float32)
    for h in range(8):
        bdm[h, h * 64:(h + 1) * 64] = 1.0
        bdm[h, 512 + h] = 1.0
    c["bdm"] = bdm
    en = np.zeros((8, NS, NS), np.float32)
    for i in range(NS):
        en[:, i, i] = 1.0
    c["en"] = en.reshape(8, NS * NS)
    return c


_CACHE = {}


def kernel(x_prompt, x_sample, cache_kv_w128, cache_kv_w512, cache_kv_w2048, state_hgrn,
           norm_w, w_in, w_att_proj, w_hg_proj, w_out, hg_norm_w, hg_lb_logits, final_norm_w,
           _nst=NST, _cores=8, _prompt=True, _sample=True):
    f = lambda a: np.ascontiguousarray(np.asarray(a, dtype=np.float32))
    key = (_nst, _prompt, _sample)
    if "nc" not in _CACHE or _CACHE.get("key") != key:
        _CACHE["nc"] = build_program(do_prompt=_prompt, do_sample=_sample, nst=_nst)
        _CACHE["key"] = key
    nc = _CACHE["nc"]
    cst = _consts()
    shared = dict(cst)
    shared["w_in"] = f(w_in[0])
    shared["wap"] = f(w_att_proj[0])
    shared["whp"] = f(w_hg_proj[0])
    shared["wout"] = f(w_out[0])
    shared["nw"] = f(np.asarray(norm_w[0]).reshape(8, 128).T)
    shared["nwr"] = f(np.broadcast_to(np.asarray(norm_w[0])[None, :], (128, D)))
    shared["hnw"] = f(np.asarray(hg_norm_w[0]).reshape(128, 1))
    shared["hnwr"] = f(np.broadcast_to(np.tile(np.asarray(hg_norm_w[0]), 8)[None, :], (NS, 1024)))
    lg = np.asarray(hg_lb_logits)
    shared["lbl"] = f(lg.reshape(2, 8, 128).transpose(2, 0, 1).reshape(128, 16))
    shared["lblr"] = f(np.broadcast_to(lg.reshape(1, 2048), (NS, 2048)))
    shared["fnw"] = f(np.broadcast_to(np.asarray(final_norm_w)[None, :], (128, D)))
    in_maps = []
    for i in range(_cores):
        m = dict(shared)
        m["x"] = f(x_prompt[i])
        sl = slice(NS * i, NS * (i + 1))
        m["xs"] = f(np.asarray(x_sample)[sl, 0, :])
        m["c128"] = f(np.asarray(cache_kv_w128)[0, sl].reshape(NS, 128, 1024))
        m["c512"] = f(np.asarray(cache_kv_w512)[0, sl].reshape(NS, 512, 1024))
        m["c2048"] = f(np.asarray(cache_kv_w2048)[0, sl].reshape(NS, 2048, 1024))
        m["sh"] = f(np.asarray(state_hgrn)[0, sl])
        in_maps.append(m)
    res = run_bass_kernel_spmd(nc, in_maps, core_ids=list(range(_cores)))
    R = res.results
    n = _cores
    y = np.stack([R[i]["y"] for i in range(n)], 0)
    ys = np.concatenate([R[i]["ys"] for i in range(n)], 0).reshape(n * NS, 1, D)
    kvp = [np.stack([R[i][k] for i in range(n)], 0).reshape(1, n, w, 2, 8, 64)
           for k, w in (("kv128", 128), ("kv512", 512), ("kv2048", 2048))]
    stp = np.stack([R[i]["stp"] for i in range(n)], 0).reshape(1, n, 8, 128, 128)
    kvs = [np.concatenate([R[i][k] for i in range(n)], 0).reshape(1, n * NS, 1, 2, 8, 64)
           for k in ("kvs128", "kvs512", "kvs2048")]
    sts = np.concatenate([R[i]["sts"] for i in range(n)], 0).reshape(1, n * NS, 8, 128, 128)
    return (y, ys, kvp[0], kvp[1], kvp[2], stp, kvs[0], kvs[1], kvs[2], sts)
```

```python
import numpy as np
import concourse.bass as bass
import concourse.mybir as mybir
from concourse.bass_utils import run_bass_kernel_spmd

F32 = mybir.dt.float32
BF16 = mybir.dt.bfloat16
AF = mybir.ActivationFunctionType
ALU = mybir.AluOpType
AX = mybir.AxisListType

ENGS = ['pe', 'act', 'dve', 'pool', 'sp']


class Buf:
    __slots__ = ('name', 't', 'lw', 'rd', 'dsem')

    def __init__(self, name, t):
        self.name = name
        self.t = t
        self.lw = None
        self.rd = {}
        self.dsem = None

    def __getitem__(self, idx):
        return self.t[idx]


class DSem:
    __slots__ = ('key', 'val', 'h')

    def __init__(self, key, h):
        self.key = key
        self.val = 0
        self.h = h


class Prog:
    def __init__(self, nc, stack):
        self.nc = nc
        self.stack = stack
        self.sem_stack = stack
        self.lists = {e: [] for e in ENGS}
        self.cnt = {e: 0 for e in ENGS}
        self.seen = {e: {} for e in ENGS}
        self.semh = {}
        for e in ENGS:
            self.semh[e] = stack.enter_context(nc.semaphore('s_' + e))
        self.ndsem = 0
        self.alldsem = []
        self.nbuf = 0

    def sb(self, name, shape, dtype):
        t = self.stack.enter_context(self.nc.sbuf_tensor("sb_" + name, list(shape), dtype))
        return Buf(name, t)

    def ps(self, name, shape, dtype=F32):
        t = self.stack.enter_context(self.nc.psum_tensor("ps_" + name, list(shape), dtype))
        return Buf(name, t)

    def wrap(self, name, t):
        return Buf(name, t)

    def _dsem(self, b):
        if b.dsem is None:
            h = self.sem_stack.enter_context(self.nc.semaphore('d%d' % self.ndsem))
            key = ('dma', self.ndsem)
            self.ndsem += 1
            self.semh[key] = h
            b.dsem = DSem(key, h)
            self.alldsem.append(b.dsem)
        return b.dsem

    def emit(self, eng, fn, reads=(), writes=(), dma=None):
        deps = {}

        def add(dep):
            if dep is None:
                return
            k, v = dep
            if deps.get(k, 0) < v:
                deps[k] = v

        for b in reads:
            add(b.lw)
        for b in writes:
            add(b.lw)
            for k, v in b.rd.items():
                add((k, v))
        ds = None
        if dma is not None:
            ds = self._dsem(dma)
            if ds.val:
                add((ds.key, ds.val))
        waits = []
        seen = self.seen[eng]
        for k, v in deps.items():
            if k == eng and eng == 'pe':
                continue
            if seen.get(k, 0) >= v:
                continue
            seen[k] = v
            waits.append((k, v))
        if ds is None:
            self.cnt[eng] += 1
            tok = (eng, self.cnt[eng])
        else:
            ds.val += 16
            tok = (ds.key, ds.val)
        self.lists[eng].append((waits, fn, ds))
        for b in reads:
            if b.rd.get(tok[0], 0) < tok[1]:
                b.rd[tok[0]] = tok[1]
        for b in writes:
            b.lw = tok
            b.rd = {}
        return tok

    def pe(self, fn, r=(), w=()):
        return self.emit('pe', fn, r, w)

    def act(self, fn, r=(), w=()):
        return self.emit('act', fn, r, w)

    def dve(self, fn, r=(), w=()):
        return self.emit('dve', fn, r, w)

    def pool(self, fn, r=(), w=()):
        return self.emit('pool', fn, r, w)

    def dma(self, q, out_ap, in_ap, r=(), w=(), sem_on=None, **kw):
        if sem_on is None:
            sem_on = (list(w) + list(r))[0]
        return self.emit(q, lambda e: e.dma_start(out=out_ap, in_=in_ap, **kw), r, w, dma=sem_on)

    def final_wait(self, eng, bufs):
        deps = {}
        for b in bufs:
            for dep in [b.lw] + list(b.rd.items()):
                if dep is None:
                    continue
                k, v = dep
                if deps.get(k, 0) < v:
                    deps[k] = v
        waits = [(k, v) for k, v in deps.items() if self.seen[eng].get(k, 0) < v]
        for k, v in waits:
            self.seen[eng][k] = v
        self.lists[eng].append((waits, None, None))

    def barrier(self):
        toks = [(e, self.cnt[e]) for e in ENGS if self.cnt[e] > 0]
        toks += [(ds.key, ds.val) for ds in self.alldsem if ds.val > 0]
        for e in ENGS:
            waits = []
            for k, v in toks:
                if k == e and e == 'pe':
                    continue
                if self.seen[e].get(k, 0) < v:
                    self.seen[e][k] = v
                    waits.append((k, v))
            self.lists[e].append((waits, None, None))

    def build(self):
        nc = self.nc
        semh = self.semh
        lists = self.lists
        needed = {e: set() for e in ENGS}
        for e in ENGS:
            for waits, fn, ds in lists[e]:
                for k, v in waits:
                    if k in needed:
                        needed[k].add(v)
        rank = {}
        for e in ENGS:
            rank[e] = {v: i + 1 for i, v in enumerate(sorted(needed[e]))}

        def run(engname):
            def body(e):
                own = semh[engname]
                seq = 0
                myrank = rank[engname]
                for waits, fn, ds in lists[engname]:
                    for k, v in waits:
                        if k in rank:
                            e.wait_ge(semh[k], rank[k][v])
                        else:
                            e.wait_ge(semh[k], v)
                    if fn is None:
                        continue
                    ins = fn(e)
                    if ds is None:
                        seq += 1
                        if seq in myrank:
                            ins.then_inc(own, 1)
                    else:
                        ins.then_inc(ds.h, 16)
            return body

        with nc.Block() as block:
            block.tensor(run('pe'))
            block.scalar(run('act'))
            block.vector(run('dve'))
            block.gpsimd(run('pool'))
            block.sync(run('sp'))


class View:
    def __init__(self, base, ap):
        self.base = base
        self.t = ap
        self.name = base.name

    def __getitem__(self, idx):
        return self.t[idx]

    lw = property(lambda s: s.base.lw, lambda s, v: setattr(s.base, 'lw', v))
    rd = property(lambda s: s.base.rd, lambda s, v: setattr(s.base, 'rd', v))
    dsem = property(lambda s: s.base.dsem, lambda s, v: setattr(s.base, 'dsem', v))

from contextlib import ExitStack

T = 8192
D = 1024
ST = 2048
NST = T // ST
SUB = 512
EPS = 1e-6
GROUPS = ((128, 1), (512, 4), (2048, 16))
NS = 16
NSL = 22


def unit_tokens(g, u):
    d = GROUPS[g][1]
    if d == 1:
        return slice(128 * u, 128 * u + 128)
    if d == 4:
        b, r = divmod(u, 4)
        return slice(512 * b + r, 512 * b + 512, 4)
    return slice(u, 2048, 16)


def build_program(do_prompt=True, do_sample=True, nst=NST):
    nc = bass.Bass("TRN2", target_bir_lowering=False)

    def din(name, shape):
        return nc.dram_tensor(name, list(shape), F32, kind="ExternalInput").ap()

    def dout(name, shape):
        return nc.dram_tensor(name, list(shape), F32, kind="ExternalOutput").ap()

    x_d = din("x", [T, D])
    xs_d = din("xs", [NS, D])
    c_d = [din("c128", [NS, 128, 1024]), din("c512", [NS, 512, 1024]), din("c2048", [NS, 2048, 1024])]
    sh_d = din("sh", [NS, 8, 128, 128])
    win_d = din("w_in", [D, 11264])
    wap_d = din("wap", [512, D])
    whp_d = din("whp", [D, D])
    wout_d = din("wout", [D, D])
    nw_d = din("nw", [128, 8])
    hnw_d = din("hnw", [128, 1])
    lbl_d = din("lbl", [128, 16])
    fnw_d = din("fnw", [128, D])
    nwr_d = din("nwr", [128, D])
    ident_d = din("ident", [128, 128])
    eb_d = din("eb", [128, 12 * 512])
    hmask_d = din("hmask", [128, 64])
    rsm_d = din("rsm", [128, 512])
    sel_d = din("sel", [128, 64])
    lblr_d = din("lblr", [NS, 2048])
    hnwr_d = din("hnwr", [NS, 1024])
    sbias_d = din("sbias", [128, 24])
    onehot_d = din("onehot", [NS, NS * 128])
    bdm_d = din("bdm", [8, 520])
    en_d = din("en", [8, NS * NS])

    y_d = dout("y", [T, D])
    ys_d = dout("ys", [NS, D])
    kvp_d = [dout("kv128", [128, 1024]), dout("kv512", [512, 1024]), dout("kv2048", [2048, 1024])]
    stp_d = dout("stp", [8, 128, 128])
    kvs_d = [dout("kvs128", [NS, 1024]), dout("kvs512", [NS, 1024]), dout("kvs2048", [NS, 1024])]
    sts_d = dout("sts", [NS, 8, 128, 128])

    wib_t = nc.dram_tensor("wib", [NSL, 128, 8, 512], BF16).ap()
    wapb_t = nc.dram_tensor("wapb", [8, 128, 4, 128], BF16).ap()
    whpb_t = nc.dram_tensor("whpb", [8, 128, 8, 128], BF16).ap()
    woutb_t = nc.dram_tensor("woutb", [2, 128, 8, 512], BF16).ap()
    NU = (1, 4, 16)
    hist_t = [[(nc.dram_tensor("histk%d_%d" % (g, hp), [128, NU[g] * 128], BF16).ap(),
                nc.dram_tensor("histv%d_%d" % (g, hp), [128, NU[g] * 132], BF16).ap())
               for hp in range(4)] for g in range(3)]

    with ExitStack() as stack:
        P = Prog(nc, stack)
        out_bufs = []

        wib = [P.wrap("wib%d" % k, wib_t) for k in range(3)]
        wsrc = win_d.rearrange("(c p) (s n) -> s p c n", p=128, n=512)
        for k, (s0, s1) in enumerate(((0, 8), (8, 16), (16, 22))):
            for s in range(s0, s1):
                P.dma('pool', wib_t[s], wsrc[s], w=[wib[k]], sem_on=wib[k])

        def wib_buf(s):
            return wib[0 if s < 8 else (1 if s < 16 else 2)]

        wapb = P.wrap("wapb", wapb_t)
        whpb = P.wrap("whpb", whpb_t)
        woutb = P.wrap("woutb", woutb_t)
        s_ap = wap_d.rearrange("(hp p) (j n) -> j p hp n", p=128, n=128)
        s_hp = whp_d.rearrange("(h p) (j n) -> j p h n", p=128, n=128)
        for j in range(8):
            P.dma('pool', wapb_t[j], s_ap[j], w=[wapb], sem_on=wapb)
            P.dma('pool', whpb_t[j], s_hp[j], w=[whpb], sem_on=whpb)
        s_wo = wout_d.rearrange("(c p) (s n) -> s p c n", p=128, n=512)
        for s in range(2):
            P.dma('pool', woutb_t[s], s_wo[s], w=[woutb], sem_on=woutb)

        ident_f = P.sb("ident_f", [128, 128], F32)
        ident = P.sb("ident", [128, 128], BF16)
        eb = P.sb("eb", [128, 12, 512], BF16)
        hmask = P.sb("hmask", [128, 64], F32)
        rsm = P.sb("rsm", [128, 512], F32)
        sel = P.sb("sel", [128, 64], F32)
        nw = P.sb("nw", [128, 8], F32)
        hnw = P.sb("hnw", [128, 1], F32)
        lbl = P.sb("lbl", [128, 16], F32)
        fnw = P.sb("fnw", [128, D], F32)
        epsc = P.sb("epsc", [128, 1], F32)
        ones_f = P.sb("ones_f", [128, 128], F32)
        lbc = P.sb("lbc", [128, 32], F32)
        P.dma('sp', ident_f[:], ident_d, w=[ident_f])
        for i in range(12):
            P.dma('pool', eb[:, i, :], eb_d[:, i * 512:(i + 1) * 512], w=[eb], sem_on=eb)
        P.dma('sp', hmask[:], hmask_d, w=[hmask])
        P.dma('sp', rsm[:], rsm_d, w=[rsm])
        P.dma('sp', sel[:], sel_d, w=[sel])
        P.dma('sp', nw[:], nw_d, w=[nw])
        P.dma('sp', hnw[:], hnw_d, w=[hnw])
        P.dma('sp', lbl[:], lbl_d, w=[lbl])
        P.dma('sp', fnw[:], fnw_d, w=[fnw])
        P.dve(lambda e: e.tensor_copy(out=ident[:], in_=ident_f[:]), r=[ident_f], w=[ident])
        P.pool(lambda e: e.memset(epsc[:], EPS), w=[epsc])
        P.pool(lambda e: e.memset(ones_f[:], 1.0), w=[ones_f])
        P.dve(lambda e: e.tensor_tensor(out=lbc[:, 24:32], in0=lbl[:, 0:8], in1=lbl[:, 8:16], op=ALU.subtract), r=[lbl], w=[lbc])
        P.act(lambda e: e.activation(out=lbc[:, 0:8], in_=lbc[:, 24:32], func=AF.Sigmoid), r=[lbc], w=[lbc])
        P.dve(lambda e: e.tensor_scalar(out=lbc[:, 8:16], in0=lbc[:, 0:8], scalar1=-1.0, scalar2=1.0, op0=ALU.mult, op1=ALU.add), r=[lbc], w=[lbc])
        P.dve(lambda e: e.tensor_scalar(out=lbc[:, 16:24], in0=lbc[:, 0:8], scalar1=1.0, scalar2=-1.0, op0=ALU.mult, op1=ALU.add), r=[lbc], w=[lbc])

        pb = [P.ps("pb%d" % i, [128, 512], F32) for i in range(7)]
        pbf = P.ps("pbf", [128, 1024], BF16)

        wsl = [P.sb("wsl%d" % i, [128, 8, 512], BF16) for i in range(3)]
        wsl_i = [0]

        def load_slice(s):
            b = wsl[wsl_i[0] % 3]
            wsl_i[0] += 1
            P.dma('sp', b[:], wib_t[s], r=[wib_buf(s)], w=[b], sem_on=b)
            return b

        def rstd_from_ss(dst, ss, scale, n):
            P.act(lambda e: e.activation(out=dst[:, 0:n], in_=ss[:, 0:n], func=AF.Ln, bias=epsc[:, 0:1], scale=scale), r=[ss, epsc], w=[dst])
            P.act(lambda e: e.activation(out=dst[:, 0:n], in_=dst[:, 0:n], func=AF.Exp, scale=-0.5), r=[dst], w=[dst])

        Lc = locals()
        if do_sample:
            with ExitStack() as sstack:
                P.stack = sstack
                sample_path(P, Lc)
                P.barrier()
            P.stack = stack
        if do_prompt:
            prompt_path(P, Lc)
        P.final_wait('sp', out_bufs)
        P.build()
    return nc


def prompt_path(P, L):
    nc = L['nc']; pb = L['pb']; pbf = L['pbf']; out_bufs = L['out_bufs']
    x_d = L['x_d']; y_d = L['y_d']; kvp_d = L['kvp_d']; stp_d = L['stp_d']
    ident = L['ident']; eb = L['eb']; hmask = L['hmask']; rsm = L['rsm']; sel = L['sel']
    hnw = L['hnw']; fnw = L['fnw']; lbc = L['lbc']; epsc = L['epsc']; ones_f = L['ones_f']
    load_slice = L['load_slice']; rstd_from_ss = L['rstd_from_ss']
    hist_t = L['hist_t']; NU = L['NU']; nst = L['nst']
    wapb_t = L['wapb_t']; whpb_t = L['whpb_t']; woutb_t = L['woutb_t']
    wapb = L['wapb']; whpb = L['whpb']; woutb = L['woutb']
    nwr_d = L['nwr_d']

    nwr = P.sb("nwr", [128, D], F32)
    P.dma('sp', nwr[:], nwr_d, w=[nwr])
    hT = P.sb("hT", [128, 8, ST], BF16)
    xt = [P.sb("xt%d" % i, [128, D], F32) for i in range(2)]
    xn = P.sb("xn", [128, D], BF16)
    ssn = P.sb("ssn", [128, 4], F32)
    rsn = P.sb("rsn", [128, 4], F32)
    qU = P.sb("qU", [128, 16, 128], BF16)
    kU = P.sb("kU", [128, 16, 128], BF16)
    vU = P.sb("vU", [128, 16, 2, 66], BF16)
    kprev = P.sb("kprev", [128, 16, 128], BF16)
    vprev = P.sb("vprev", [128, 16, 2, 66], BF16)
    big = P.sb("big", [128, 9 * 512], F32)
    acc = Buf("acc", big.t[0:65, 0:4096].rearrange("p (h t) -> p h t", h=2))
    tmps = [Buf("tmp%d" % i, big.t[:, i * 512:(i + 1) * 512]) for i in range(9)]
    esb = [[P.sb("esb%d_%d" % (i, hh), [128, 512], BF16) for hh in range(2)] for i in range(2)]
    pTb = [[[P.sb("pTb%d_%d_%d" % (i, hh, uu), [128, 256], BF16) for uu in range(2)] for hh in range(2)] for i in range(2)]
    attT = P.sb("attT", [128, 4, ST], BF16)
    att1 = P.sb("att1", [64, ST], BF16)
    gsA = P.sb("gsA", [64, 512], F32)
    tmpA = P.sb("tmpA", [64, 512], F32)
    rcpA = tmps[8]
    Sck = View(kU, kU.t[:, 0:8, :])
    aTm8 = View(qU, qU.t[:, 0:4, :].rearrange("p a (b t) -> p (a b) t", t=64))
    aTms = [P.sb("aTm%d" % i, [128, 64], BF16) for i in range(2)]
    Sxs = [P.sb("Sx%d" % i, [128, 128], BF16) for i in range(2)]
    histb = [[P.wrap("hist%d_%d" % (g, hp), hist_t[g][hp][0]) for hp in range(4)] for g in range(3)]
    vI = P.sb("vI", [128, 4, 1024], BF16)
    S32 = P.sb("S32", [128, 8, 128], F32)
    Sbf = P.sb("Sbf", [128, 8, 128], BF16)
    sig = tmps[0]
    logf = tmps[1]
    hk = tmps[2]
    bcs = tmps[3]
    Ep = tmps[4]
    En = tmps[5]
    hq = tmps[6]
    kt32 = tmps[7]
    qtl = P.sb("qtl", [128, 512], BF16)
    ktl = P.sb("ktl", [128, 512], BF16)
    kend = P.sb("kend", [128, 512], BF16)
    kendT = P.sb("kendT", [128, 4, 128], BF16)
    o32 = tmps[3]
    o2 = tmps[1]
    rs_h = tmps[5]
    gsH = tmps[8]
    hoT = P.sb("hoT", [128, 8, 512], BF16)
    ga = tmps[0]
    gbt = tmps[1]
    m1 = tmps[2]
    m2 = tmps[3]
    mergedT = View(vI, vI.t[:].rearrange("p a b -> p (a b)").rearrange("p (c t) -> p c t", t=512))
    wapj = [P.sb("wapj%d" % i, [128, 4, 128], BF16) for i in range(2)]
    whpj = [P.sb("whpj%d" % i, [128, 8, 128], BF16) for i in range(2)]
    xr = P.sb("xr", [128, D], F32)
    yo = [P.sb("yo%d" % i, [128, D], F32) for i in range(2)]
    kvtb = yo
    sq = xr
    ssf = P.sb("ssf", [128, 4], F32)
    rsf = P.sb("rsf", [128, 4], F32)

    P.pool(lambda e: e.memset(vU[:].rearrange("p u h e -> p (u h e)"), 1.0), w=[vU])
    P.pool(lambda e: e.memset(kprev[:].rearrange("p u i -> p (u i)"), 0.0), w=[kprev])
    P.pool(lambda e: e.memset(vprev[:].rearrange("p u h e -> p (u h e)"), 0.0), w=[vprev])
    P.pool(lambda e: e.memset(S32[:].rearrange("p h v -> p (h v)"), 0.0), w=[S32])
    P.pool(lambda e: e.memset(Sbf[:].rearrange("p h v -> p (h v)"), 0.0), w=[Sbf])

    def mm(bank_ap, lhsT, rhs, first, last, r, w):
        P.pe(lambda e: e.matmul(bank_ap, lhsT=lhsT, rhs=rhs, start=first, stop=last), r=r, w=w)

    def proj_fm(bank, wsb, off, m, tsl):
        for c in range(8):
            mm(bank[0:m, :], wsb[:, c, off:off + m], hT[:, c, tsl], c == 0, c == 7, [wsb, hT], [bank])

    def phase_n(st):
        for tt in range(16):
            t0 = st * ST + tt * 128
            xb = xt[tt % 2]
            P.dma('sp', xb[:], x_d[t0:t0 + 128, :], w=[xb])
            P.act(lambda e, xb=xb: e.activation(out=sq[:], in_=xb[:], func=AF.Square), r=[xb], w=[sq])
            P.dve(lambda e: e.tensor_reduce(out=ssn[:, 0:1], in_=sq[:], axis=AX.X, op=ALU.add), r=[sq], w=[ssn])
            rstd_from_ss(rsn, ssn, 1.0 / D, 1)
            P.dve(lambda e, xb=xb: e.scalar_tensor_tensor(out=xn[:], in0=xb[:], scalar=rsn[:, 0:1], in1=nwr[:], op0=ALU.mult, op1=ALU.mult), r=[xb, rsn, nwr], w=[xn])
            for c in range(8):
                P.pe(lambda e, c=c: e.transpose(out=pbf[:, c * 128:(c + 1) * 128], in_=xn[:, c * 128:(c + 1) * 128], identity=ident[:]), r=[xn, ident], w=[pbf])
            P.act(lambda e, tt=tt: e.copy(out=hT[:, :, tt * 128:(tt + 1) * 128], in_=pbf[:].rearrange("p (c t) -> p c t", t=128)), r=[pbf], w=[hT])

    def att_gh(st, g, hp):
        d = GROUPS[g][1]
        nu = NU[g]
        off = hp * 128
        wq = load_slice(3 * g)
        wk = load_slice(3 * g + 1)
        wv = load_slice(3 * g + 2)
        hb = histb[g][hp]
        htk, htv = hist_t[g][hp]
        if st > 0:
            P.dma('sp', kprev[:, 0:nu, :].rearrange("p u i -> p (u i)"), htk, r=[hb], w=[kprev], sem_on=kprev)
            P.dma('sp', vprev[:, 0:nu].rearrange("p u h e -> p (u h e)"), htv, r=[hb], w=[vprev], sem_on=vprev)
        for wsb, dst, eng in ((wq, qU, 'act'), (wk, kU, 'dve')):
            for n in range(4):
                bank = pb[n % 2]
                proj_fm(bank, wsb, off, 128, slice(n * 512, (n + 1) * 512))
                if d == 1:
                    o_ap = dst[:, 4 * n:4 * n + 4, :]
                    i_ap = bank[:].rearrange("p (u i) -> p u i", i=128)
                elif d == 4:
                    o_ap = dst[:, 4 * n:4 * n + 4, :]
                    i_ap = bank[:].rearrange("p (i r) -> p r i", r=4)
                else:
                    o_ap = dst[:, :, 32 * n:32 * n + 32]
                    i_ap = bank[:].rearrange("p (i r) -> p r i", r=16)
                if eng == 'act':
                    P.act(lambda e, o_ap=o_ap, i_ap=i_ap: e.copy(out=o_ap, in_=i_ap), r=[bank], w=[dst])
                else:
                    P.dve(lambda e, o_ap=o_ap, i_ap=i_ap: e.tensor_copy(out=o_ap, in_=i_ap), r=[bank], w=[dst])
        import os
        sub = int(os.environ.get("KSUB", "9"))
        if sub < 1:
            return
        for ug in range(4):
            bank = pb[2 + ug % 2]
            for uu in range(4):
                ts = unit_tokens(g, ug * 4 + uu)
                for c in range(8):
                    mm(bank[:, uu * 128:(uu + 1) * 128], hT[:, c, ts], wv[:, c, off:off + 128], c == 0, c == 7, [hT, wv], [bank])
            o_ap = vU[:, ug * 4:ug * 4 + 4, :, 0:64]
            i_ap = bank[:].rearrange("p (u h e) -> p u h e", u=4, h=2)
            if ug % 2 == 0:
                P.act(lambda e, o_ap=o_ap, i_ap=i_ap: e.copy(out=o_ap, in_=i_ap), r=[bank], w=[vU])
            else:
                P.dve(lambda e, o_ap=o_ap, i_ap=i_ap: e.tensor_copy(out=o_ap, in_=i_ap), r=[bank], w=[vU])
        if st == NST - 1 and hp == 0:
            ntile = GROUPS[g][0] // 128
            for tt in range(16 - ntile, 16):
                kvt = kvtb[tt % 2]
                for half, wsb in enumerate((wk, wv)):
                    bank = pb[half]
                    for c in range(8):
                        mm(bank[:], hT[:, c, tt * 128:(tt + 1) * 128], wsb[:, c, :], c == 0, c == 7, [hT, wsb], [bank])
                    if half == 0:
                        P.act(lambda e, kvt=kvt, bank=bank: e.copy(out=kvt[:, 0:512], in_=bank[:]), r=[bank], w=[kvt])
                    else:
                        P.dve(lambda e, kvt=kvt, bank=bank: e.tensor_copy(out=kvt[:, 512:1024], in_=bank[:]), r=[bank], w=[kvt])
                row0 = (tt - (16 - ntile)) * 128
                P.dma('sp', kvp_d[g][row0:row0 + 128, :], kvt[:], r=[kvt], sem_on=kvt)
                if kvt not in out_bufs:
                    out_bufs.append(kvt)
        if sub < 2:
            return
        def prev_of(u):
            if g == 0:
                return (kU, u - 1) if u > 0 else (kprev, 0)
            if g == 1:
                return (kU, u - 4) if u >= 4 else (kprev, u)
            return (kprev, u)

        ebi = g * 4 + hp
        sbks = ((pb[4], pb[5]), (pb[0], pb[1]))
        obs = (pb[6], pb[2])

        def scores(up):
            for hh in range(2):
                lo = hh * 64
                sbk = sbks[up % 2][hh]
                for uu in range(2):
                    u = up * 2 + uu
                    kpb, ku = prev_of(u)
                    mm(sbk[:, uu * 256:uu * 256 + 128], kU[lo:lo + 64, u, :], qU[lo:lo + 64, u, :], True, True, [kU, qU], [sbk])
                    mm(sbk[:, uu * 256 + 128:uu * 256 + 256], kpb[lo:lo + 64, ku, :], qU[lo:lo + 64, u, :], True, True, [kpb, qU], [sbk])

        def softmax(up):
            for hh in range(2):
                e_ = esb[up % 2][hh]
                sbk = sbks[up % 2][hh]
                P.act(lambda e, e_=e_, sbk=sbk: e.activation(out=e_[:], in_=sbk[:], func=AF.Exp, scale=0.125), r=[sbk], w=[e_])
                for uu in range(2):
                    p_ = pTb[up % 2][hh][uu]
                    fn = lambda e, e_=e_, p_=p_, hh=hh, uu=uu: e.tensor_tensor(out=p_[:], in0=e_[:, uu * 256:(uu + 1) * 256], in1=eb[:, ebi, hh * 256:(hh + 1) * 256], op=ALU.mult)
                    if uu == 0 or os.environ.get("KPOOL", "0") == "0":
                        P.dve(fn, r=[e_, eb], w=[p_])
                    else:
                        P.pool(fn, r=[e_, eb], w=[p_])

        def pv(up):
            for uu in range(2):
                u = up * 2 + uu
                ts = unit_tokens(g, u)
                kpb, ku = prev_of(u)
                vpb = vU if kpb is kU else vprev
                ob = obs[uu]
                for hh in range(2):
                    p_ = pTb[up % 2][hh][uu]
                    mm(ob[0:65, hh * 128:(hh + 1) * 128], vU[:, u, hh, 0:65], p_[:, 0:128], True, False, [vU, p_], [ob])
                    mm(ob[0:65, hh * 128:(hh + 1) * 128], vpb[:, ku, hh, 0:65], p_[:, 128:256], False, True, [vpb, p_], [ob])
                accv = acc[:, :, ts]
                src = ob[0:65, 0:256].rearrange("p (h q) -> p h q", h=2)
                if g == 0:
                    P.dve(lambda e, accv=accv, src=src: e.tensor_copy(out=accv, in_=src), r=[ob], w=[acc])
                else:
                    P.dve(lambda e, accv=accv, src=src: e.tensor_tensor(out=accv, in0=accv, in1=src, op=ALU.add), r=[ob, acc], w=[acc])

        scores(0)
        softmax(0)
        for up in range(8):
            if up + 1 < 8:
                scores(up + 1)
                softmax(up + 1)
            pv(up)
        if st < nst - 1:
            u0 = 16 - nu
            if g == 2:
                P.dve(lambda e: e.tensor_copy(out=kprev[:].rearrange("p u i -> p (u i)"), in_=kU[:].rearrange("p u i -> p (u i)")), r=[kU], w=[kprev])
                P.dma('sp', htk, kprev[:].rearrange("p u i -> p (u i)"), r=[kprev], w=[hb], sem_on=hb)
                if st == 0:
                    P.pool(lambda e: e.memset(kprev[:].rearrange("p u i -> p (u i)"), 0.0), w=[kprev])
            else:
                P.dma('sp', htk, kU[:, u0:16, :].rearrange("p u i -> p (u i)"), r=[kU], w=[hb], sem_on=hb)
            P.dma('sp', htv, vU[:, u0:16].rearrange("p u h e -> p (u h e)"), r=[vU], w=[hb], sem_on=hb)

    def att_final(hp):
        wg = load_slice(9)
        for hh in range(2):
            h = hp * 2 + hh
            for n in range(4):
                tsl = slice(n * 512, (n + 1) * 512)
                rb = pb[2]
                gbk = pb[3]
                mm(rb[0:64, :], sel[0:65, :], acc[0:65, hh, tsl], True, True, [sel, acc], [rb])
                proj_fm(gbk, wg, h * 64, 64, tsl)
                P.act(lambda e, gbk=gbk: e.activation(out=gsA[:], in_=gbk[0:64, :], func=AF.Silu), r=[gbk], w=[gsA])
                P.dve(lambda e, rb=rb: e.reciprocal(out=rcpA[0:64, :], in_=rb[0:64, :]), r=[rb], w=[rcpA])
                P.dve(lambda e, hh=hh, tsl=tsl: e.tensor_tensor(out=tmpA[:], in0=acc[0:64, hh, tsl], in1=rcpA[0:64, :], op=ALU.mult), r=[acc, rcpA], w=[tmpA])
                dst = attT[0:64, hp, tsl] if hh == 0 else att1[:, tsl]
                dbuf = attT if hh == 0 else att1
                P.dve(lambda e, dst=dst: e.tensor_tensor(out=dst, in0=tmpA[:], in1=gsA[:], op=ALU.mult), r=[tmpA, gsA], w=[dbuf])
            if hh == 1:
                P.dma('sp', attT[64:128, hp, :], att1[:], r=[att1], w=[attT], sem_on=att1)

    def hgrn_sub(st, j):
        c0 = j * 512
        tsl = slice(c0, c0 + 512)
        wi = [load_slice(14), load_slice(15)]
        for tt in range(4):
            for half in range(2):
                bank = pb[half]
                for c in range(8):
                    mm(bank[:], hT[:, c, c0 + tt * 128:c0 + (tt + 1) * 128], wi[half][:, c, :], c == 0, c == 7, [hT, wi[half]], [bank])
                o_ap = vI[:, tt, half * 512:(half + 1) * 512]
                if half == 0:
                    P.act(lambda e, o_ap=o_ap, bank=bank: e.copy(out=o_ap, in_=bank[:]), r=[bank], w=[vI])
                else:
                    P.dve(lambda e, o_ap=o_ap, bank=bank: e.tensor_copy(out=o_ap, in_=bank[:]), r=[bank], w=[vI])
        for h4 in range(2):
            wq_ = load_slice(10 + h4)
            wf_ = load_slice(12 + h4)
            wg_ = load_slice(16 + h4)
            for hq_ in range(4):
                h = h4 * 4 + hq_
                off = hq_ * 128
                bF = pb[0]
                proj_fm(bF, wf_, off, 128, tsl)
                P.act(lambda e, bF=bF: e.activation(out=sig[:], in_=bF[:], func=AF.Sigmoid), r=[bF], w=[sig])
                bQ = pb[1]
                proj_fm(bQ, wq_, off, 128, tsl)
                P.act(lambda e, bQ=bQ: e.activation(out=hq[:], in_=bQ[:], func=AF.Silu), r=[bQ], w=[hq])
                bG = pb[2]
                proj_fm(bG, wg_, off, 128, tsl)
                P.act(lambda e, bG=bG: e.activation(out=gsH[:], in_=bG[:], func=AF.Silu), r=[bG], w=[gsH])
                P.act(lambda e, h=h: e.activation(out=logf[:], in_=sig[:], func=AF.Ln, bias=lbc[:, h:h + 1], scale=lbc[:, 8 + h:9 + h]), r=[sig, lbc], w=[logf])
                P.dve(lambda e, h=h: e.tensor_scalar(out=hk[:], in0=sig[:], scalar1=lbc[:, 16 + h:17 + h], scalar2=lbc[:, 8 + h:9 + h], op0=ALU.mult, op1=ALU.add), r=[sig, lbc], w=[hk])
                P.dve(lambda e: e.tensor_tensor_scan(out=bcs[:], data0=rsm[:], data1=logf[:], initial=0.0, op0=ALU.mult, op1=ALU.add), r=[rsm, logf], w=[bcs])
                P.act(lambda e: e.activation(out=Ep[:], in_=bcs[:], func=AF.Exp), r=[bcs], w=[Ep])
                P.act(lambda e: e.activation(out=En[:], in_=bcs[:], func=AF.Exp, scale=-1.0), r=[bcs], w=[En])
                P.dve(lambda e: e.scalar_tensor_tensor(out=qtl[:], in0=hq[:], scalar=float(128 ** -0.5), in1=Ep[:], op0=ALU.mult, op1=ALU.mult), r=[hq, Ep], w=[qtl])
                P.dve(lambda e: e.tensor_tensor(out=kt32[:], in0=hk[:], in1=En[:], op=ALU.mult), r=[hk, En], w=[kt32])
                P.dve(lambda e: e.tensor_copy(out=ktl[:], in_=kt32[:]), r=[kt32], w=[ktl])
                for cc in range(8):
                    P.dve(lambda e, cc=cc: e.tensor_scalar(out=kend[:, cc * 64:(cc + 1) * 64], in0=kt32[:, cc * 64:(cc + 1) * 64], scalar1=Ep[:, cc * 64 + 63:cc * 64 + 64], scalar2=None, op0=ALU.mult), r=[kt32, Ep], w=[kend])
                for tt in range(4):
                    P.pe(lambda e, tt=tt: e.transpose(out=pbf[:, tt * 128:(tt + 1) * 128], in_=kend[:, tt * 128:(tt + 1) * 128], identity=ident[:]), r=[kend, ident], w=[pbf])
                P.act(lambda e: e.copy(out=kendT[:].rearrange("p a b -> p (a b)"), in_=pbf[:, 0:512]), r=[pbf], w=[kendT])
                bOs = (pb[3], pb[2])
                for cc in range(8):
                    bO = bOs[cc % 2]
                    osl = slice((cc // 2) * 64, (cc // 2) * 64 + 64)
                    tt = cc // 2
                    lo = (cc % 2) * 64
                    csl = slice(cc * 64, (cc + 1) * 64)
                    bA = pb[4]
                    aTm = aTms[cc % 2]
                    mm(bA[lo:lo + 64, 0:64], ktl[:, csl], qtl[:, csl], True, True, [ktl, qtl], [bA])
                    bS = pb[5 + cc % 2]
                    mm(bS[:, 0:128], kendT[lo:lo + 64, tt, :], vI[lo:lo + 64, tt, h * 128:(h + 1) * 128], True, True, [kendT, vI], [bS])
                    P.dve(lambda e, lo=lo, bA=bA, aTm=aTm: e.tensor_tensor(out=aTm[lo:lo + 64, :], in0=bA[lo:lo + 64, 0:64], in1=hmask[lo:lo + 64, :], op=ALU.mult), r=[bA, hmask], w=[aTm])
                    if cc == 0:
                        mm(bO[:, osl], Sbf[:, h, :], qtl[:, csl], True, False, [Sbf, qtl], [bO])
                    else:
                        Sp = Sxs[(cc - 1) % 2]
                        mm(bO[:, osl], Sp[:], qtl[:, csl], True, False, [Sp, qtl], [bO])
                    mm(bO[:, osl], vI[lo:lo + 64, tt, h * 128:(h + 1) * 128], aTm[lo:lo + 64, :], False, True, [vI, aTm], [bO])
                    P.dve(lambda e, h=h, cc=cc, bS=bS: e.scalar_tensor_tensor(out=S32[:, h, :], in0=S32[:, h, :], scalar=Ep[:, cc * 64 + 63:cc * 64 + 64], in1=bS[:, 0:128], op0=ALU.mult, op1=ALU.add), r=[S32, Ep, bS], w=[S32])
                    if cc < 7:
                        Sn_ = Sxs[cc % 2]
                        P.dve(lambda e, h=h, Sn_=Sn_: e.tensor_copy(out=Sn_[:], in_=S32[:, h, :]), r=[S32], w=[Sn_])
                    else:
                        P.dve(lambda e, h=h: e.tensor_copy(out=Sbf[:, h, :], in_=S32[:, h, :]), r=[S32], w=[Sbf])
                for par in range(2):
                    bO = bOs[par]
                    o2v = o2[:].rearrange("p (a b t) -> p a b t", b=2, t=64)[:, :, par, :]
                    o32v = o32[:].rearrange("p (a b t) -> p a b t", b=2, t=64)[:, :, par, :]
                    srcv = bO[:, 0:256].rearrange("p (a t) -> p a t", t=64)
                    P.act(lambda e, o2v=o2v, srcv=srcv: e.activation(out=o2v, in_=srcv, func=AF.Square), r=[bO], w=[o2])
                    P.dve(lambda e, o32v=o32v, srcv=srcv: e.tensor_copy(out=o32v, in_=srcv), r=[bO], w=[o32])
                bN = pb[0]
                mm(bN[:], ones_f[:], o2[:], True, True, [ones_f, o2], [bN])
                P.act(lambda e, bN=bN: e.activation(out=rs_h[:], in_=bN[:], func=AF.Ln, bias=epsc[:, 0:1], scale=1.0 / 128), r=[bN, epsc], w=[rs_h])
                P.act(lambda e: e.activation(out=rs_h[:], in_=rs_h[:], func=AF.Exp, scale=-0.5), r=[rs_h], w=[rs_h])
                P.dve(lambda e: e.scalar_tensor_tensor(out=o32[:], in0=o32[:], scalar=hnw[:, 0:1], in1=rs_h[:], op0=ALU.mult, op1=ALU.mult), r=[o32, hnw, rs_h], w=[o32])
                P.dve(lambda e, h=h: e.tensor_tensor(out=hoT[:, h, :], in0=o32[:], in1=gsH[:], op=ALU.mult), r=[o32, gsH], w=[hoT])

    def out_sub(st, j):
        c0 = j * 512
        tsl = slice(c0, c0 + 512)
        wma = [None, None]
        wmb = [None, None]
        for jj in range(8):
            if jj % 4 == 0:
                wma[jj // 4] = load_slice(18 + jj // 4)
                wmb[jj // 4] = load_slice(20 + jj // 4)
            wa_ = wapj[jj % 2]
            wh_ = whpj[jj % 2]
            P.dma('sp', wa_[:], wapb_t[jj], r=[wapb], w=[wa_], sem_on=wa_)
            P.dma('sp', wh_[:], whpb_t[jj], r=[whpb], w=[wh_], sem_on=wh_)
            off = (jj % 4) * 128
            bA = pb[0]
            proj_fm(bA, wma[jj // 4], off, 128, tsl)
            P.act(lambda e, bA=bA: e.activation(out=ga[:], in_=bA[:], func=AF.Sigmoid), r=[bA], w=[ga])
            bB = pb[1]
            proj_fm(bB, wmb[jj // 4], off, 128, tsl)
            P.act(lambda e, bB=bB: e.activation(out=gbt[:], in_=bB[:], func=AF.Sigmoid), r=[bB], w=[gbt])
            b1 = pb[2]
            for hp in range(4):
                mm(b1[:], wa_[:, hp, :], attT[:, hp, tsl], hp == 0, hp == 3, [wa_, attT], [b1])
            b2 = pb[3]
            for h in range(8):
                mm(b2[:], wh_[:, h, :], hoT[:, h, :], h == 0, h == 7, [wh_, hoT], [b2])
            P.dve(lambda e, b1=b1: e.tensor_tensor(out=m1[:], in0=ga[:], in1=b1[:], op=ALU.mult), r=[ga, b1], w=[m1])
            P.dve(lambda e, b2=b2: e.tensor_tensor(out=m2[:], in0=gbt[:], in1=b2[:], op=ALU.mult), r=[gbt, b2], w=[m2])
            P.dve(lambda e, jj=jj: e.tensor_tensor(out=mergedT[:, jj, :], in0=m1[:], in1=m2[:], op=ALU.add), r=[m1, m2], w=[mergedT])
        wo = []
        for s in range(2):
            b = L['wsl'][L['wsl_i'][0] % 3]
            L['wsl_i'][0] += 1
            P.dma('sp', b[:], woutb_t[s], r=[woutb], w=[b], sem_on=b)
            wo.append(b)
        for tt in range(4):
            t0 = st * ST + c0 + tt * 128
            xb = xt[tt % 2]
            P.dma('sp', xb[:], x_d[t0:t0 + 128, :], w=[xb])
            for half in range(2):
                bank = pb[4 + half]
                for c in range(8):
                    mm(bank[:], mergedT[:, c, tt * 128:(tt + 1) * 128], wo[half][:, c, :], c == 0, c == 7, [mergedT, wo[half]], [bank])
                P.dve(lambda e, half=half, bank=bank, xb=xb: e.tensor_tensor(out=xr[:, half * 512:(half + 1) * 512], in0=bank[:], in1=xb[:, half * 512:(half + 1) * 512], op=ALU.add), r=[bank, xb], w=[xr])
            yb = yo[tt % 2]
            P.act(lambda e, yb=yb: e.activation(out=yb[:], in_=xr[:], func=AF.Square), r=[xr], w=[yb])
            P.dve(lambda e, yb=yb: e.tensor_reduce(out=ssf[:, 0:1], in_=yb[:], axis=AX.X, op=ALU.add), r=[yb], w=[ssf])
            rstd_from_ss(rsf, ssf, 1.0 / D, 1)
            P.dve(lambda e, yb=yb: e.scalar_tensor_tensor(out=yb[:], in0=xr[:], scalar=rsf[:, 0:1], in1=fnw[:], op0=ALU.mult, op1=ALU.mult), r=[xr, rsf, fnw], w=[yb])
            P.dma('sp', y_d[t0:t0 + 128, :], yb[:], r=[yb], sem_on=yb)
            if yb not in out_bufs:
                out_bufs.append(yb)

    import os
    stage0 = int(os.environ.get("KSTAGE", "9"))
    stage1 = int(os.environ.get("KSTAGE1", "9"))
    for st in range(nst):
        stage = stage0 if st == 0 else stage1
        if st > 0:
            P.barrier()
        if stage >= 1:
            phase_n(st)
        if stage >= 2:
            P.dve(lambda e: e.memset(acc[0:1, 0, 0:1], 0.0), w=tmps + [acc])
            for hp in range(4):
                for g in range(3):
                    att_gh(st, g, hp)
                if stage >= 3:
                    att_final(hp)
        if stage >= 4:
            P.barrier()
            P.dve(lambda e: e.memset(tmps[0][0:1, 0:1], 0.0), w=[acc] + tmps)
            for j in range(4):
                hgrn_sub(st, j)
                if stage >= 5:
                    out_sub(st, j)
    P.dma('sp', stp_d.rearrange("h k v -> k h v"), S32[:], r=[S32], sem_on=S32)
    out_bufs.append(S32)


def sample_path(P, L):
    nc = L['nc']; pb = L['pb']; pbf = L['pbf']; out_bufs = L['out_bufs']
    xs_d = L['xs_d']; c_d = L['c_d']; sh_d = L['sh_d']; ys_d = L['ys_d']; kvs_d = L['kvs_d']; sts_d = L['sts_d']
    ident = L['ident']; ident_f = L['ident_f']; fnw = L['fnw']; epsc = L['epsc']; ones_f = L['ones_f']
    load_slice = L['load_slice']; rstd_from_ss = L['rstd_from_ss']
    wapb_t = L['wapb_t']; whpb_t = L['whpb_t']; woutb_t = L['woutb_t']
    wapb = L['wapb']; whpb = L['whpb']; woutb = L['woutb']
    nwr_d = L['nwr_d']; lblr_d = L['lblr_d']; hnwr_d = L['hnwr_d']; sbias_d = L['sbias_d']
    onehot_d = L['onehot_d']; bdm_d = L['bdm_d']; en_d = L['en_d']
    A = AF
    OFF_G = 4608; OFF_Q = 5120; OFF_F = 6144; OFF_I = 7168; OFF_HG = 8192; OFF_MA = 9216; OFF_MB = 10240

    def sb(name, shape, dt=F32):
        return P.sb("s_" + name, shape, dt)

    class phase:
        def __enter__(self):
            self.outer = P.stack
            self.es = ExitStack()
            self.es.__enter__()
            P.stack = self.es
            return self

        def __exit__(self, *a):
            P.barrier()
            P.stack = self.outer
            return self.es.__exit__(*a)

    att_b = sb("att_b", [NS, 512], BF16)
    ho_b = sb("ho_b", [NS, 1024], BF16)

    nwrs = sb("nwrs", [NS, D]); P.dma('sp', nwrs[:], nwr_d[0:NS, :], w=[nwrs])
    lblr = sb("lblr", [NS, 2048]); P.dma('sp', lblr[:], lblr_d, w=[lblr])
    hnwr = sb("hnwr", [NS, 1024]); P.dma('sp', hnwr[:], hnwr_d, w=[hnwr])
    sbias = sb("sbias", [128, 24]); P.dma('sp', sbias[:], sbias_d, w=[sbias])
    onehot = sb("onehot", [NS, NS * 128]); P.dma('sp', onehot[:], onehot_d, w=[onehot])
    bdm = sb("bdm", [8, 520]); P.dma('sp', bdm[:], bdm_d, w=[bdm])
    en = sb("en", [8, NS * NS]); P.dma('sp', en[:], en_d, w=[en])
    xs = sb("xs", [NS, D]); P.dma('sp', xs[:], xs_d, w=[xs])
    sq = sb("sq", [NS, D])
    st1 = sb("st1", [NS, 16]); st2 = sb("st2", [NS, 16])
    xn = sb("xn", [NS, D], BF16)
    hsT = sb("hsT", [128, 8, NS], BF16)
    projs = sb("projs", [NS, 11264])

    def mm(bank_ap, lhsT, rhs, first, last, r, w):
        P.pe(lambda e: e.matmul(bank_ap, lhsT=lhsT, rhs=rhs, start=first, stop=last), r=r, w=w)

    def rms_rows(dst_bf, src, wrow):
        P.act(lambda e: e.activation(out=sq[:], in_=src[:], func=A.Square), r=[src], w=[sq])
        P.dve(lambda e: e.tensor_reduce(out=st1[:, 0:1], in_=sq[:], axis=AX.X, op=ALU.add), r=[sq], w=[st1])
        P.act(lambda e: e.activation(out=st2[:, 0:1], in_=st1[:, 0:1], func=A.Ln, bias=epsc[0:NS, 0:1], scale=1.0 / D), r=[st1, epsc], w=[st2])
        P.act(lambda e: e.activation(out=st2[:, 0:1], in_=st2[:, 0:1], func=A.Exp, scale=-0.5), r=[st2], w=[st2])
        P.dve(lambda e: e.scalar_tensor_tensor(out=dst_bf[:], in0=src[:], scalar=st2[:, 0:1], in1=wrow[:], op0=ALU.mult, op1=ALU.mult), r=[src, st2, wrow], w=[dst_bf])

    def to_fm_bf(dstT, src_bf, nch):
        for c in range(nch):
            P.pe(lambda e, c=c: e.transpose(out=pbf[:, c * NS:(c + 1) * NS], in_=src_bf[0:NS, c * 128:(c + 1) * 128], identity=ident[0:NS, 0:NS]), r=[src_bf, ident], w=[pbf])
        P.act(lambda e: e.copy(out=dstT[:].rearrange("p c n -> p (c n)"), in_=pbf[:, 0:nch * NS]), r=[pbf], w=[dstT])

    rms_rows(xn, xs, nwrs)
    to_fm_bf(hsT, xn, 8)
    for s in range(NSL):
        wsb = load_slice(s)
        bank = pb[s % 2]
        for c in range(8):
            mm(bank[0:NS, :], hsT[:, c, :], wsb[:, c, :], c == 0, c == 7, [hsT, wsb], [bank])
        if s % 2 == 0:
            P.act(lambda e, s=s, bank=bank: e.copy(out=projs[:, s * 512:(s + 1) * 512], in_=bank[0:NS, :]), r=[bank], w=[projs])
        else:
            P.dve(lambda e, s=s, bank=bank: e.tensor_copy(out=projs[:, s * 512:(s + 1) * 512], in_=bank[0:NS, :]), r=[bank], w=[projs])
    for g in range(3):
        P.dma('sp', kvs_d[g], projs[:, g * 1536 + 512:g * 1536 + 1536], r=[projs], sem_on=projs)
    out_bufs.append(projs)

    phB = phase(); phB.__enter__()
    accn = sb("accn", [NS, 512]); accd = sb("accd", [NS, 8])
    tq = sb("tq", [NS, 512]); ts_ = sb("ts", [NS, 8]); tp = sb("tp", [NS, 8])
    for g in range(3):
        b0 = g * 1536
        P.dve(lambda e, b0=b0: e.tensor_tensor(out=tq[:], in0=projs[:, b0:b0 + 512], in1=projs[:, b0 + 512:b0 + 1024], op=ALU.mult), r=[projs], w=[tq])
        P.dve(lambda e: e.tensor_reduce(out=ts_[:], in_=tq[:].rearrange("p (h e) -> p h e", e=64), axis=AX.X, op=ALU.add), r=[tq], w=[ts_])
        P.act(lambda e: e.activation(out=tp[:], in_=ts_[:], func=A.Exp, scale=0.125), r=[ts_], w=[tp])
        for h in range(8):
            hs = slice(h * 64, (h + 1) * 64)
            if g == 0:
                P.dve(lambda e, h=h, hs=hs, b0=b0: e.tensor_scalar(out=accn[:, hs], in0=projs[:, b0 + 1024 + h * 64:b0 + 1024 + (h + 1) * 64], scalar1=tp[:, h:h + 1], scalar2=None, op0=ALU.mult), r=[projs, tp], w=[accn])
            else:
                P.dve(lambda e, h=h, hs=hs, b0=b0: e.scalar_tensor_tensor(out=accn[:, hs], in0=projs[:, b0 + 1024 + h * 64:b0 + 1024 + (h + 1) * 64], scalar=tp[:, h:h + 1], in1=accn[:, hs], op0=ALU.mult, op1=ALU.add), r=[projs, tp, accn], w=[accn])
        if g == 0:
            P.dve(lambda e: e.tensor_copy(out=accd[:], in_=tp[:]), r=[tp], w=[accd])
        else:
            P.dve(lambda e: e.tensor_tensor(out=accd[:], in0=accd[:], in1=tp[:], op=ALU.add), r=[accd, tp], w=[accd])
    ck = [sb("ck%d" % i, [128, 1024]) for i in range(2)]
    prod = sb("prod", [128, 512])
    s8 = sb("s8", [128, 8]); p8 = sb("p8", [128, 8])
    msk = sb("msk", [8, 520])
    it = 0
    for g, (win, dil) in enumerate(GROUPS):
        b0 = g * 1536
        for n in range(NS):
            c_ = ck[it % 2]
            it += 1
            P.dma('sp', c_[:], c_d[g][n, 0:win:dil, :], w=[c_])
            qb = pb[2]
            mm(qb[:], onehot[:, n * 128:(n + 1) * 128], projs[:, b0:b0 + 512], True, True, [onehot, projs], [qb])
            P.dve(lambda e, c_=c_, qb=qb: e.tensor_tensor(out=prod[:], in0=c_[:, 0:512], in1=qb[:], op=ALU.mult), r=[c_, qb], w=[prod])
            P.dve(lambda e: e.tensor_reduce(out=s8[:], in_=prod[:].rearrange("p (h e) -> p h e", e=64), axis=AX.X, op=ALU.add), r=[prod], w=[s8])
            P.dve(lambda e, g=g: e.scalar_tensor_tensor(out=s8[:], in0=s8[:], scalar=0.125, in1=sbias[:, g * 8:(g + 1) * 8], op0=ALU.mult, op1=ALU.add), r=[s8, sbias], w=[s8])
            P.act(lambda e: e.activation(out=p8[:], in_=s8[:], func=A.Exp), r=[s8], w=[p8])
            nb = pb[3]; db = pb[4]
            mm(nb[0:8, :], p8[:], c_[:, 512:1024], True, True, [p8, c_], [nb])
            mm(db[0:8, 0:8], p8[:], ones_f[:, 0:8], True, True, [p8, ones_f], [db])
            P.dve(lambda e, nb=nb: e.tensor_tensor(out=msk[:, 0:512], in0=nb[0:8, :], in1=bdm[:, 0:512], op=ALU.mult), r=[nb, bdm], w=[msk])
            P.dve(lambda e, db=db: e.tensor_tensor(out=msk[:, 512:520], in0=db[0:8, 0:8], in1=bdm[:, 512:520], op=ALU.mult), r=[db, bdm], w=[msk])
            rb = pb[5]; rd = pb[6]
            mm(rb[0:NS, :], en[:, n * NS:(n + 1) * NS], msk[:, 0:512], True, True, [en, msk], [rb])
            mm(rd[0:NS, 0:8], en[:, n * NS:(n + 1) * NS], msk[:, 512:520], True, True, [en, msk], [rd])
            P.dve(lambda e, rb=rb: e.tensor_tensor(out=accn[:], in0=accn[:], in1=rb[0:NS, :], op=ALU.add), r=[accn, rb], w=[accn])
            P.dve(lambda e, rd=rd: e.tensor_tensor(out=accd[:], in0=accd[:], in1=rd[0:NS, 0:8], op=ALU.add), r=[accd, rd], w=[accd])
    att_s = sb("att_s", [NS, 512])
    gsa = sb("gsa", [NS, 512])
    P.dve(lambda e: e.reciprocal(out=accd[:], in_=accd[:]), r=[accd], w=[accd])
    for h in range(8):
        hs = slice(h * 64, (h + 1) * 64)
        P.dve(lambda e, h=h, hs=hs: e.tensor_scalar(out=att_s[:, hs], in0=accn[:, hs], scalar1=accd[:, h:h + 1], scalar2=None, op0=ALU.mult), r=[accn, accd], w=[att_s])
    P.act(lambda e: e.activation(out=gsa[:], in_=projs[:, OFF_G:OFF_G + 512], func=A.Silu), r=[projs], w=[gsa])
    P.dve(lambda e: e.tensor_tensor(out=att_b[:], in0=att_s[:], in1=gsa[:], op=ALU.mult), r=[att_s, gsa], w=[att_b])
    phB.__exit__(None, None, None)
    phC = phase(); phC.__enter__()

    lb = sb("lb", [NS, 1024]); oml = sb("oml", [NS, 1024])
    f_t = sb("f_t", [NS, 1024]); hk_t = sb("hk_t", [NS, 1024]); hq_t = sb("hq_t", [NS, 1024])
    P.dve(lambda e: e.tensor_tensor(out=lb[:], in0=lblr[:, 0:1024], in1=lblr[:, 1024:2048], op=ALU.subtract), r=[lblr], w=[lb])
    P.act(lambda e: e.activation(out=lb[:], in_=lb[:], func=A.Sigmoid), r=[lb], w=[lb])
    P.dve(lambda e: e.tensor_scalar(out=oml[:], in0=lb[:], scalar1=-1.0, scalar2=1.0, op0=ALU.mult, op1=ALU.add), r=[lb], w=[oml])
    P.act(lambda e: e.activation(out=f_t[:], in_=projs[:, OFF_F:OFF_F + 1024], func=A.Sigmoid), r=[projs], w=[f_t])
    P.dve(lambda e: e.tensor_tensor(out=f_t[:], in0=f_t[:], in1=oml[:], op=ALU.mult), r=[f_t, oml], w=[f_t])
    P.dve(lambda e: e.tensor_tensor(out=hk_t[:], in0=oml[:], in1=f_t[:], op=ALU.subtract), r=[oml, f_t], w=[hk_t])
    P.dve(lambda e: e.tensor_tensor(out=f_t[:], in0=f_t[:], in1=lb[:], op=ALU.add), r=[f_t, lb], w=[f_t])
    P.act(lambda e: e.activation(out=hq_t[:], in_=projs[:, OFF_Q:OFF_Q + 1024], func=A.Silu), r=[projs], w=[hq_t])
    P.dve(lambda e: e.tensor_scalar(out=hq_t[:], in0=hq_t[:], scalar1=float(128 ** -0.5), scalar2=None, op0=ALU.mult), r=[hq_t], w=[hq_t])
    fT = sb("fT", [128, 8, NS]); hkT = sb("hkT", [128, 8, NS]); hqT = sb("hqT", [128, 8, NS])
    for src, dst, bank in ((f_t, fT, pb[0]), (hk_t, hkT, pb[1]), (hq_t, hqT, pb[2])):
        for h in range(8):
            P.pe(lambda e, h=h, src=src, bank=bank: e.transpose(out=bank[:, h * NS:(h + 1) * NS], in_=src[0:NS, h * 128:(h + 1) * 128], identity=ident_f[0:NS, 0:NS]), r=[src, ident_f], w=[bank])
        P.dve(lambda e, dst=dst, bank=bank: e.tensor_copy(out=dst[:].rearrange("p h n -> p (h n)"), in_=bank[:, 0:8 * NS]), r=[bank], w=[dst])
    Qm = sb("Qm", [128, 8, NS, NS])
    P.pool(lambda e: e.memset(Qm[:].rearrange("p h a b -> p (h a b)"), 0.0), w=[Qm])
    for n in range(NS):
        P.dve(lambda e, n=n: e.tensor_copy(out=Qm[:, :, n, n], in_=hqT[:, :, n]), r=[hqT], w=[Qm])
    oacc = sb("oacc", [NS, 1024])
    P.pool(lambda e: e.memset(oacc[:], 0.0), w=[oacc])
    S0 = [sb("S0_%d" % i, [128, 8, 128]) for i in range(2)]
    Sn = [sb("Sn_%d" % i, [128, 8, 128]) for i in range(2)]
    tmpk = sb("tmpk", [128, 128])
    for n in range(NS):
        s0 = S0[n % 2]; sn = Sn[n % 2]
        P.dma('sp', s0[:], sh_d[n].rearrange("h k v -> k h v"), w=[s0])
        ib = (pb[3], pb[4])
        for half in range(2):
            mm(ib[half][:], onehot[:, n * 128:(n + 1) * 128], projs[:, OFF_I + half * 512:OFF_I + (half + 1) * 512], True, True, [onehot, projs], [ib[half]])
        for h in range(8):
            src = ib[h // 4][:, (h % 4) * 128:(h % 4 + 1) * 128]
            P.dve(lambda e, src=src, h=h, n=n: e.tensor_scalar(out=tmpk[:], in0=src, scalar1=hkT[:, h, n:n + 1], scalar2=None, op0=ALU.mult), r=[ib[h // 4], hkT], w=[tmpk])
            P.dve(lambda e, h=h, n=n, s0=s0, sn=sn: e.scalar_tensor_tensor(out=sn[:, h, :], in0=s0[:, h, :], scalar=fT[:, h, n:n + 1], in1=tmpk[:], op0=ALU.mult, op1=ALU.add), r=[s0, fT, tmpk], w=[sn])
        P.dma('sp', sts_d[n].rearrange("h k v -> k h v"), sn[:], r=[sn], sem_on=sn)
        if sn not in out_bufs:
            out_bufs.append(sn)
        ob = (pb[5], pb[6])
        for h in range(8):
            mm(ob[h // 4][0:NS, (h % 4) * 128:(h % 4 + 1) * 128], Qm[:, h, n, :], sn[:, h, :], True, True, [Qm, sn], [ob[h // 4]])
        for half in range(2):
            P.dve(lambda e, half=half, ob=ob: e.tensor_tensor(out=oacc[:, half * 512:(half + 1) * 512], in0=oacc[:, half * 512:(half + 1) * 512], in1=ob[half][0:NS, :], op=ALU.add), r=[oacc, ob[half]], w=[oacc])
    o2s = sb("o2s", [NS, 1024]); ssh = sb("ssh", [NS, 8]); rsh = sb("rsh", [NS, 8])
    gsh = sb("gsh", [NS, 1024])
    P.act(lambda e: e.activation(out=o2s[:], in_=oacc[:], func=A.Square), r=[oacc], w=[o2s])
    P.dve(lambda e: e.tensor_reduce(out=ssh[:], in_=o2s[:].rearrange("p (h v) -> p h v", v=128), axis=AX.X, op=ALU.add), r=[o2s], w=[ssh])
    P.act(lambda e: e.activation(out=rsh[:], in_=ssh[:], func=A.Ln, bias=epsc[0:NS, 0:1], scale=1.0 / 128), r=[ssh, epsc], w=[rsh])
    P.act(lambda e: e.activation(out=rsh[:], in_=rsh[:], func=A.Exp, scale=-0.5), r=[rsh], w=[rsh])
    P.act(lambda e: e.activation(out=gsh[:], in_=projs[:, OFF_HG:OFF_HG + 1024], func=A.Silu), r=[projs], w=[gsh])
    for h in range(8):
        hs = slice(h * 128, (h + 1) * 128)
        P.dve(lambda e, h=h, hs=hs: e.scalar_tensor_tensor(out=o2s[:, hs], in0=oacc[:, hs], scalar=rsh[:, h:h + 1], in1=hnwr[:, hs], op0=ALU.mult, op1=ALU.mult), r=[oacc, rsh, hnwr], w=[o2s])
    P.dve(lambda e: e.tensor_tensor(out=ho_b[:], in0=o2s[:], in1=gsh[:], op=ALU.mult), r=[o2s, gsh], w=[ho_b])
    phC.__exit__(None, None, None)

    attTs = sb("attTs", [128, 4, NS], BF16); hoTs = sb("hoTs", [128, 8, NS], BF16)
    to_fm_bf(attTs, att_b, 4)
    to_fm_bf(hoTs, ho_b, 8)
    gas = sb("gas", [NS, 1024]); gbs = sb("gbs", [NS, 1024])
    P.act(lambda e: e.activation(out=gas[:], in_=projs[:, OFF_MA:OFF_MA + 1024], func=A.Sigmoid), r=[projs], w=[gas])
    P.act(lambda e: e.activation(out=gbs[:], in_=projs[:, OFF_MB:OFF_MB + 1024], func=A.Sigmoid), r=[projs], w=[gbs])
    mg = sb("mg", [NS, 1024]); mg2 = sb("mg2", [NS, 1024]); mgb = sb("mgb", [NS, 1024], BF16)
    wa_s = [sb("wa_s%d" % i, [128, 4, 128], BF16) for i in range(2)]
    wh_s = [sb("wh_s%d" % i, [128, 8, 128], BF16) for i in range(2)]
    for jj in range(8):
        wa_ = wa_s[jj % 2]; wh_ = wh_s[jj % 2]
        P.dma('sp', wa_[:], wapb_t[jj], r=[wapb], w=[wa_], sem_on=wa_)
        P.dma('sp', wh_[:], whpb_t[jj], r=[whpb], w=[wh_], sem_on=wh_)
        js = slice(jj * 128, (jj + 1) * 128)
        b1 = pb[0]; b2 = pb[1]
        for hp in range(4):
            mm(b1[0:NS, 0:128], attTs[:, hp, :], wa_[:, hp, :], hp == 0, hp == 3, [attTs, wa_], [b1])
        for h in range(8):
            mm(b2[0:NS, 0:128], hoTs[:, h, :], wh_[:, h, :], h == 0, h == 7, [hoTs, wh_], [b2])
        P.dve(lambda e, js=js, b1=b1: e.tensor_tensor(out=mg[:, js], in0=gas[:, js], in1=b1[0:NS, 0:128], op=ALU.mult), r=[gas, b1], w=[mg])
        P.dve(lambda e, js=js, b2=b2: e.tensor_tensor(out=mg2[:, js], in0=gbs[:, js], in1=b2[0:NS, 0:128], op=ALU.mult), r=[gbs, b2], w=[mg2])
    P.dve(lambda e: e.tensor_tensor(out=mgb[:], in0=mg[:], in1=mg2[:], op=ALU.add), r=[mg, mg2], w=[mgb])
    mgT = sb("mgT", [128, 8, NS], BF16)
    to_fm_bf(mgT, mgb, 8)
    xr_s = sb("xr_s", [NS, D]); fnws = sb("fnws", [NS, D]); ysb = sb("ysb", [NS, D])
    P.dma('sp', fnws[:], L['fnw_d'][0:NS, :], w=[fnws])
    for half in range(2):
        wsl = L['wsl'][L['wsl_i'][0] % 3]
        L['wsl_i'][0] += 1
        P.dma('sp', wsl[:], woutb_t[half], r=[woutb], w=[wsl], sem_on=wsl)
        bank = pb[2 + half]
        for c in range(8):
            mm(bank[0:NS, :], mgT[:, c, :], wsl[:, c, :], c == 0, c == 7, [mgT, wsl], [bank])
        P.dve(lambda e, half=half, bank=bank: e.tensor_tensor(out=xr_s[:, half * 512:(half + 1) * 512], in0=bank[0:NS, :], in1=xs[:, half * 512:(half + 1) * 512], op=ALU.add), r=[bank, xs], w=[xr_s])
    rms_rows(ysb, xr_s, fnws)
    P.dma('sp', ys_d, ysb[:], r=[ysb], sem_on=ysb)
    out_bufs.append(ysb)


def _consts():
    c = {}
    c["ident"] = np.eye(128, dtype=np.float32)
    n = 24
    slopes = (2.0 ** (-8.0 * np.arange(1, n + 1) / n)).astype(np.float32).reshape(3, 8)
    k = np.arange(128)[:, None].astype(np.float32)
    q = np.arange(128)[None, :].astype(np.float32)
    eb = np.zeros((128, 12, 4, 128), np.float32)
    for g, (win, d) in enumerate(GROUPS):
        for hp in range(4):
            for hh in range(2):
                s = slopes[g, hp * 2 + hh]
                cur = np.where(q >= k, np.exp(-s * d * np.maximum(q - k, 0.0)), 0.0)
                prev = np.where(k >= q, np.exp(-s * d * (q + 128 - k)), 0.0)
                eb[:, g * 4 + hp, 2 * hh, :] = cur
                eb[:, g * 4 + hp, 2 * hh + 1, :] = prev
    c["eb"] = eb.reshape(128, 12 * 512)
    p = np.arange(128)[:, None] % 64
    t = np.arange(64)[None, :]
    c["hmask"] = (t >= p).astype(np.float32)
    rs = np.ones((128, 512), np.float32)
    rs[:, ::64] = 0.0
    c["rsm"] = rs
    sel = np.zeros((128, 64), np.float32)
    sel[64, :] = 1.0
    c["sel"] = sel
    sb = np.zeros((128, 24), np.float32)
    for g, (win, d) in enumerate(GROUPS):
        for h in range(8):
            sb[:, g * 8 + h] = -slopes[g, h] * d * (128 - np.arange(128))
    c["sbias"] = sb
    oh = np.zeros((NS, NS, 128), np.float32)
    for i in range(NS):
        oh[i, i, :] = 1.0
    c["onehot"] = oh.reshape(NS, NS * 128)
    bdm = np.zeros((8, 520), np.float32)
    for h in range(8):
        bdm[h, h * 64:(h + 1) * 64] = 1.0
        bdm[h, 512 + h] = 1.0
    c["bdm"] = bdm
    en = np.zeros((8, NS, NS), np.float32)
    for i in range(NS):
        en[:, i, i] = 1.0
    c["en"] = en.reshape(8, NS * NS)
    return c


_CACHE = {}


def kernel(x_prompt, x_sample, cache_kv_w128, cache_kv_w512, cache_kv_w2048, state_hgrn,
           norm_w, w_in, w_att_proj, w_hg_proj, w_out, hg_norm_w, hg_lb_logits, final_norm_w,
           _nst=NST, _cores=8, _prompt=True, _sample=True):
    f = lambda a: np.ascontiguousarray(np.asarray(a, dtype=np.float32))
    key = (_nst, _prompt, _sample)
    if "nc" not in _CACHE or _CACHE.get("key") != key:
        _CACHE["nc"] = build_program(do_prompt=_prompt, do_sample=_sample, nst=_nst)
        _CACHE["key"] = key
    nc = _CACHE["nc"]
    cst = _consts()
    shared = dict(cst)
    shared["w_in"] = f(w_in[0])
    shared["wap"] = f(w_att_proj[0])
    shared["whp"] = f(w_hg_proj[0])
    shared["wout"] = f(w_out[0])
    shared["nw"] = f(np.asarray(norm_w[0]).reshape(8, 128).T)
    shared["nwr"] = f(np.broadcast_to(np.asarray(norm_w[0])[None, :], (128, D)))
    shared["hnw"] = f(np.asarray(hg_norm_w[0]).reshape(128, 1))
    shared["hnwr"] = f(np.broadcast_to(np.tile(np.asarray(hg_norm_w[0]), 8)[None, :], (NS, 1024)))
    lg = np.asarray(hg_lb_logits)
    shared["lbl"] = f(lg.reshape(2, 8, 128).transpose(2, 0, 1).reshape(128, 16))
    shared["lblr"] = f(np.broadcast_to(lg.reshape(1, 2048), (NS, 2048)))
    shared["fnw"] = f(np.broadcast_to(np.asarray(final_norm_w)[None, :], (128, D)))
    in_maps = []
    for i in range(_cores):
        m = dict(shared)
        m["x"] = f(x_prompt[i])
        sl = slice(NS * i, NS * (i + 1))
        m["xs"] = f(np.asarray(x_sample)[sl, 0, :])
        m["c128"] = f(np.asarray(cache_kv_w128)[0, sl].reshape(NS, 128, 1024))
        m["c512"] = f(np.asarray(cache_kv_w512)[0, sl].reshape(NS, 512, 1024))
        m["c2048"] = f(np.asarray(cache_kv_w2048)[0, sl].reshape(NS, 2048, 1024))
        m["sh"] = f(np.asarray(state_hgrn)[0, sl])
        in_maps.append(m)
    res = run_bass_kernel_spmd(nc, in_maps, core_ids=list(range(_cores)))
    R = res.results
    n = _cores
    y = np.stack([R[i]["y"] for i in range(n)], 0)
    ys = np.concatenate([R[i]["ys"] for i in range(n)], 0).reshape(n * NS, 1, D)
    kvp = [np.stack([R[i][k] for i in range(n)], 0).reshape(1, n, w, 2, 8, 64)
           for k, w in (("kv128", 128), ("kv512", 512), ("kv2048", 2048))]
    stp = np.stack([R[i]["stp"] for i in range(n)], 0).reshape(1, n, 8, 128, 128)
    kvs = [np.concatenate([R[i][k] for i in range(n)], 0).reshape(1, n * NS, 1, 2, 8, 64)
           for k in ("kvs128", "kvs512", "kvs2048")]
    sts = np.concatenate([R[i]["sts"] for i in range(n)], 0).reshape(1, n * NS, 8, 128, 128)
    return (y, ys, kvp[0], kvp[1], kvp[2], stp, kvs[0], kvs[1], kvs[2], sts)
```

```python
import numpy as np
import concourse.bass as bass
import concourse.mybir as mybir
from concourse.bass_utils import run_bass_kernel_spmd

F32 = mybir.dt.float32
BF16 = mybir.dt.bfloat16
AF = mybir.ActivationFunctionType
ALU = mybir.AluOpType
AX = mybir.AxisListType

ENGS = ['pe', 'act', 'dve', 'pool', 'sp']


class Buf:
    __slots__ = ('name', 't', 'lw', 'rd', 'dsem')

    def __init__(self, name, t):
        self.name = name
        self.t = t
        self.lw = None
        self.rd = {}
        self.dsem = None

    def __getitem__(self, idx):
        return self.t[idx]


class DSem:
    __slots__ = ('key', 'val', 'h')

    def __init__(self, key, h):
        self.key = key
        self.val = 0
        self.h = h


class Prog:
    def __init__(self, nc, stack):
        self.nc = nc
        self.stack = stack
        self.sem_stack = stack
        self.lists = {e: [] for e in ENGS}
        self.cnt = {e: 0 for e in ENGS}
        self.seen = {e: {} for e in ENGS}
        self.semh = {}
        for e in ENGS:
            self.semh[e] = stack.enter_context(nc.semaphore('s_' + e))
        self.ndsem = 0
        self.alldsem = []
        self.nbuf = 0

    def sb(self, name, shape, dtype):
        t = self.stack.enter_context(self.nc.sbuf_tensor("sb_" + name, list(shape), dtype))
        return Buf(name, t)

    def ps(self, name, shape, dtype=F32):
        t = self.stack.enter_context(self.nc.psum_tensor("ps_" + name, list(shape), dtype))
        return Buf(name, t)

    def wrap(self, name, t):
        return Buf(name, t)

    def _dsem(self, b):
        if b.dsem is None:
            h = self.sem_stack.enter_context(self.nc.semaphore('d%d' % self.ndsem))
            key = ('dma', self.ndsem)
            self.ndsem += 1
            self.semh[key] = h
            b.dsem = DSem(key, h)
            self.alldsem.append(b.dsem)
        return b.dsem

    def emit(self, eng, fn, reads=(), writes=(), dma=None):
        deps = {}

        def add(dep):
            if dep is None:
                return
            k, v = dep
            if deps.get(k, 0) < v:
                deps[k] = v

        for b in reads:
            add(b.lw)
        for b in writes:
            add(b.lw)
            for k, v in b.rd.items():
                add((k, v))
        ds = None
        if dma is not None:
            ds = self._dsem(dma)
            if ds.val:
                add((ds.key, ds.val))
        waits = []
        seen = self.seen[eng]
        for k, v in deps.items():
            if k == eng and eng == 'pe':
                continue
            if seen.get(k, 0) >= v:
                continue
            seen[k] = v
            waits.append((k, v))
        if ds is None:
            self.cnt[eng] += 1
            tok = (eng, self.cnt[eng])
        else:
            ds.val += 16
            tok = (ds.key, ds.val)
        self.lists[eng].append((waits, fn, ds))
        for b in reads:
            if b.rd.get(tok[0], 0) < tok[1]:
                b.rd[tok[0]] = tok[1]
        for b in writes:
            b.lw = tok
            b.rd = {}
        return tok

    def pe(self, fn, r=(), w=()):
        return self.emit('pe', fn, r, w)

    def act(self, fn, r=(), w=()):
        return self.emit('act', fn, r, w)

    def dve(self, fn, r=(), w=()):
        return self.emit('dve', fn, r, w)

    def pool(self, fn, r=(), w=()):
        return self.emit('pool', fn, r, w)

    def dma(self, q, out_ap, in_ap, r=(), w=(), sem_on=None, **kw):
        if sem_on is None:
            sem_on = (list(w) + list(r))[0]
        return self.emit(q, lambda e: e.dma_start(out=out_ap, in_=in_ap, **kw), r, w, dma=sem_on)

    def final_wait(self, eng, bufs):
        deps = {}
        for b in bufs:
            for dep in [b.lw] + list(b.rd.items()):
                if dep is None:
                    continue
                k, v = dep
                if deps.get(k, 0) < v:
                    deps[k] = v
        waits = [(k, v) for k, v in deps.items() if self.seen[eng].get(k, 0) < v]
        for k, v in waits:
            self.seen[eng][k] = v
        self.lists[eng].append((waits, None, None))

    def barrier(self):
        toks = [(e, self.cnt[e]) for e in ENGS if self.cnt[e] > 0]
        toks += [(ds.key, ds.val) for ds in self.alldsem if ds.val > 0]
        for e in ENGS:
            waits = []
            for k, v in toks:
                if k == e and e == 'pe':
                    continue
                if self.seen[e].get(k, 0) < v:
                    self.seen[e][k] = v
                    waits.append((k, v))
            self.lists[e].append((waits, None, None))

    def build(self):
        nc = self.nc
        semh = self.semh
        lists = self.lists
        needed = {e: set() for e in ENGS}
        for e in ENGS:
            for waits, fn, ds in lists[e]:
                for k, v in waits:
                    if k in needed:
                        needed[k].add(v)
        rank = {}
        for e in ENGS:
            rank[e] = {v: i + 1 for i, v in enumerate(sorted(needed[e]))}

        def run(engname):
            def body(e):
                own = semh[engname]
                seq = 0
                myrank = rank[engname]
                for waits, fn, ds in lists[engname]:
                    for k, v in waits:
                        if k in rank:
                            e.wait_ge(semh[k], rank[k][v])
                        else:
                            e.wait_ge(semh[k], v)
                    if fn is None:
                        continue
                    ins = fn(e)
                    if ds is None:
                        seq += 1
                        if seq in myrank:
                            ins.then_inc(own, 1)
                    else:
                        ins.then_inc(ds.h, 16)
            return body

        with nc.Block() as block:
            block.tensor(run('pe'))
            block.scalar(run('act'))
            block.vector(run('dve'))
            block.gpsimd(run('pool'))
            block.sync(run('sp'))


class View:
    def __init__(self, base, ap):
        self.base = base
        self.t = ap
        self.name = base.name

    def __getitem__(self, idx):
        return self.t[idx]

    lw = property(lambda s: s.base.lw, lambda s, v: setattr(s.base, 'lw', v))
    rd = property(lambda s: s.base.rd, lambda s, v: setattr(s.base, 'rd', v))
    dsem = property(lambda s: s.base.dsem, lambda s, v: setattr(s.base, 'dsem', v))

from contextlib import ExitStack

T = 8192
D = 1024
ST = 2048
NST = T // ST
SUB = 512
EPS = 1e-6
GROUPS = ((128, 1), (512, 4), (2048, 16))
NS = 16
NSL = 22


def unit_tokens(g, u):
    d = GROUPS[g][1]
    if d == 1:
        return slice(128 * u, 128 * u + 128)
    if d == 4:
        b, r = divmod(u, 4)
        return slice(512 * b + r, 512 * b + 512, 4)
    return slice(u, 2048, 16)


def build_program(do_prompt=True, do_sample=True, nst=NST):
    nc = bass.Bass("TRN2", target_bir_lowering=False)

    def din(name, shape):
        return nc.dram_tensor(name, list(shape), F32, kind="ExternalInput").ap()

    def dout(name, shape):
        return nc.dram_tensor(name, list(shape), F32, kind="ExternalOutput").ap()

    x_d = din("x", [T, D])
    xs_d = din("xs", [NS, D])
    c_d = [din("c128", [NS, 128, 1024]), din("c512", [NS, 512, 1024]), din("c2048", [NS, 2048, 1024])]
    sh_d = din("sh", [NS, 8, 128, 128])
    win_d = din("w_in", [D, 11264])
    wap_d = din("wap", [512, D])
    whp_d = din("whp", [D, D])
    wout_d = din("wout", [D, D])
    nw_d = din("nw", [128, 8])
    hnw_d = din("hnw", [128, 1])
    lbl_d = din("lbl", [128, 16])
    fnw_d = din("fnw", [128, D])
    nwr_d = din("nwr", [128, D])
    ident_d = din("ident", [128, 128])
    eb_d = din("eb", [128, 12 * 512])
    hmask_d = din("hmask", [128, 64])
    rsm_d = din("rsm", [128, 512])
    sel_d = din("sel", [128, 64])
    lblr_d = din("lblr", [NS, 2048])
    hnwr_d = din("hnwr", [NS, 1024])
    sbias_d = din("sbias", [128, 24])
    onehot_d = din("onehot", [NS, NS * 128])
    bdm_d = din("bdm", [8, 520])
    en_d = din("en", [8, NS * NS])

    y_d = dout("y", [T, D])
    ys_d = dout("ys", [NS, D])
    kvp_d = [dout("kv128", [128, 1024]), dout("kv512", [512, 1024]), dout("kv2048", [2048, 1024])]
    stp_d = dout("stp", [8, 128, 128])
    kvs_d = [dout("kvs128", [NS, 1024]), dout("kvs512", [NS, 1024]), dout("kvs2048", [NS, 1024])]
    sts_d = dout("sts", [NS, 8, 128, 128])

    wib_t = nc.dram_tensor("wib", [NSL, 128, 8, 512], BF16).ap()
    wapb_t = nc.dram_tensor("wapb", [8, 128, 4, 128], BF16).ap()
    whpb_t = nc.dram_tensor("whpb", [8, 128, 8, 128], BF16).ap()
    woutb_t = nc.dram_tensor("woutb", [2, 128, 8, 512], BF16).ap()
    NU = (1, 4, 16)
    hist_t = [[(nc.dram_tensor("histk%d_%d" % (g, hp), [128, NU[g] * 128], BF16).ap(),
                nc.dram_tensor("histv%d_%d" % (g, hp), [128, NU[g] * 132], BF16).ap())
               for hp in range(4)] for g in range(3)]

    with ExitStack() as stack:
        P = Prog(nc, stack)
        out_bufs = []

        wib = [P.wrap("wib%d" % k, wib_t) for k in range(3)]
        wsrc = win_d.rearrange("(c p) (s n) -> s p c n", p=128, n=512)
        for k, (s0, s1) in enumerate(((0, 8), (8, 16), (16, 22))):
            for s in range(s0, s1):
                P.dma('pool', wib_t[s], wsrc[s], w=[wib[k]], sem_on=wib[k])

        def wib_buf(s):
            return wib[0 if s < 8 else (1 if s < 16 else 2)]

        wapb = P.wrap("wapb", wapb_t)
        whpb = P.wrap("whpb", whpb_t)
        woutb = P.wrap("woutb", woutb_t)
        s_ap = wap_d.rearrange("(hp p) (j n) -> j p hp n", p=128, n=128)
        s_hp = whp_d.rearrange("(h p) (j n) -> j p h n", p=128, n=128)
        for j in range(8):
            P.dma('pool', wapb_t[j], s_ap[j], w=[wapb], sem_on=wapb)
            P.dma('pool', whpb_t[j], s_hp[j], w=[whpb], sem_on=whpb)
        s_wo = wout_d.rearrange("(c p) (s n) -> s p c n", p=128, n=512)
        for s in range(2):
            P.dma('pool', woutb_t[s], s_wo[s], w=[woutb], sem_on=woutb)

        ident_f = P.sb("ident_f", [128, 128], F32)
        ident = P.sb("ident", [128, 128], BF16)
        eb = P.sb("eb", [128, 12, 512], BF16)
        hmask = P.sb("hmask", [128, 64], F32)
        rsm = P.sb("rsm", [128, 512], F32)
        sel = P.sb("sel", [128, 64], F32)
        nw = P.sb("nw", [128, 8], F32)
        hnw = P.sb("hnw", [128, 1], F32)
        lbl = P.sb("lbl", [128, 16], F32)
        fnw = P.sb("fnw", [128, D], F32)
        epsc = P.sb("epsc", [128, 1], F32)
        ones_f = P.sb("ones_f", [128, 128], F32)
        lbc = P.sb("lbc", [128, 32], F32)
        P.dma('sp', ident_f[:], ident_d, w=[ident_f])
        for i in range(12):
            P.dma('pool', eb[:, i, :], eb_d[:, i * 512:(i + 1) * 512], w=[eb], sem_on=eb)
        P.dma('sp', hmask[:], hmask_d, w=[hmask])
        P.dma('sp', rsm[:], rsm_d, w=[rsm])
        P.dma('sp', sel[:], sel_d, w=[sel])
        P.dma('sp', nw[:], nw_d, w=[nw])
        P.dma('sp', hnw[:], hnw_d, w=[hnw])
        P.dma('sp', lbl[:], lbl_d, w=[lbl])
        P.dma('sp', fnw[:], fnw_d, w=[fnw])
        P.dve(lambda e: e.tensor_copy(out=ident[:], in_=ident_f[:]), r=[ident_f], w=[ident])
        P.pool(lambda e: e.memset(epsc[:], EPS), w=[epsc])
        P.pool(lambda e: e.memset(ones_f[:], 1.0), w=[ones_f])
        P.dve(lambda e: e.tensor_tensor(out=lbc[:, 24:32], in0=lbl[:, 0:8], in1=lbl[:, 8:16], op=ALU.subtract), r=[lbl], w=[lbc])
        P.act(lambda e: e.activation(out=lbc[:, 0:8], in_=lbc[:, 24:32], func=AF.Sigmoid), r=[lbc], w=[lbc])
        P.dve(lambda e: e.tensor_scalar(out=lbc[:, 8:16], in0=lbc[:, 0:8], scalar1=-1.0, scalar2=1.0, op0=ALU.mult, op1=ALU.add), r=[lbc], w=[lbc])
        P.dve(lambda e: e.tensor_scalar(out=lbc[:, 16:24], in0=lbc[:, 0:8], scalar1=1.0, scalar2=-1.0, op0=ALU.mult, op1=ALU.add), r=[lbc], w=[lbc])

        pb = [P.ps("pb%d" % i, [128, 512], F32) for i in range(7)]
        pbf = P.ps("pbf", [128, 1024], BF16)

        wsl = [P.sb("wsl%d" % i, [128, 8, 512], BF16) for i in range(3)]
        wsl_i = [0]

        def load_slice(s):
            b = wsl[wsl_i[0] % 3]
            wsl_i[0] += 1
            P.dma('sp', b[:], wib_t[s], r=[wib_buf(s)], w=[b], sem_on=b)
            return b

        def rstd_from_ss(dst, ss, scale, n):
            P.act(lambda e: e.activation(out=dst[:, 0:n], in_=ss[:, 0:n], func=AF.Ln, bias=epsc[:, 0:1], scale=scale), r=[ss, epsc], w=[dst])
            P.act(lambda e: e.activation(out=dst[:, 0:n], in_=dst[:, 0:n], func=AF.Exp, scale=-0.5), r=[dst], w=[dst])

        Lc = locals()
        if do_sample:
            with ExitStack() as sstack:
                P.stack = sstack
                sample_path(P, Lc)
                P.barrier()
            P.stack = stack
        if do_prompt:
            prompt_path(P, Lc)
        P.final_wait('sp', out_bufs)
        P.build()
    return nc


def prompt_path(P, L):
    nc = L['nc']; pb = L['pb']; pbf = L['pbf']; out_bufs = L['out_bufs']
    x_d = L['x_d']; y_d = L['y_d']; kvp_d = L['kvp_d']; stp_d = L['stp_d']
    ident = L['ident']; eb = L['eb']; hmask = L['hmask']; rsm = L['rsm']; sel = L['sel']
    hnw = L['hnw']; fnw = L['fnw']; lbc = L['lbc']; epsc = L['epsc']; ones_f = L['ones_f']
    load_slice = L['load_slice']; rstd_from_ss = L['rstd_from_ss']
    hist_t = L['hist_t']; NU = L['NU']; nst = L['nst']
    wapb_t = L['wapb_t']; whpb_t = L['whpb_t']; woutb_t = L['woutb_t']
    wapb = L['wapb']; whpb = L['whpb']; woutb = L['woutb']
    nwr_d = L['nwr_d']

    nwr = P.sb("nwr", [128, D], F32)
    P.dma('sp', nwr[:], nwr_d, w=[nwr])
    hT = P.sb("hT", [128, 8, ST], BF16)
    xt = [P.sb("xt%d" % i, [128, D], F32) for i in range(2)]
    xn = P.sb("xn", [128, D], BF16)
    ssn = P.sb("ssn", [128, 4], F32)
    rsn = P.sb("rsn", [128, 4], F32)
    qU = P.sb("qU", [128, 16, 128], BF16)
    kU = P.sb("kU", [128, 16, 128], BF16)
    vU = P.sb("vU", [128, 16, 2, 66], BF16)
    kprev = P.sb("kprev", [128, 16, 128], BF16)
    vprev = P.sb("vprev", [128, 16, 2, 66], BF16)
    big = P.sb("big", [128, 9 * 512], F32)
    acc = Buf("acc", big.t[0:65, 0:4096].rearrange("p (h t) -> p h t", h=2))
    tmps = [Buf("tmp%d" % i, big.t[:, i * 512:(i + 1) * 512]) for i in range(9)]
    esb = [[P.sb("esb%d_%d" % (i, hh), [128, 512], BF16) for hh in range(2)] for i in range(2)]
    pTb = [[[P.sb("pTb%d_%d_%d" % (i, hh, uu), [128, 256], BF16) for uu in range(2)] for hh in range(2)] for i in range(2)]
    attT = P.sb("attT", [128, 4, ST], BF16)
    att1 = P.sb("att1", [64, ST], BF16)
    gsA = P.sb("gsA", [64, 512], F32)
    tmpA = P.sb("tmpA", [64, 512], F32)
    rcpA = tmps[8]
    Sck = View(kU, kU.t[:, 0:8, :])
    aTm8 = View(qU, qU.t[:, 0:4, :].rearrange("p a (b t) -> p (a b) t", t=64))
    aTms = [P.sb("aTm%d" % i, [128, 64], BF16) for i in range(2)]
    Sxs = [P.sb("Sx%d" % i, [128, 128], BF16) for i in range(2)]
    histb = [[P.wrap("hist%d_%d" % (g, hp), hist_t[g][hp][0]) for hp in range(4)] for g in range(3)]
    vI = P.sb("vI", [128, 4, 1024], BF16)
    S32 = P.sb("S32", [128, 8, 128], F32)
    Sbf = P.sb("Sbf", [128, 8, 128], BF16)
    sig = tmps[0]
    logf = tmps[1]
    hk = tmps[2]
    bcs = tmps[3]
    Ep = tmps[4]
    En = tmps[5]
    hq = tmps[6]
    kt32 = tmps[7]
    qtl = P.sb("qtl", [128, 512], BF16)
    ktl = P.sb("ktl", [128, 512], BF16)
    kend = P.sb("kend", [128, 512], BF16)
    kendT = P.sb("kendT", [128, 4, 128], BF16)
    o32 = tmps[3]
    o2 = tmps[1]
    rs_h = tmps[5]
    gsH = tmps[8]
    hoT = P.sb("hoT", [128, 8, 512], BF16)
    ga = tmps[0]
    gbt = tmps[1]
    m1 = tmps[2]
    m2 = tmps[3]
    mergedT = View(vI, vI.t[:].rearrange("p a b -> p (a b)").rearrange("p (c t) -> p c t", t=512))
    wapj = [P.sb("wapj%d" % i, [128, 4, 128], BF16) for i in range(2)]
    whpj = [P.sb("whpj%d" % i, [128, 8, 128], BF16) for i in range(2)]
    xr = P.sb("xr", [128, D], F32)
    yo = [P.sb("yo%d" % i, [128, D], F32) for i in range(2)]
    kvtb = yo
    sq = xr
    ssf = P.sb("ssf", [128, 4], F32)
    rsf = P.sb("rsf", [128, 4], F32)

    P.pool(lambda e: e.memset(vU[:].rearrange("p u h e -> p (u h e)"), 1.0), w=[vU])
    P.pool(lambda e: e.memset(kprev[:].rearrange("p u i -> p (u i)"), 0.0), w=[kprev])
    P.pool(lambda e: e.memset(vprev[:].rearrange("p u h e -> p (u h e)"), 0.0), w=[vprev])
    P.pool(lambda e: e.memset(S32[:].rearrange("p h v -> p (h v)"), 0.0), w=[S32])
    P.pool(lambda e: e.memset(Sbf[:].rearrange("p h v -> p (h v)"), 0.0), w=[Sbf])

    def mm(bank_ap, lhsT, rhs, first, last, r, w):
        P.pe(lambda e: e.matmul(bank_ap, lhsT=lhsT, rhs=rhs, start=first, stop=last), r=r, w=w)

    def proj_fm(bank, wsb, off, m, tsl):
        for c in range(8):
            mm(bank[0:m, :], wsb[:, c, off:off + m], hT[:, c, tsl], c == 0, c == 7, [wsb, hT], [bank])

    def phase_n(st):
        for tt in range(16):
            t0 = st * ST + tt * 128
            xb = xt[tt % 2]
            P.dma('sp', xb[:], x_d[t0:t0 + 128, :], w=[xb])
            P.act(lambda e, xb=xb: e.activation(out=sq[:], in_=xb[:], func=AF.Square), r=[xb], w=[sq])
            P.dve(lambda e: e.tensor_reduce(out=ssn[:, 0:1], in_=sq[:], axis=AX.X, op=ALU.add), r=[sq], w=[ssn])
            rstd_from_ss(rsn, ssn, 1.0 / D, 1)
            P.dve(lambda e, xb=xb: e.scalar_tensor_tensor(out=xn[:], in0=xb[:], scalar=rsn[:, 0:1], in1=nwr[:], op0=ALU.mult, op1=ALU.mult), r=[xb, rsn, nwr], w=[xn])
            for c in range(8):
                P.pe(lambda e, c=c: e.transpose(out=pbf[:, c * 128:(c + 1) * 128], in_=xn[:, c * 128:(c + 1) * 128], identity=ident[:]), r=[xn, ident], w=[pbf])
            P.act(lambda e, tt=tt: e.copy(out=hT[:, :, tt * 128:(tt + 1) * 128], in_=pbf[:].rearrange("p (c t) -> p c t", t=128)), r=[pbf], w=[hT])

    def att_gh(st, g, hp):
        d = GROUPS[g][1]
        nu = NU[g]
        off = hp * 128
        wq = load_slice(3 * g)
        wk = load_slice(3 * g + 1)
        wv = load_slice(3 * g + 2)
        hb = histb[g][hp]
        htk, htv = hist_t[g][hp]
        if st > 0:
            P.dma('sp', kprev[:, 0:nu, :].rearrange("p u i -> p (u i)"), htk, r=[hb], w=[kprev], sem_on=kprev)
            P.dma('sp', vprev[:, 0:nu].rearrange("p u h e -> p (u h e)"), htv, r=[hb], w=[vprev], sem_on=vprev)
        for wsb, dst, eng in ((wq, qU, 'act'), (wk, kU, 'dve')):
            for n in range(4):
                bank = pb[n % 2]
                proj_fm(bank, wsb, off, 128, slice(n * 512, (n + 1) * 512))
                if d == 1:
                    o_ap = dst[:, 4 * n:4 * n + 4, :]
                    i_ap = bank[:].rearrange("p (u i) -> p u i", i=128)
                elif d == 4:
                    o_ap = dst[:, 4 * n:4 * n + 4, :]
                    i_ap = bank[:].rearrange("p (i r) -> p r i", r=4)
                else:
                    o_ap = dst[:, :, 32 * n:32 * n + 32]
                    i_ap = bank[:].rearrange("p (i r) -> p r i", r=16)
                if eng == 'act':
                    P.act(lambda e, o_ap=o_ap, i_ap=i_ap: e.copy(out=o_ap, in_=i_ap), r=[bank], w=[dst])
                else:
                    P.dve(lambda e, o_ap=o_ap, i_ap=i_ap: e.tensor_copy(out=o_ap, in_=i_ap), r=[bank], w=[dst])
        import os
        sub = int(os.environ.get("KSUB", "9"))
        if sub < 1:
            return
        for ug in range(4):
            bank = pb[2 + ug % 2]
            for uu in range(4):
                ts = unit_tokens(g, ug * 4 + uu)
                for c in range(8):
                    mm(bank[:, uu * 128:(uu + 1) * 128], hT[:, c, ts], wv[:, c, off:off + 128], c == 0, c == 7, [hT, wv], [bank])
            o_ap = vU[:, ug * 4:ug * 4 + 4, :, 0:64]
            i_ap = bank[:].rearrange("p (u h e) -> p u h e", u=4, h=2)
            if ug % 2 == 0:
                P.act(lambda e, o_ap=o_ap, i_ap=i_ap: e.copy(out=o_ap, in_=i_ap), r=[bank], w=[vU])
            else:
                P.dve(lambda e, o_ap=o_ap, i_ap=i_ap: e.tensor_copy(out=o_ap, in_=i_ap), r=[bank], w=[vU])
        if st == NST - 1 and hp == 0:
            ntile = GROUPS[g][0] // 128
            for tt in range(16 - ntile, 16):
                kvt = kvtb[tt % 2]
                for half, wsb in enumerate((wk, wv)):
                    bank = pb[half]
                    for c in range(8):
                        mm(bank[:], hT[:, c, tt * 128:(tt + 1) * 128], wsb[:, c, :], c == 0, c == 7, [hT, wsb], [bank])
                    if half == 0:
                        P.act(lambda e, kvt=kvt, bank=bank: e.copy(out=kvt[:, 0:512], in_=bank[:]), r=[bank], w=[kvt])
                    else:
                        P.dve(lambda e, kvt=kvt, bank=bank: e.tensor_copy(out=kvt[:, 512:1024], in_=bank[:]), r=[bank], w=[kvt])
                row0 = (tt - (16 - ntile)) * 128
                P.dma('sp', kvp_d[g][row0:row0 + 128, :], kvt[:], r=[kvt], sem_on=kvt)
                if kvt not in out_bufs:
                    out_bufs.append(kvt)
        if sub < 2:
            return
        def prev_of(u):
            if g == 0:
                return (kU, u - 1) if u > 0 else (kprev, 0)
            if g == 1:
                return (kU, u - 4) if u >= 4 else (kprev, u)
            return (kprev, u)

        ebi = g * 4 + hp
        sbks = ((pb[4], pb[5]), (pb[0], pb[1]))
        obs = (pb[6], pb[2])

        def scores(up):
            for hh in range(2):
                lo = hh * 64
                sbk = sbks[up % 2][hh]
                for uu in range(2):
                    u = up * 2 + uu
                    kpb, ku = prev_of(u)
                    mm(sbk[:, uu * 256:uu * 256 + 128], kU[lo:lo + 64, u, :], qU[lo:lo + 64, u, :], True, True, [kU, qU], [sbk])
                    mm(sbk[:, uu * 256 + 128:uu * 256 + 256], kpb[lo:lo + 64, ku, :], qU[lo:lo + 64, u, :], True, True, [kpb, qU], [sbk])

        def softmax(up):
            for hh in range(2):
                e_ = esb[up % 2][hh]
                sbk = sbks[up % 2][hh]
                P.act(lambda e, e_=e_, sbk=sbk: e.activation(out=e_[:], in_=sbk[:], func=AF.Exp, scale=0.125), r=[sbk], w=[e_])
                for uu in range(2):
                    p_ = pTb[up % 2][hh][uu]
                    fn = lambda e, e_=e_, p_=p_, hh=hh, uu=uu: e.tensor_tensor(out=p_[:], in0=e_[:, uu * 256:(uu + 1) * 256], in1=eb[:, ebi, hh * 256:(hh + 1) * 256], op=ALU.mult)
                    if uu == 0 or os.environ.get("KPOOL", "0") == "0":
                        P.dve(fn, r=[e_, eb], w=[p_])
                    else:
                        P.pool(fn, r=[e_, eb], w=[p_])

        def pv(up):
            for uu in range(2):
                u = up * 2 + uu
                ts = unit_tokens(g, u)
                kpb, ku = prev_of(u)
                vpb = vU if kpb is kU else vprev
                ob = obs[uu]
                for hh in range(2):
                    p_ = pTb[up % 2][hh][uu]
                    mm(ob[0:65, hh * 128:(hh + 1) * 128], vU[:, u, hh, 0:65], p_[:, 0:128], True, False, [vU, p_], [ob])
                    mm(ob[0:65, hh * 128:(hh + 1) * 128], vpb[:, ku, hh, 0:65], p_[:, 128:256], False, True, [vpb, p_], [ob])
                accv = acc[:, :, ts]
                src = ob[0:65, 0:256].rearrange("p (h q) -> p h q", h=2)
                if g == 0:
                    P.dve(lambda e, accv=accv, src=src: e.tensor_copy(out=accv, in_=src), r=[ob], w=[acc])
                else:
                    P.dve(lambda e, accv=accv, src=src: e.tensor_tensor(out=accv, in0=accv, in1=src, op=ALU.add), r=[ob, acc], w=[acc])

        scores(0)
        softmax(0)
        for up in range(8):
            if up + 1 < 8:
                scores(up + 1)
                softmax(up + 1)
            pv(up)
        if st < nst - 1:
            u0 = 16 - nu
            if g == 2:
                P.dve(lambda e: e.tensor_copy(out=kprev[:].rearrange("p u i -> p (u i)"), in_=kU[:].rearrange("p u i -> p (u i)")), r=[kU], w=[kprev])
                P.dma('sp', htk, kprev[:].rearrange("p u i -> p (u i)"), r=[kprev], w=[hb], sem_on=hb)
                if st == 0:
                    P.pool(lambda e: e.memset(kprev[:].rearrange("p u i -> p (u i)"), 0.0), w=[kprev])
            else:
                P.dma('sp', htk, kU[:, u0:16, :].rearrange("p u i -> p (u i)"), r=[kU], w=[hb], sem_on=hb)
            P.dma('sp', htv, vU[:, u0:16].rearrange("p u h e -> p (u h e)"), r=[vU], w=[hb], sem_on=hb)

    def att_final(hp):
        wg = load_slice(9)
        for hh in range(2):
            h = hp * 2 + hh
            for n in range(4):
                tsl = slice(n * 512, (n + 1) * 512)
                rb = pb[2]
                gbk = pb[3]
                mm(rb[0:64, :], sel[0:65, :], acc[0:65, hh, tsl], True, True, [sel, acc], [rb])
                proj_fm(gbk, wg, h * 64, 64, tsl)
                P.act(lambda e, gbk=gbk: e.activation(out=gsA[:], in_=gbk[0:64, :], func=AF.Silu), r=[gbk], w=[gsA])
                P.dve(lambda e, rb=rb: e.reciprocal(out=rcpA[0:64, :], in_=rb[0:64, :]), r=[rb], w=[rcpA])
                P.dve(lambda e, hh=hh, tsl=tsl: e.tensor_tensor(out=tmpA[:], in0=acc[0:64, hh, tsl], in1=rcpA[0:64, :], op=ALU.mult), r=[acc, rcpA], w=[tmpA])
                dst = attT[0:64, hp, tsl] if hh == 0 else att1[:, tsl]
                dbuf = attT if hh == 0 else att1
                P.dve(lambda e, dst=dst: e.tensor_tensor(out=dst, in0=tmpA[:], in1=gsA[:], op=ALU.mult), r=[tmpA, gsA], w=[dbuf])
            if hh == 1:
                P.dma('sp', attT[64:128, hp, :], att1[:], r=[att1], w=[attT], sem_on=att1)

    def hgrn_sub(st, j):
        c0 = j * 512
        tsl = slice(c0, c0 + 512)
        wi = [load_slice(14), load_slice(15)]
        for tt in range(4):
            for half in range(2):
                bank = pb[half]
                for c in range(8):
                    mm(bank[:], hT[:, c, c0 + tt * 128:c0 + (tt + 1) * 128], wi[half][:, c, :], c == 0, c == 7, [hT, wi[half]], [bank])
                o_ap = vI[:, tt, half * 512:(half + 1) * 512]
                if half == 0:
                    P.act(lambda e, o_ap=o_ap, bank=bank: e.copy(out=o_ap, in_=bank[:]), r=[bank], w=[vI])
                else:
                    P.dve(lambda e, o_ap=o_ap, bank=bank: e.tensor_copy(out=o_ap, in_=bank[:]), r=[bank], w=[vI])
        for h4 in range(2):
            wq_ = load_slice(10 + h4)
            wf_ = load_slice(12 + h4)
            wg_ = load_slice(16 + h4)
            for hq_ in range(4):
                h = h4 * 4 + hq_
                off = hq_ * 128
                bF = pb[0]
                proj_fm(bF, wf_, off, 128, tsl)
                P.act(lambda e, bF=bF: e.activation(out=sig[:], in_=bF[:], func=AF.Sigmoid), r=[bF], w=[sig])
                bQ = pb[1]
                proj_fm(bQ, wq_, off, 128, tsl)
                P.act(lambda e, bQ=bQ: e.activation(out=hq[:], in_=bQ[:], func=AF.Silu), r=[bQ], w=[hq])
                bG = pb[2]
                proj_fm(bG, wg_, off, 128, tsl)
                P.act(lambda e, bG=bG: e.activation(out=gsH[:], in_=bG[:], func=AF.Silu), r=[bG], w=[gsH])
                P.act(lambda e, h=h: e.activation(out=logf[:], in_=sig[:], func=AF.Ln, bias=lbc[:, h:h + 1], scale=lbc[:, 8 + h:9 + h]), r=[sig, lbc], w=[logf])
                P.dve(lambda e, h=h: e.tensor_scalar(out=hk[:], in0=sig[:], scalar1=lbc[:, 16 + h:17 + h], scalar2=lbc[:, 8 + h:9 + h], op0=ALU.mult, op1=ALU.add), r=[sig, lbc], w=[hk])
                P.dve(lambda e: e.tensor_tensor_scan(out=bcs[:], data0=rsm[:], data1=logf[:], initial=0.0, op0=ALU.mult, op1=ALU.add), r=[rsm, logf], w=[bcs])
                P.act(lambda e: e.activation(out=Ep[:], in_=bcs[:], func=AF.Exp), r=[bcs], w=[Ep])
                P.act(lambda e: e.activation(out=En[:], in_=bcs[:], func=AF.Exp, scale=-1.0), r=[bcs], w=[En])
                P.dve(lambda e: e.scalar_tensor_tensor(out=qtl[:], in0=hq[:], scalar=float(128 ** -0.5), in1=Ep[:], op0=ALU.mult, op1=ALU.mult), r=[hq, Ep], w=[qtl])
                P.dve(lambda e: e.tensor_tensor(out=kt32[:], in0=hk[:], in1=En[:], op=ALU.mult), r=[hk, En], w=[kt32])
                P.dve(lambda e: e.tensor_copy(out=ktl[:], in_=kt32[:]), r=[kt32], w=[ktl])
                for cc in range(8):
                    P.dve(lambda e, cc=cc: e.tensor_scalar(out=kend[:, cc * 64:(cc + 1) * 64], in0=kt32[:, cc * 64:(cc + 1) * 64], scalar1=Ep[:, cc * 64 + 63:cc * 64 + 64], scalar2=None, op0=ALU.mult), r=[kt32, Ep], w=[kend])
                for tt in range(4):
                    P.pe(lambda e, tt=tt: e.transpose(out=pbf[:, tt * 128:(tt + 1) * 128], in_=kend[:, tt * 128:(tt + 1) * 128], identity=ident[:]), r=[kend, ident], w=[pbf])
                P.act(lambda e: e.copy(out=kendT[:].rearrange("p a b -> p (a b)"), in_=pbf[:, 0:512]), r=[pbf], w=[kendT])
                bOs = (pb[3], pb[2])
                for cc in range(8):
                    bO = bOs[cc % 2]
                    osl = slice((cc // 2) * 64, (cc // 2) * 64 + 64)
                    tt = cc // 2
                    lo = (cc % 2) * 64
                    csl = slice(cc * 64, (cc + 1) * 64)
                    bA = pb[4]
                    aTm = aTms[cc % 2]
                    mm(bA[lo:lo + 64, 0:64], ktl[:, csl], qtl[:, csl], True, True, [ktl, qtl], [bA])
                    bS = pb[5 + cc % 2]
                    mm(bS[:, 0:128], kendT[lo:lo + 64, tt, :], vI[lo:lo + 64, tt, h * 128:(h + 1) * 128], True, True, [kendT, vI], [bS])
                    P.dve(lambda e, lo=lo, bA=bA, aTm=aTm: e.tensor_tensor(out=aTm[lo:lo + 64, :], in0=bA[lo:lo + 64, 0:64], in1=hmask[lo:lo + 64, :], op=ALU.mult), r=[bA, hmask], w=[aTm])
                    if cc == 0:
                        mm(bO[:, osl], Sbf[:, h, :], qtl[:, csl], True, False, [Sbf, qtl], [bO])
                    else:
                        Sp = Sxs[(cc - 1) % 2]
                        mm(bO[:, osl], Sp[:], qtl[:, csl], True, False, [Sp, qtl], [bO])
                    mm(bO[:, osl], vI[lo:lo + 64, tt, h * 128:(h + 1) * 128], aTm[lo:lo + 64, :], False, True, [vI, aTm], [bO])
                    P.dve(lambda e, h=h, cc=cc, bS=bS: e.scalar_tensor_tensor(out=S32[:, h, :], in0=S32[:, h, :], scalar=Ep[:, cc * 64 + 63:cc * 64 + 64], in1=bS[:, 0:128], op0=ALU.mult, op1=ALU.add), r=[S32, Ep, bS], w=[S32])
                    if cc < 7:
                        Sn_ = Sxs[cc % 2]
                        P.dve(lambda e, h=h, Sn_=Sn_: e.tensor_copy(out=Sn_[:], in_=S32[:, h, :]), r=[S32], w=[Sn_])
                    else:
                        P.dve(lambda e, h=h: e.tensor_copy(out=Sbf[:, h, :], in_=S32[:, h, :]), r=[S32], w=[Sbf])
                for par in range(2):
                    bO = bOs[par]
                    o2v = o2[:].rearrange("p (a b t) -> p a b t", b=2, t=64)[:, :, par, :]
                    o32v = o32[:].rearrange("p (a b t) -> p a b t", b=2, t=64)[:, :, par, :]
                    srcv = bO[:, 0:256].rearrange("p (a t) -> p a t", t=64)
                    P.act(lambda e, o2v=o2v, srcv=srcv: e.activation(out=o2v, in_=srcv, func=AF.Square), r=[bO], w=[o2])
                    P.dve(lambda e, o32v=o32v, srcv=srcv: e.tensor_copy(out=o32v, in_=srcv), r=[bO], w=[o32])
                bN = pb[0]
                mm(bN[:], ones_f[:], o2[:], True, True, [ones_f, o2], [bN])
                P.act(lambda e, bN=bN: e.activation(out=rs_h[:], in_=bN[:], func=AF.Ln, bias=epsc[:, 0:1], scale=1.0 / 128), r=[bN, epsc], w=[rs_h])
                P.act(lambda e: e.activation(out=rs_h[:], in_=rs_h[:], func=AF.Exp, scale=-0.5), r=[rs_h], w=[rs_h])
                P.dve(lambda e: e.scalar_tensor_tensor(out=o32[:], in0=o32[:], scalar=hnw[:, 0:1], in1=rs_h[:], op0=ALU.mult, op1=ALU.mult), r=[o32, hnw, rs_h], w=[o32])
                P.dve(lambda e, h=h: e.tensor_tensor(out=hoT[:, h, :], in0=o32[:], in1=gsH[:], op=ALU.mult), r=[o32, gsH], w=[hoT])

    def out_sub(st, j):
        c0 = j * 512
        tsl = slice(c0, c0 + 512)
        wma = [None, None]
        wmb = [None, None]
        for jj in range(8):
            if jj % 4 == 0:
                wma[jj // 4] = load_slice(18 + jj // 4)
                wmb[jj // 4] = load_slice(20 + jj // 4)
            wa_ = wapj[jj % 2]
            wh_ = whpj[jj % 2]
            P.dma('sp', wa_[:], wapb_t[jj], r=[wapb], w=[wa_], sem_on=wa_)
            P.dma('sp', wh_[:], whpb_t[jj], r=[whpb], w=[wh_], sem_on=wh_)
            off = (jj % 4) * 128
            bA = pb[0]
            proj_fm(bA, wma[jj // 4], off, 128, tsl)
            P.act(lambda e, bA=bA: e.activation(out=ga[:], in_=bA[:], func=AF.Sigmoid), r=[bA], w=[ga])
            bB = pb[1]
            proj_fm(bB, wmb[jj // 4], off, 128, tsl)
            P.act(lambda e, bB=bB: e.activation(out=gbt[:], in_=bB[:], func=AF.Sigmoid), r=[bB], w=[gbt])
            b1 = pb[2]
            for hp in range(4):
                mm(b1[:], wa_[:, hp, :], attT[:, hp, tsl], hp == 0, hp == 3, [wa_, attT], [b1])
            b2 = pb[3]
            for h in range(8):
                mm(b2[:], wh_[:, h, :], hoT[:, h, :], h == 0, h == 7, [wh_, hoT], [b2])
            P.dve(lambda e, b1=b1: e.tensor_tensor(out=m1[:], in0=ga[:], in1=b1[:], op=ALU.mult), r=[ga, b1], w=[m1])
            P.dve(lambda e, b2=b2: e.tensor_tensor(out=m2[:], in0=gbt[:], in1=b2[:], op=ALU.mult), r=[gbt, b2], w=[m2])
            P.dve(lambda e, jj=jj: e.tensor_tensor(out=mergedT[:, jj, :], in0=m1[:], in1=m2[:], op=ALU.add), r=[m1, m2], w=[mergedT])
        wo = []
        for s in range(2):
            b = L['wsl'][L['wsl_i'][0] % 3]
            L['wsl_i'][0] += 1
            P.dma('sp', b[:], woutb_t[s], r=[woutb], w=[b], sem_on=b)
            wo.append(b)
        for tt in range(4):
            t0 = st * ST + c0 + tt * 128
            xb = xt[tt % 2]
            P.dma('sp', xb[:], x_d[t0:t0 + 128, :], w=[xb])
            for half in range(2):
                bank = pb[4 + half]
                for c in range(8):
                    mm(bank[:], mergedT[:, c, tt * 128:(tt + 1) * 128], wo[half][:, c, :], c == 0, c == 7, [mergedT, wo[half]], [bank])
                P.dve(lambda e, half=half, bank=bank, xb=xb: e.tensor_tensor(out=xr[:, half * 512:(half + 1) * 512], in0=bank[:], in1=xb[:, half * 512:(half + 1) * 512], op=ALU.add), r=[bank, xb], w=[xr])
            yb = yo[tt % 2]
            P.act(lambda e, yb=yb: e.activation(out=yb[:], in_=xr[:], func=AF.Square), r=[xr], w=[yb])
            P.dve(lambda e, yb=yb: e.tensor_reduce(out=ssf[:, 0:1], in_=yb[:], axis=AX.X, op=ALU.add), r=[yb], w=[ssf])
            rstd_from_ss(rsf, ssf, 1.0 / D, 1)
            P.dve(lambda e, yb=yb: e.scalar_tensor_tensor(out=yb[:], in0=xr[:], scalar=rsf[:, 0:1], in1=fnw[:], op0=ALU.mult, op1=ALU.mult), r=[xr, rsf, fnw], w=[yb])
            P.dma('sp', y_d[t0:t0 + 128, :], yb[:], r=[yb], sem_on=yb)
            if yb not in out_bufs:
                out_bufs.append(yb)

    import os
    stage0 = int(os.environ.get("KSTAGE", "9"))
    stage1 = int(os.environ.get("KSTAGE1", "9"))
    for st in range(nst):
        stage = stage0 if st == 0 else stage1
        if st > 0:
            P.barrier()
        if stage >= 1:
            phase_n(st)
        if stage >= 2:
            P.dve(lambda e: e.memset(acc[0:1, 0, 0:1], 0.0), w=tmps + [acc])
            for hp in range(4):
                for g in range(3):
                    att_gh(st, g, hp)
                if stage >= 3:
                    att_final(hp)
        if stage >= 4:
            P.barrier()
            P.dve(lambda e: e.memset(tmps[0][0:1, 0:1], 0.0), w=[acc] + tmps)
            for j in range(4):
                hgrn_sub(st, j)
                if stage >= 5:
                    out_sub(st, j)
    P.dma('sp', stp_d.rearrange("h k v -> k h v"), S32[:], r=[S32], sem_on=S32)
    out_bufs.append(S32)


def sample_path(P, L):
    nc = L['nc']; pb = L['pb']; pbf = L['pbf']; out_bufs = L['out_bufs']
    xs_d = L['xs_d']; c_d = L['c_d']; sh_d = L['sh_d']; ys_d = L['ys_d']; kvs_d = L['kvs_d']; sts_d = L['sts_d']
    ident = L['ident']; ident_f = L['ident_f']; fnw = L['fnw']; epsc = L['epsc']; ones_f = L['ones_f']
    load_slice = L['load_slice']; rstd_from_ss = L['rstd_from_ss']
    wapb_t = L['wapb_t']; whpb_t = L['whpb_t']; woutb_t = L['woutb_t']
    wapb = L['wapb']; whpb = L['whpb']; woutb = L['woutb']
    nwr_d = L['nwr_d']; lblr_d = L['lblr_d']; hnwr_d = L['hnwr_d']; sbias_d = L['sbias_d']
    onehot_d = L['onehot_d']; bdm_d = L['bdm_d']; en_d = L['en_d']
    A = AF
    OFF_G = 4608; OFF_Q = 5120; OFF_F = 6144; OFF_I = 7168; OFF_HG = 8192; OFF_MA = 9216; OFF_MB = 10240

    def sb(name, shape, dt=F32):
        return P.sb("s_" + name, shape, dt)

    class phase:
        def __enter__(self):
            self.outer = P.stack
            self.es = ExitStack()
            self.es.__enter__()
            P.stack = self.es
            return self

        def __exit__(self, *a):
            P.barrier()
            P.stack = self.outer
            return self.es.__exit__(*a)

    att_b = sb("att_b", [NS, 512], BF16)
    ho_b = sb("ho_b", [NS, 1024], BF16)

    nwrs = sb("nwrs", [NS, D]); P.dma('sp', nwrs[:], nwr_d[0:NS, :], w=[nwrs])
    lblr = sb("lblr", [NS, 2048]); P.dma('sp', lblr[:], lblr_d, w=[lblr])
    hnwr = sb("hnwr", [NS, 1024]); P.dma('sp', hnwr[:], hnwr_d, w=[hnwr])
    sbias = sb("sbias", [128, 24]); P.dma('sp', sbias[:], sbias_d, w=[sbias])
    onehot = sb("onehot", [NS, NS * 128]); P.dma('sp', onehot[:], onehot_d, w=[onehot])
    bdm = sb("bdm", [8, 520]); P.dma('sp', bdm[:], bdm_d, w=[bdm])
    en = sb("en", [8, NS * NS]); P.dma('sp', en[:], en_d, w=[en])
    xs = sb("xs", [NS, D]); P.dma('sp', xs[:], xs_d, w=[xs])
    sq = sb("sq", [NS, D])
    st1 = sb("st1", [NS, 16]); st2 = sb("st2", [NS, 16])
    xn = sb("xn", [NS, D], BF16)
    hsT = sb("hsT", [128, 8, NS], BF16)
    projs = sb("projs", [NS, 11264])

    def mm(bank_ap, lhsT, rhs, first, last, r, w):
        P.pe(lambda e: e.matmul(bank_ap, lhsT=lhsT, rhs=rhs, start=first, stop=last), r=r, w=w)

    def rms_rows(dst_bf, src, wrow):
        P.act(lambda e: e.activation(out=sq[:], in_=src[:], func=A.Square), r=[src], w=[sq])
        P.dve(lambda e: e.tensor_reduce(out=st1[:, 0:1], in_=sq[:], axis=AX.X, op=ALU.add), r=[sq], w=[st1])
        P.act(lambda e: e.activation(out=st2[:, 0:1], in_=st1[:, 0:1], func=A.Ln, bias=epsc[0:NS, 0:1], scale=1.0 / D), r=[st1, epsc], w=[st2])
        P.act(lambda e: e.activation(out=st2[:, 0:1], in_=st2[:, 0:1], func=A.Exp, scale=-0.5), r=[st2], w=[st2])
        P.dve(lambda e: e.scalar_tensor_tensor(out=dst_bf[:], in0=src[:], scalar=st2[:, 0:1], in1=wrow[:], op0=ALU.mult, op1=ALU.mult), r=[src, st2, wrow], w=[dst_bf])

    def to_fm_bf(dstT, src_bf, nch):
        for c in range(nch):
            P.pe(lambda e, c=c: e.transpose(out=pbf[:, c * NS:(c + 1) * NS], in_=src_bf[0:NS, c * 128:(c + 1) * 128], identity=ident[0:NS, 0:NS]), r=[src_bf, ident], w=[pbf])
        P.act(lambda e: e.copy(out=dstT[:].rearrange("p c n -> p (c n)"), in_=pbf[:, 0:nch * NS]), r=[pbf], w=[dstT])

    rms_rows(xn, xs, nwrs)
    to_fm_bf(hsT, xn, 8)
    for s in range(NSL):
        wsb = load_slice(s)
        bank = pb[s % 2]
        for c in range(8):
            mm(bank[0:NS, :], hsT[:, c, :], wsb[:, c, :], c == 0, c == 7, [hsT, wsb], [bank])
        if s % 2 == 0:
            P.act(lambda e, s=s, bank=bank: e.copy(out=projs[:, s * 512:(s + 1) * 512], in_=bank[0:NS, :]), r=[bank], w=[projs])
        else:
            P.dve(lambda e, s=s, bank=bank: e.tensor_copy(out=projs[:, s * 512:(s + 1) * 512], in_=bank[0:NS, :]), r=[bank], w=[projs])
    for g in range(3):
        P.dma('sp', kvs_d[g], projs[:, g * 1536 + 512:g * 1536 + 1536], r=[projs], sem_on=projs)
    out_bufs.append(projs)

    phB = phase(); phB.__enter__()
    accn = sb("accn", [NS, 512]); accd = sb("accd", [NS, 8])
    tq = sb("tq", [NS, 512]); ts_ = sb("ts", [NS, 8]); tp = sb("tp", [NS, 8])
    for g in range(3):
        b0 = g * 1536
        P.dve(lambda e, b0=b0: e.tensor_tensor(out=tq[:], in0=projs[:, b0:b0 + 512], in1=projs[:, b0 + 512:b0 + 1024], op=ALU.mult), r=[projs], w=[tq])
        P.dve(lambda e: e.tensor_reduce(out=ts_[:], in_=tq[:].rearrange("p (h e) -> p h e", e=64), axis=AX.X, op=ALU.add), r=[tq], w=[ts_])
        P.act(lambda e: e.activation(out=tp[:], in_=ts_[:], func=A.Exp, scale=0.125), r=[ts_], w=[tp])
        for h in range(8):
            hs = slice(h * 64, (h + 1) * 64)
            if g == 0:
                P.dve(lambda e, h=h, hs=hs, b0=b0: e.tensor_scalar(out=accn[:, hs], in0=projs[:, b0 + 1024 + h * 64:b0 + 1024 + (h + 1) * 64], scalar1=tp[:, h:h + 1], scalar2=None, op0=ALU.mult), r=[projs, tp], w=[accn])
            else:
                P.dve(lambda e, h=h, hs=hs, b0=b0: e.scalar_tensor_tensor(out=accn[:, hs], in0=projs[:, b0 + 1024 + h * 64:b0 + 1024 + (h + 1) * 64], scalar=tp[:, h:h + 1], in1=accn[:, hs], op0=ALU.mult, op1=ALU.add), r=[projs, tp, accn], w=[accn])
        if g == 0:
            P.dve(lambda e: e.tensor_copy(out=accd[:], in_=tp[:]), r=[tp], w=[accd])
        else:
            P.dve(lambda e: e.tensor_tensor(out=accd[:], in0=accd[:], in1=tp[:], op=ALU.add), r=[accd, tp], w=[accd])
    ck = [sb("ck%d" % i, [128, 1024]) for i in range(3)]
    prod = sb("prod", [128, 512])
    s8 = sb("s8", [128, 8]); p8s = [sb("p8_%d" % i, [128, 8]) for i in range(2)]
    msk = sb("msk", [8, 520])
    items = [(g, n) for g in range(3) for n in range(NS)]

    def stage1(it):
        g, n = items[it]
        win, dil = GROUPS[g]
        b0 = g * 1536
        c_ = ck[it % 3]
        p8 = p8s[it % 2]
        P.dma('sp', c_[:], c_d[g][n, 0:win:dil, :], w=[c_])
        qb = pb[2]
        mm(qb[:], onehot[:, n * 128:(n + 1) * 128], projs[:, b0:b0 + 512], True, True, [onehot, projs], [qb])
        P.dve(lambda e, c_=c_, qb=qb: e.tensor_tensor(out=prod[:], in0=c_[:, 0:512], in1=qb[:], op=ALU.mult), r=[c_, qb], w=[prod])
        P.dve(lambda e: e.tensor_reduce(out=s8[:], in_=prod[:].rearrange("p (h e) -> p h e", e=64), axis=AX.X, op=ALU.add), r=[prod], w=[s8])
        P.dve(lambda e, g=g: e.scalar_tensor_tensor(out=s8[:], in0=s8[:], scalar=0.125, in1=sbias[:, g * 8:(g + 1) * 8], op0=ALU.mult, op1=ALU.add), r=[s8, sbias], w=[s8])
        P.act(lambda e, p8=p8: e.activation(out=p8[:], in_=s8[:], func=A.Exp), r=[s8], w=[p8])

    def stage2(it):
        g, n = items[it]
        c_ = ck[it % 3]
        p8 = p8s[it % 2]
        nb = pb[3]; db = pb[4]
        mm(nb[0:8, :], p8[:], c_[:, 512:1024], True, True, [p8, c_], [nb])
        mm(db[0:8, 0:8], p8[:], ones_f[:, 0:8], True, True, [p8, ones_f], [db])
        P.dve(lambda e, nb=nb: e.tensor_tensor(out=msk[:, 0:512], in0=nb[0:8, :], in1=bdm[:, 0:512], op=ALU.mult), r=[nb, bdm], w=[msk])
        P.dve(lambda e, db=db: e.tensor_tensor(out=msk[:, 512:520], in0=db[0:8, 0:8], in1=bdm[:, 512:520], op=ALU.mult), r=[db, bdm], w=[msk])
        rb = pb[5]; rd = pb[6]
        mm(rb[0:NS, :], en[:, n * NS:(n + 1) * NS], msk[:, 0:512], True, True, [en, msk], [rb])
        mm(rd[0:NS, 0:8], en[:, n * NS:(n + 1) * NS], msk[:, 512:520], True, True, [en, msk], [rd])
        P.dve(lambda e, rb=rb: e.tensor_tensor(out=accn[:], in0=accn[:], in1=rb[0:NS, :], op=ALU.add), r=[accn, rb], w=[accn])
        P.dve(lambda e, rd=rd: e.tensor_tensor(out=accd[:], in0=accd[:], in1=rd[0:NS, 0:8], op=ALU.add), r=[accd, rd], w=[accd])

    stage1(0)
    for it in range(len(items)):
        if it + 1 < len(items):
            stage1(it + 1)
        stage2(it)
    att_s = sb("att_s", [NS, 512])
    gsa = sb("gsa", [NS, 512])
    P.dve(lambda e: e.reciprocal(out=accd[:], in_=accd[:]), r=[accd], w=[accd])
    for h in range(8):
        hs = slice(h * 64, (h + 1) * 64)
        P.dve(lambda e, h=h, hs=hs: e.tensor_scalar(out=att_s[:, hs], in0=accn[:, hs], scalar1=accd[:, h:h + 1], scalar2=None, op0=ALU.mult), r=[accn, accd], w=[att_s])
    P.act(lambda e: e.activation(out=gsa[:], in_=projs[:, OFF_G:OFF_G + 512], func=A.Silu), r=[projs], w=[gsa])
    P.dve(lambda e: e.tensor_tensor(out=att_b[:], in0=att_s[:], in1=gsa[:], op=ALU.mult), r=[att_s, gsa], w=[att_b])
    phB.__exit__(None, None, None)
    phC = phase(); phC.__enter__()

    lb = sb("lb", [NS, 1024]); oml = sb("oml", [NS, 1024])
    f_t = sb("f_t", [NS, 1024]); hk_t = sb("hk_t", [NS, 1024]); hq_t = sb("hq_t", [NS, 1024])
    P.dve(lambda e: e.tensor_tensor(out=lb[:], in0=lblr[:, 0:1024], in1=lblr[:, 1024:2048], op=ALU.subtract), r=[lblr], w=[lb])
    P.act(lambda e: e.activation(out=lb[:], in_=lb[:], func=A.Sigmoid), r=[lb], w=[lb])
    P.dve(lambda e: e.tensor_scalar(out=oml[:], in0=lb[:], scalar1=-1.0, scalar2=1.0, op0=ALU.mult, op1=ALU.add), r=[lb], w=[oml])
    P.act(lambda e: e.activation(out=f_t[:], in_=projs[:, OFF_F:OFF_F + 1024], func=A.Sigmoid), r=[projs], w=[f_t])
    P.dve(lambda e: e.tensor_tensor(out=f_t[:], in0=f_t[:], in1=oml[:], op=ALU.mult), r=[f_t, oml], w=[f_t])
    P.dve(lambda e: e.tensor_tensor(out=hk_t[:], in0=oml[:], in1=f_t[:], op=ALU.subtract), r=[oml, f_t], w=[hk_t])
    P.dve(lambda e: e.tensor_tensor(out=f_t[:], in0=f_t[:], in1=lb[:], op=ALU.add), r=[f_t, lb], w=[f_t])
    P.act(lambda e: e.activation(out=hq_t[:], in_=projs[:, OFF_Q:OFF_Q + 1024], func=A.Silu), r=[projs], w=[hq_t])
    P.dve(lambda e: e.tensor_scalar(out=hq_t[:], in0=hq_t[:], scalar1=float(128 ** -0.5), scalar2=None, op0=ALU.mult), r=[hq_t], w=[hq_t])
    fT = sb("fT", [128, 8, NS]); hkT = sb("hkT", [128, 8, NS]); hqT = sb("hqT", [128, 8, NS])
    for src, dst, bank in ((f_t, fT, pb[0]), (hk_t, hkT, pb[1]), (hq_t, hqT, pb[2])):
        for h in range(8):
            P.pe(lambda e, h=h, src=src, bank=bank: e.transpose(out=bank[:, h * NS:(h + 1) * NS], in_=src[0:NS, h * 128:(h + 1) * 128], identity=ident_f[0:NS, 0:NS]), r=[src, ident_f], w=[bank])
        P.dve(lambda e, dst=dst, bank=bank: e.tensor_copy(out=dst[:].rearrange("p h n -> p (h n)"), in_=bank[:, 0:8 * NS]), r=[bank], w=[dst])
    Qm = sb("Qm", [128, 8, NS, NS])
    P.pool(lambda e: e.memset(Qm[:].rearrange("p h a b -> p (h a b)"), 0.0), w=[Qm])
    for n in range(NS):
        P.dve(lambda e, n=n: e.tensor_copy(out=Qm[:, :, n, n], in_=hqT[:, :, n]), r=[hqT], w=[Qm])
    oacc = sb("oacc", [NS, 1024])
    P.pool(lambda e: e.memset(oacc[:], 0.0), w=[oacc])
    S0 = [sb("S0_%d" % i, [128, 8, 128]) for i in range(2)]
    Sn = [sb("Sn_%d" % i, [128, 8, 128]) for i in range(2)]
    tmpk = sb("tmpk", [128, 128])
    def hstage1(n):
        s0 = S0[n % 2]; sn = Sn[n % 2]
        P.dma('sp', s0[:], sh_d[n].rearrange("h k v -> k h v"), w=[s0])
        ib = (pb[3], pb[4])
        for half in range(2):
            mm(ib[half][:], onehot[:, n * 128:(n + 1) * 128], projs[:, OFF_I + half * 512:OFF_I + (half + 1) * 512], True, True, [onehot, projs], [ib[half]])
        for h in range(8):
            src = ib[h // 4][:, (h % 4) * 128:(h % 4 + 1) * 128]
            P.dve(lambda e, src=src, h=h, n=n: e.tensor_scalar(out=tmpk[:], in0=src, scalar1=hkT[:, h, n:n + 1], scalar2=None, op0=ALU.mult), r=[ib[h // 4], hkT], w=[tmpk])
            P.dve(lambda e, h=h, n=n, s0=s0, sn=sn: e.scalar_tensor_tensor(out=sn[:, h, :], in0=s0[:, h, :], scalar=fT[:, h, n:n + 1], in1=tmpk[:], op0=ALU.mult, op1=ALU.add), r=[s0, fT, tmpk], w=[sn])
        P.dma('sp', sts_d[n].rearrange("h k v -> k h v"), sn[:], r=[sn], sem_on=sn)
        if sn not in out_bufs:
            out_bufs.append(sn)

    def hstage2(n):
        sn = Sn[n % 2]
        ob = (pb[5], pb[6])
        for h in range(8):
            mm(ob[h // 4][0:NS, (h % 4) * 128:(h % 4 + 1) * 128], Qm[:, h, n, :], sn[:, h, :], True, True, [Qm, sn], [ob[h // 4]])
        for half in range(2):
            P.dve(lambda e, half=half, ob=ob: e.tensor_tensor(out=oacc[:, half * 512:(half + 1) * 512], in0=oacc[:, half * 512:(half + 1) * 512], in1=ob[half][0:NS, :], op=ALU.add), r=[oacc, ob[half]], w=[oacc])

    hstage1(0)
    for n in range(NS):
        if n + 1 < NS:
            hstage1(n + 1)
        hstage2(n)
    o2s = sb("o2s", [NS, 1024]); ssh = sb("ssh", [NS, 8]); rsh = sb("rsh", [NS, 8])
    gsh = sb("gsh", [NS, 1024])
    P.act(lambda e: e.activation(out=o2s[:], in_=oacc[:], func=A.Square), r=[oacc], w=[o2s])
    P.dve(lambda e: e.tensor_reduce(out=ssh[:], in_=o2s[:].rearrange("p (h v) -> p h v", v=128), axis=AX.X, op=ALU.add), r=[o2s], w=[ssh])
    P.act(lambda e: e.activation(out=rsh[:], in_=ssh[:], func=A.Ln, bias=epsc[0:NS, 0:1], scale=1.0 / 128), r=[ssh, epsc], w=[rsh])
    P.act(lambda e: e.activation(out=rsh[:], in_=rsh[:], func=A.Exp, scale=-0.5), r=[rsh], w=[rsh])
    P.act(lambda e: e.activation(out=gsh[:], in_=projs[:, OFF_HG:OFF_HG + 1024], func=A.Silu), r=[projs], w=[gsh])
    for h in range(8):
        hs = slice(h * 128, (h + 1) * 128)
        P.dve(lambda e, h=h, hs=hs: e.scalar_tensor_tensor(out=o2s[:, hs], in0=oacc[:, hs], scalar=rsh[:, h:h + 1], in1=hnwr[:, hs], op0=ALU.mult, op1=ALU.mult), r=[oacc, rsh, hnwr], w=[o2s])
    P.dve(lambda e: e.tensor_tensor(out=ho_b[:], in0=o2s[:], in1=gsh[:], op=ALU.mult), r=[o2s, gsh], w=[ho_b])
    phC.__exit__(None, None, None)

    attTs = sb("attTs", [128, 4, NS], BF16); hoTs = sb("hoTs", [128, 8, NS], BF16)
    to_fm_bf(attTs, att_b, 4)
    to_fm_bf(hoTs, ho_b, 8)
    gas = sb("gas", [NS, 1024]); gbs = sb("gbs", [NS, 1024])
    P.act(lambda e: e.activation(out=gas[:], in_=projs[:, OFF_MA:OFF_MA + 1024], func=A.Sigmoid), r=[projs], w=[gas])
    P.act(lambda e: e.activation(out=gbs[:], in_=projs[:, OFF_MB:OFF_MB + 1024], func=A.Sigmoid), r=[projs], w=[gbs])
    mg = sb("mg", [NS, 1024]); mg2 = sb("mg2", [NS, 1024]); mgb = sb("mgb", [NS, 1024], BF16)
    wa_s = [sb("wa_s%d" % i, [128, 4, 128], BF16) for i in range(2)]
    wh_s = [sb("wh_s%d" % i, [128, 8, 128], BF16) for i in range(2)]
    for jj in range(8):
        wa_ = wa_s[jj % 2]; wh_ = wh_s[jj % 2]
        P.dma('sp', wa_[:], wapb_t[jj], r=[wapb], w=[wa_], sem_on=wa_)
        P.dma('sp', wh_[:], whpb_t[jj], r=[whpb], w=[wh_], sem_on=wh_)
        js = slice(jj * 128, (jj + 1) * 128)
        b1 = pb[0]; b2 = pb[1]
        for hp in range(4):
            mm(b1[0:NS, 0:128], attTs[:, hp, :], wa_[:, hp, :], hp == 0, hp == 3, [attTs, wa_], [b1])
        for h in range(8):
            mm(b2[0:NS, 0:128], hoTs[:, h, :], wh_[:, h, :], h == 0, h == 7, [hoTs, wh_], [b2])
        P.dve(lambda e, js=js, b1=b1: e.tensor_tensor(out=mg[:, js], in0=gas[:, js], in1=b1[0:NS, 0:128], op=ALU.mult), r=[gas, b1], w=[mg])
        P.dve(lambda e, js=js, b2=b2: e.tensor_tensor(out=mg2[:, js], in0=gbs[:, js], in1=b2[0:NS, 0:128], op=ALU.mult), r=[gbs, b2], w=[mg2])
    P.dve(lambda e: e.tensor_tensor(out=mgb[:], in0=mg[:], in1=mg2[:], op=ALU.add), r=[mg, mg2], w=[mgb])
    mgT = sb("mgT", [128, 8, NS], BF16)
    to_fm_bf(mgT, mgb, 8)
    xr_s = sb("xr_s", [NS, D]); fnws = sb("fnws", [NS, D]); ysb = sb("ysb", [NS, D])
    P.dma('sp', fnws[:], L['fnw_d'][0:NS, :], w=[fnws])
    for half in range(2):
        wsl = L['wsl'][L['wsl_i'][0] % 3]
        L['wsl_i'][0] += 1
        P.dma('sp', wsl[:], woutb_t[half], r=[woutb], w=[wsl], sem_on=wsl)
        bank = pb[2 + half]
        for c in range(8):
            mm(bank[0:NS, :], mgT[:, c, :], wsl[:, c, :], c == 0, c == 7, [mgT, wsl], [bank])
        P.dve(lambda e, half=half, bank=bank: e.tensor_tensor(out=xr_s[:, half * 512:(half + 1) * 512], in0=bank[0:NS, :], in1=xs[:, half * 512:(half + 1) * 512], op=ALU.add), r=[bank, xs], w=[xr_s])
    rms_rows(ysb, xr_s, fnws)
    P.dma('sp', ys_d, ysb[:], r=[ysb], sem_on=ysb)
    out_bufs.append(ysb)


def _consts():
    c = {}
    c["ident"] = np.eye(128, dtype=np.float32)
    n = 24
    slopes = (2.0 ** (-8.0 * np.arange(1, n + 1) / n)).astype(np.float32).reshape(3, 8)
    k = np.arange(128)[:, None].astype(np.float32)
    q = np.arange(128)[None, :].astype(np.float32)
    eb = np.zeros((128, 12, 4, 128), np.float32)
    for g, (win, d) in enumerate(GROUPS):
        for hp in range(4):
            for hh in range(2):
                s = slopes[g, hp * 2 + hh]
                cur = np.where(q >= k, np.exp(-s * d * np.maximum(q - k, 0.0)), 0.0)
                prev = np.where(k >= q, np.exp(-s * d * (q + 128 - k)), 0.0)
                eb[:, g * 4 + hp, 2 * hh, :] = cur
                eb[:, g * 4 + hp, 2 * hh + 1, :] = prev
    c["eb"] = eb.reshape(128, 12 * 512)
    p = np.arange(128)[:, None] % 64
    t = np.arange(64)[None, :]
    c["hmask"] = (t >= p).astype(np.float32)
    rs = np.ones((128, 512), np.float32)
    rs[:, ::64] = 0.0
    c["rsm"] = rs
    sel = np.zeros((128, 64), np.float32)
    sel[64, :] = 1.0
    c["sel"] = sel
    sb = np.zeros((128, 24), np.float32)
    for g, (win, d) in enumerate(GROUPS):
        for h in range(8):
            sb[:, g * 8 + h] = -slopes[g, h] * d * (128 - np.arange(128))
    c["sbias"] = sb
    oh = np.zeros((NS, NS, 128), np.float32)
    for i in range(NS):
        oh[i, i, :] = 1.0
    c["onehot"] = oh.reshape(NS, NS * 128)
    bdm = np.zeros((8, 520), np.float32)
    for h in range(8):
        bdm[h, h * 64:(h + 1) * 64] = 1.0
        bdm[h, 512 + h] = 1.0
    c["bdm"] = bdm
    en = np.zeros((8, NS, NS), np.float32)
    for i in range(NS):
        en[:, i, i] = 1.0
    c["en"] = en.reshape(8, NS * NS)
    return c


_CACHE = {}


def kernel(x_prompt, x_sample, cache_kv_w128, cache_kv_w512, cache_kv_w2048, state_hgrn,
           norm_w, w_in, w_att_proj, w_hg_proj, w_out, hg_norm_w, hg_lb_logits, final_norm_w,
           _nst=NST, _cores=8, _prompt=True, _sample=True):
    f = lambda a: np.ascontiguousarray(np.asarray(a, dtype=np.float32))
    key = (_nst, _prompt, _sample)
    if "nc" not in _CACHE or _CACHE.get("key") != key:
        _CACHE["nc"] = build_program(do_prompt=_prompt, do_sample=_sample, nst=_nst)
        _CACHE["key"] = key
    nc = _CACHE["nc"]
    cst = _consts()
    shared = dict(cst)
    shared["w_in"] = f(w_in[0])
    shared["wap"] = f(w_att_proj[0])
    shared["whp"] = f(w_hg_proj[0])
    shared["wout"] = f(w_out[0])
    shared["nw"] = f(np.asarray(norm_w[0]).reshape(8, 128).T)
    shared["nwr"] = f(np.broadcast_to(np.asarray(norm_w[0])[None, :], (128, D)))
    shared["hnw"] = f(np.asarray(hg_norm_w[0]).reshape(128, 1))
    shared["hnwr"] = f(np.broadcast_to(np.tile(np.asarray(hg_norm_w[0]), 8)[None, :], (NS, 1024)))
    lg = np.asarray(hg_lb_logits)
    shared["lbl"] = f(lg.reshape(2, 8, 128).transpose(2, 0, 1).reshape(128, 16))
    shared["lblr"] = f(np.broadcast_to(lg.reshape(1, 2048), (NS, 2048)))
    shared["fnw"] = f(np.broadcast_to(np.asarray(final_norm_w)[None, :], (128, D)))
    in_maps = []
    for i in range(_cores):
        m = dict(shared)
        m["x"] = f(x_prompt[i])
        sl = slice(NS * i, NS * (i + 1))
        m["xs"] = f(np.asarray(x_sample)[sl, 0, :])
        m["c128"] = f(np.asarray(cache_kv_w128)[0, sl].reshape(NS, 128, 1024))
        m["c512"] = f(np.asarray(cache_kv_w512)[0, sl].reshape(NS, 512, 1024))
        m["c2048"] = f(np.asarray(cache_kv_w2048)[0, sl].reshape(NS, 2048, 1024))
        m["sh"] = f(np.asarray(state_hgrn)[0, sl])
        in_maps.append(m)
    res = run_bass_kernel_spmd(nc, in_maps, core_ids=list(range(_cores)))
    R = res.results
    n = _cores
    y = np.stack([R[i]["y"] for i in range(n)], 0)
    ys = np.concatenate([R[i]["ys"] for i in range(n)], 0).reshape(n * NS, 1, D)
    kvp = [np.stack([R[i][k] for i in range(n)], 0).reshape(1, n, w, 2, 8, 64)
           for k, w in (("kv128", 128), ("kv512", 512), ("kv2048", 2048))]
    stp = np.stack([R[i]["stp"] for i in range(n)], 0).reshape(1, n, 8, 128, 128)
    kvs = [np.concatenate([R[i][k] for i in range(n)], 0).reshape(1, n * NS, 1, 2, 8, 64)
           for k in ("kvs128", "kvs512", "kvs2048")]
    sts = np.concatenate([R[i]["sts"] for i in range(n)], 0).reshape(1, n * NS, 8, 128, 128)
    return (y, ys, kvp[0], kvp[1], kvp[2], stp, kvs[0], kvs[1], kvs[2], sts)
```

```python
import numpy as np
import concourse.bass as bass
import concourse.mybir as mybir
from concourse.bass_utils import run_bass_kernel_spmd

F32 = mybir.dt.float32
BF16 = mybir.dt.bfloat16
AF = mybir.ActivationFunctionType
ALU = mybir.AluOpType
AX = mybir.AxisListType

ENGS = ['pe', 'act', 'dve', 'pool', 'sp']


class Buf:
    __slots__ = ('name', 't', 'lw', 'rd', 'dsem')

    def __init__(self, name, t):
        self.name = name
        self.t = t
        self.lw = None
        self.rd = {}
        self.dsem = None

    def __getitem__(self, idx):
        return self.t[idx]


class DSem:
    __slots__ = ('key', 'val', 'h')

    def __init__(self, key, h):
        self.key = key
        self.val = 0
        self.h = h


class Prog:
    def __init__(self, nc, stack):
        self.nc = nc
        self.stack = stack
        self.sem_stack = stack
        self.lists = {e: [] for e in ENGS}
        self.cnt = {e: 0 for e in ENGS}
        self.seen = {e: {} for e in ENGS}
        self.semh = {}
        for e in ENGS:
            self.semh[e] = stack.enter_context(nc.semaphore('s_' + e))
        self.ndsem = 0
        self.alldsem = []
        self.nbuf = 0

    def sb(self, name, shape, dtype):
        t = self.stack.enter_context(self.nc.sbuf_tensor("sb_" + name, list(shape), dtype))
        return Buf(name, t)

    def ps(self, name, shape, dtype=F32):
        t = self.stack.enter_context(self.nc.psum_tensor("ps_" + name, list(shape), dtype))
        return Buf(name, t)

    def wrap(self, name, t):
        return Buf(name, t)

    def _dsem(self, b):
        if b.dsem is None:
            h = self.sem_stack.enter_context(self.nc.semaphore('d%d' % self.ndsem))
            key = ('dma', self.ndsem)
            self.ndsem += 1
            self.semh[key] = h
            b.dsem = DSem(key, h)
            self.alldsem.append(b.dsem)
        return b.dsem

    def emit(self, eng, fn, reads=(), writes=(), dma=None):
        deps = {}

        def add(dep):
            if dep is None:
                return
            k, v = dep
            if deps.get(k, 0) < v:
                deps[k] = v

        for b in reads:
            add(b.lw)
        for b in writes:
            add(b.lw)
            for k, v in b.rd.items():
                add((k, v))
        ds = None
        if dma is not None:
            ds = self._dsem(dma)
            if ds.val:
                add((ds.key, ds.val))
        waits = []
        seen = self.seen[eng]
        for k, v in deps.items():
            if k == eng and eng == 'pe':
                continue
            if seen.get(k, 0) >= v:
                continue
            seen[k] = v
            waits.append((k, v))
        if ds is None:
            self.cnt[eng] += 1
            tok = (eng, self.cnt[eng])
        else:
            ds.val += 16
            tok = (ds.key, ds.val)
        self.lists[eng].append((waits, fn, ds))
        for b in reads:
            if b.rd.get(tok[0], 0) < tok[1]:
                b.rd[tok[0]] = tok[1]
        for b in writes:
            b.lw = tok
            b.rd = {}
        return tok

    def pe(self, fn, r=(), w=()):
        return self.emit('pe', fn, r, w)

    def act(self, fn, r=(), w=()):
        return self.emit('act', fn, r, w)

    def dve(self, fn, r=(), w=()):
        return self.emit('dve', fn, r, w)

    def pool(self, fn, r=(), w=()):
        return self.emit('pool', fn, r, w)

    def dma(self, q, out_ap, in_ap, r=(), w=(), sem_on=None, **kw):
        if sem_on is None:
            sem_on = (list(w) + list(r))[0]
        return self.emit(q, lambda e: e.dma_start(out=out_ap, in_=in_ap, **kw), r, w, dma=sem_on)

    def final_wait(self, eng, bufs):
        deps = {}
        for b in bufs:
            for dep in [b.lw] + list(b.rd.items()):
                if dep is None:
                    continue
                k, v = dep
                if deps.get(k, 0) < v:
                    deps[k] = v
        waits = [(k, v) for k, v in deps.items() if self.seen[eng].get(k, 0) < v]
        for k, v in waits:
            self.seen[eng][k] = v
        self.lists[eng].append((waits, None, None))

    def barrier(self):
        toks = [(e, self.cnt[e]) for e in ENGS if self.cnt[e] > 0]
        toks += [(ds.key, ds.val) for ds in self.alldsem if ds.val > 0]
        for e in ENGS:
            waits = []
            for k, v in toks:
                if k == e and e == 'pe':
                    continue
                if self.seen[e].get(k, 0) < v:
                    self.seen[e][k] = v
                    waits.append((k, v))
            self.lists[e].append((waits, None, None))

    def build(self):
        nc = self.nc
        semh = self.semh
        lists = self.lists
        needed = {e: set() for e in ENGS}
        for e in ENGS:
            for waits, fn, ds in lists[e]:
                for k, v in waits:
                    if k in needed:
                        needed[k].add(v)
        rank = {}
        for e in ENGS:
            rank[e] = {v: i + 1 for i, v in enumerate(sorted(needed[e]))}

        def run(engname):
            def body(e):
                own = semh[engname]
                seq = 0
                myrank = rank[engname]
                for waits, fn, ds in lists[engname]:
                    for k, v in waits:
                        if k in rank:
                            e.wait_ge(semh[k], rank[k][v])
                        else:
                            e.wait_ge(semh[k], v)
                    if fn is None:
                        continue
                    ins = fn(e)
                    if ds is None:
                        seq += 1
                        if seq in myrank:
                            ins.then_inc(own, 1)
                    else:
                        ins.then_inc(ds.h, 16)
            return body

        with nc.Block() as block:
            block.tensor(run('pe'))
            block.scalar(run('act'))
            block.vector(run('dve'))
            block.gpsimd(run('pool'))
            block.sync(run('sp'))


class View:
    def __init__(self, base, ap):
        self.base = base
        self.t = ap
        self.name = base.name

    def __getitem__(self, idx):
        return self.t[idx]

    lw = property(lambda s: s.base.lw, lambda s, v: setattr(s.base, 'lw', v))
    rd = property(lambda s: s.base.rd, lambda s, v: setattr(s.base, 'rd', v))
    dsem = property(lambda s: s.base.dsem, lambda s, v: setattr(s.base, 'dsem', v))

from contextlib import ExitStack

T = 8192
D = 1024
ST = 2048
NST = T // ST
SUB = 512
EPS = 1e-6
GROUPS = ((128, 1), (512, 4), (2048, 16))
NS = 16
NSL = 22


def unit_tokens(g, u):
    d = GROUPS[g][1]
    if d == 1:
        return slice(128 * u, 128 * u + 128)
    if d == 4:
        b, r = divmod(u, 4)
        return slice(512 * b + r, 512 * b + 512, 4)
    return slice(u, 2048, 16)


def build_program(do_prompt=True, do_sample=True, nst=NST):
    nc = bass.Bass("TRN2", target_bir_lowering=False)

    def din(name, shape):
        return nc.dram_tensor(name, list(shape), F32, kind="ExternalInput").ap()

    def dout(name, shape):
        return nc.dram_tensor(name, list(shape), F32, kind="ExternalOutput").ap()

    x_d = din("x", [T, D])
    xs_d = din("xs", [NS, D])
    c_d = [din("c128", [NS, 128, 1024]), din("c512", [NS, 512, 1024]), din("c2048", [NS, 2048, 1024])]
    sh_d = din("sh", [NS, 8, 128, 128])
    win_d = din("w_in", [D, 11264])
    wap_d = din("wap", [512, D])
    whp_d = din("whp", [D, D])
    wout_d = din("wout", [D, D])
    nw_d = din("nw", [128, 8])
    hnw_d = din("hnw", [128, 1])
    lbl_d = din("lbl", [128, 16])
    fnw_d = din("fnw", [128, D])
    nwr_d = din("nwr", [128, D])
    ident_d = din("ident", [128, 128])
    eb_d = din("eb", [128, 12 * 512])
    hmask_d = din("hmask", [128, 64])
    rsm_d = din("rsm", [128, 512])
    sel_d = din("sel", [128, 64])
    lblr_d = din("lblr", [NS, 2048])
    hnwr_d = din("hnwr", [NS, 1024])
    sbias_d = din("sbias", [128, 24])
    onehot_d = din("onehot", [NS, NS * 128])
    bdm_d = din("bdm", [8, 520])
    en_d = din("en", [8, NS * NS])

    y_d = dout("y", [T, D])
    ys_d = dout("ys", [NS, D])
    kvp_d = [dout("kv128", [128, 1024]), dout("kv512", [512, 1024]), dout("kv2048", [2048, 1024])]
    stp_d = dout("stp", [8, 128, 128])
    kvs_d = [dout("kvs128", [NS, 1024]), dout("kvs512", [NS, 1024]), dout("kvs2048", [NS, 1024])]
    sts_d = dout("sts", [NS, 8, 128, 128])

    wib_t = nc.dram_tensor("wib", [NSL, 128, 8, 512], BF16).ap()
    wapb_t = nc.dram_tensor("wapb", [8, 128, 4, 128], BF16).ap()
    whpb_t = nc.dram_tensor("whpb", [8, 128, 8, 128], BF16).ap()
    woutb_t = nc.dram_tensor("woutb", [2, 128, 8, 512], BF16).ap()
    NU = (1, 4, 16)
    hist_t = [[(nc.dram_tensor("histk%d_%d" % (g, hp), [128, NU[g] * 128], BF16).ap(),
                nc.dram_tensor("histv%d_%d" % (g, hp), [128, NU[g] * 132], BF16).ap())
               for hp in range(4)] for g in range(3)]

    with ExitStack() as stack:
        P = Prog(nc, stack)
        out_bufs = []

        wib = [P.wrap("wib%d" % k, wib_t) for k in range(3)]
        wsrc = win_d.rearrange("(c p) (s n) -> s p c n", p=128, n=512)
        for k, (s0, s1) in enumerate(((0, 8), (8, 16), (16, 22))):
            for s in range(s0, s1):
                P.dma('pool', wib_t[s], wsrc[s], w=[wib[k]], sem_on=wib[k])

        def wib_buf(s):
            return wib[0 if s < 8 else (1 if s < 16 else 2)]

        wapb = P.wrap("wapb", wapb_t)
        whpb = P.wrap("whpb", whpb_t)
        woutb = P.wrap("woutb", woutb_t)
        s_ap = wap_d.rearrange("(hp p) (j n) -> j p hp n", p=128, n=128)
        s_hp = whp_d.rearrange("(h p) (j n) -> j p h n", p=128, n=128)
        for j in range(8):
            P.dma('pool', wapb_t[j], s_ap[j], w=[wapb], sem_on=wapb)
            P.dma('pool', whpb_t[j], s_hp[j], w=[whpb], sem_on=whpb)
        s_wo = wout_d.rearrange("(c p) (s n) -> s p c n", p=128, n=512)
        for s in range(2):
            P.dma('pool', woutb_t[s], s_wo[s], w=[woutb], sem_on=woutb)

        ident_f = P.sb("ident_f", [128, 128], F32)
        ident = P.sb("ident", [128, 128], BF16)
        eb = P.sb("eb", [128, 12, 512], BF16)
        hmask = P.sb("hmask", [128, 64], F32)
        rsm = P.sb("rsm", [128, 512], F32)
        sel = P.sb("sel", [128, 64], F32)
        nw = P.sb("nw", [128, 8], F32)
        hnw = P.sb("hnw", [128, 1], F32)
        lbl = P.sb("lbl", [128, 16], F32)
        fnw = P.sb("fnw", [128, D], F32)
        epsc = P.sb("epsc", [128, 1], F32)
        ones_f = P.sb("ones_f", [128, 128], F32)
        lbc = P.sb("lbc", [128, 32], F32)
        P.dma('sp', ident_f[:], ident_d, w=[ident_f])
        for i in range(12):
            P.dma('pool', eb[:, i, :], eb_d[:, i * 512:(i + 1) * 512], w=[eb], sem_on=eb)
        P.dma('sp', hmask[:], hmask_d, w=[hmask])
        P.dma('sp', rsm[:], rsm_d, w=[rsm])
        P.dma('sp', sel[:], sel_d, w=[sel])
        P.dma('sp', nw[:], nw_d, w=[nw])
        P.dma('sp', hnw[:], hnw_d, w=[hnw])
        P.dma('sp', lbl[:], lbl_d, w=[lbl])
        P.dma('sp', fnw[:], fnw_d, w=[fnw])
        P.dve(lambda e: e.tensor_copy(out=ident[:], in_=ident_f[:]), r=[ident_f], w=[ident])
        P.pool(lambda e: e.memset(epsc[:], EPS), w=[epsc])
        P.pool(lambda e: e.memset(ones_f[:], 1.0), w=[ones_f])
        P.dve(lambda e: e.tensor_tensor(out=lbc[:, 24:32], in0=lbl[:, 0:8], in1=lbl[:, 8:16], op=ALU.subtract), r=[lbl], w=[lbc])
        P.act(lambda e: e.activation(out=lbc[:, 0:8], in_=lbc[:, 24:32], func=AF.Sigmoid), r=[lbc], w=[lbc])
        P.dve(lambda e: e.tensor_scalar(out=lbc[:, 8:16], in0=lbc[:, 0:8], scalar1=-1.0, scalar2=1.0, op0=ALU.mult, op1=ALU.add), r=[lbc], w=[lbc])
        P.dve(lambda e: e.tensor_scalar(out=lbc[:, 16:24], in0=lbc[:, 0:8], scalar1=1.0, scalar2=-1.0, op0=ALU.mult, op1=ALU.add), r=[lbc], w=[lbc])

        pb = [P.ps("pb%d" % i, [128, 512], F32) for i in range(7)]
        pbf = P.ps("pbf", [128, 1024], BF16)

        wsl = [P.sb("wsl%d" % i, [128, 8, 512], BF16) for i in range(3)]
        wsl_i = [0]

        def load_slice(s):
            b = wsl[wsl_i[0] % 3]
            wsl_i[0] += 1
            P.dma('sp', b[:], wib_t[s], r=[wib_buf(s)], w=[b], sem_on=b)
            return b

        def rstd_from_ss(dst, ss, scale, n):
            P.act(lambda e: e.activation(out=dst[:, 0:n], in_=ss[:, 0:n], func=AF.Ln, bias=epsc[:, 0:1], scale=scale), r=[ss, epsc], w=[dst])
            P.act(lambda e: e.activation(out=dst[:, 0:n], in_=dst[:, 0:n], func=AF.Exp, scale=-0.5), r=[dst], w=[dst])

        Lc = locals()
        if do_sample:
            with ExitStack() as sstack:
                P.stack = sstack
                sample_path(P, Lc)
                P.barrier()
            P.stack = stack
        if do_prompt:
            prompt_path(P, Lc)
        P.final_wait('sp', out_bufs)
        P.build()
    return nc


def prompt_path(P, L):
    nc = L['nc']; pb = L['pb']; pbf = L['pbf']; out_bufs = L['out_bufs']
    x_d = L['x_d']; y_d = L['y_d']; kvp_d = L['kvp_d']; stp_d = L['stp_d']
    ident = L['ident']; eb = L['eb']; hmask = L['hmask']; rsm = L['rsm']; sel = L['sel']
    hnw = L['hnw']; fnw = L['fnw']; lbc = L['lbc']; epsc = L['epsc']; ones_f = L['ones_f']
    load_slice = L['load_slice']; rstd_from_ss = L['rstd_from_ss']
    hist_t = L['hist_t']; NU = L['NU']; nst = L['nst']
    wapb_t = L['wapb_t']; whpb_t = L['whpb_t']; woutb_t = L['woutb_t']
    wapb = L['wapb']; whpb = L['whpb']; woutb = L['woutb']
    nwr_d = L['nwr_d']

    nwr = P.sb("nwr", [128, D], F32)
    P.dma('sp', nwr[:], nwr_d, w=[nwr])
    hT = P.sb("hT", [128, 8, ST], BF16)
    xt = [P.sb("xt%d" % i, [128, D], F32) for i in range(2)]
    xn = P.sb("xn", [128, D], BF16)
    ssn = P.sb("ssn", [128, 4], F32)
    rsn = P.sb("rsn", [128, 4], F32)
    qU = P.sb("qU", [128, 16, 128], BF16)
    kU = P.sb("kU", [128, 16, 128], BF16)
    vU = P.sb("vU", [128, 16, 2, 66], BF16)
    kprev = P.sb("kprev", [128, 16, 128], BF16)
    vprev = P.sb("vprev", [128, 16, 2, 66], BF16)
    big = P.sb("big", [128, 9 * 512], F32)
    acc = Buf("acc", big.t[0:65, 0:4096].rearrange("p (h t) -> p h t", h=2))
    tmps = [Buf("tmp%d" % i, big.t[:, i * 512:(i + 1) * 512]) for i in range(9)]
    esb = [[P.sb("esb%d_%d" % (i, hh), [128, 512], BF16) for hh in range(2)] for i in range(2)]
    pTb = [[[P.sb("pTb%d_%d_%d" % (i, hh, uu), [128, 256], BF16) for uu in range(2)] for hh in range(2)] for i in range(2)]
    attT = P.sb("attT", [128, 4, ST], BF16)
    att1 = P.sb("att1", [64, ST], BF16)
    gsA = P.sb("gsA", [64, 512], F32)
    tmpA = P.sb("tmpA", [64, 512], F32)
    rcpA = tmps[8]
    Sck = View(kU, kU.t[:, 0:8, :])
    aTm8 = View(qU, qU.t[:, 0:4, :].rearrange("p a (b t) -> p (a b) t", t=64))
    aTms = [P.sb("aTm%d" % i, [128, 64], BF16) for i in range(2)]
    Sxs = [P.sb("Sx%d" % i, [128, 128], BF16) for i in range(2)]
    histb = [[P.wrap("hist%d_%d" % (g, hp), hist_t[g][hp][0]) for hp in range(4)] for g in range(3)]
    vI = P.sb("vI", [128, 4, 1024], BF16)
    S32 = P.sb("S32", [128, 8, 128], F32)
    Sbf = P.sb("Sbf", [128, 8, 128], BF16)
    sig = tmps[0]
    logf = tmps[1]
    hk = tmps[2]
    bcs = tmps[3]
    Ep = tmps[4]
    En = tmps[5]
    hq = tmps[6]
    kt32 = tmps[7]
    qtl = P.sb("qtl", [128, 512], BF16)
    ktl = P.sb("ktl", [128, 512], BF16)
    kend = P.sb("kend", [128, 512], BF16)
    kendT = P.sb("kendT", [128, 4, 128], BF16)
    o32 = tmps[3]
    o2 = tmps[1]
    rs_h = tmps[5]
    gsH = tmps[8]
    hoT = P.sb("hoT", [128, 8, 512], BF16)
    ga = tmps[0]
    gbt = tmps[1]
    m1 = tmps[2]
    m2 = tmps[3]
    mergedT = View(vI, vI.t[:].rearrange("p a b -> p (a b)").rearrange("p (c t) -> p c t", t=512))
    wapj = [P.sb("wapj%d" % i, [128, 4, 128], BF16) for i in range(2)]
    whpj = [P.sb("whpj%d" % i, [128, 8, 128], BF16) for i in range(2)]
    xr = P.sb("xr", [128, D], F32)
    yo = [P.sb("yo%d" % i, [128, D], F32) for i in range(2)]
    kvtb = yo
    sq = xr
    ssf = P.sb("ssf", [128, 4], F32)
    rsf = P.sb("rsf", [128, 4], F32)

    P.pool(lambda e: e.memset(vU[:].rearrange("p u h e -> p (u h e)"), 1.0), w=[vU])
    P.pool(lambda e: e.memset(kprev[:].rearrange("p u i -> p (u i)"), 0.0), w=[kprev])
    P.pool(lambda e: e.memset(vprev[:].rearrange("p u h e -> p (u h e)"), 0.0), w=[vprev])
    P.pool(lambda e: e.memset(S32[:].rearrange("p h v -> p (h v)"), 0.0), w=[S32])
    P.pool(lambda e: e.memset(Sbf[:].rearrange("p h v -> p (h v)"), 0.0), w=[Sbf])

    def mm(bank_ap, lhsT, rhs, first, last, r, w):
        P.pe(lambda e: e.matmul(bank_ap, lhsT=lhsT, rhs=rhs, start=first, stop=last), r=r, w=w)

    def proj_fm(bank, wsb, off, m, tsl):
        for c in range(8):
            mm(bank[0:m, :], wsb[:, c, off:off + m], hT[:, c, tsl], c == 0, c == 7, [wsb, hT], [bank])

    def phase_n(st):
        def front(tt):
            t0 = st * ST + tt * 128
            xb = xt[tt % 2]
            k = tt % 2
            P.dma('sp', xb[:], x_d[t0:t0 + 128, :], w=[xb])
            P.act(lambda e, xb=xb: e.activation(out=sq[:], in_=xb[:], func=AF.Square), r=[xb], w=[sq])
            P.dve(lambda e, k=k: e.tensor_reduce(out=ssn[:, k:k + 1], in_=sq[:], axis=AX.X, op=ALU.add), r=[sq], w=[ssn])
            P.act(lambda e, k=k: e.activation(out=rsn[:, 2 + k:3 + k], in_=ssn[:, k:k + 1], func=AF.Ln, bias=epsc[:, 0:1], scale=1.0 / D), r=[ssn, epsc], w=[rsn])
            P.act(lambda e, k=k: e.activation(out=rsn[:, k:k + 1], in_=rsn[:, 2 + k:3 + k], func=AF.Exp, scale=-0.5), r=[rsn], w=[rsn])

        def back(tt):
            xb = xt[tt % 2]
            k = tt % 2
            P.dve(lambda e, xb=xb, k=k: e.scalar_tensor_tensor(out=xn[:], in0=xb[:], scalar=rsn[:, k:k + 1], in1=nwr[:], op0=ALU.mult, op1=ALU.mult), r=[xb, rsn, nwr], w=[xn])
            for c in range(8):
                P.pe(lambda e, c=c: e.transpose(out=pbf[:, c * 128:(c + 1) * 128], in_=xn[:, c * 128:(c + 1) * 128], identity=ident[:]), r=[xn, ident], w=[pbf])
            P.act(lambda e, tt=tt: e.copy(out=hT[:, :, tt * 128:(tt + 1) * 128], in_=pbf[:].rearrange("p (c t) -> p c t", t=128)), r=[pbf], w=[hT])

        front(0)
        for tt in range(16):
            if tt + 1 < 16:
                front(tt + 1)
            back(tt)

    def att_gh(st, g, hp):
        d = GROUPS[g][1]
        nu = NU[g]
        off = hp * 128
        wq = load_slice(3 * g)
        wk = load_slice(3 * g + 1)
        wv = load_slice(3 * g + 2)
        hb = histb[g][hp]
        htk, htv = hist_t[g][hp]
        if st > 0:
            P.dma('sp', kprev[:, 0:nu, :].rearrange("p u i -> p (u i)"), htk, r=[hb], w=[kprev], sem_on=kprev)
            P.dma('sp', vprev[:, 0:nu].rearrange("p u h e -> p (u h e)"), htv, r=[hb], w=[vprev], sem_on=vprev)
        for wsb, dst, eng in ((wq, qU, 'act'), (wk, kU, 'dve')):
            for n in range(4):
                bank = pb[n % 2]
                proj_fm(bank, wsb, off, 128, slice(n * 512, (n + 1) * 512))
                if d == 1:
                    o_ap = dst[:, 4 * n:4 * n + 4, :]
                    i_ap = bank[:].rearrange("p (u i) -> p u i", i=128)
                elif d == 4:
                    o_ap = dst[:, 4 * n:4 * n + 4, :]
                    i_ap = bank[:].rearrange("p (i r) -> p r i", r=4)
                else:
                    o_ap = dst[:, :, 32 * n:32 * n + 32]
                    i_ap = bank[:].rearrange("p (i r) -> p r i", r=16)
                if eng == 'act':
                    P.act(lambda e, o_ap=o_ap, i_ap=i_ap: e.copy(out=o_ap, in_=i_ap), r=[bank], w=[dst])
                else:
                    P.dve(lambda e, o_ap=o_ap, i_ap=i_ap: e.tensor_copy(out=o_ap, in_=i_ap), r=[bank], w=[dst])
        import os
        sub = int(os.environ.get("KSUB", "9"))
        if sub < 1:
            return
        for ug in range(4):
            bank = pb[2 + ug % 2]
            for uu in range(4):
                ts = unit_tokens(g, ug * 4 + uu)
                for c in range(8):
                    mm(bank[:, uu * 128:(uu + 1) * 128], hT[:, c, ts], wv[:, c, off:off + 128], c == 0, c == 7, [hT, wv], [bank])
            o_ap = vU[:, ug * 4:ug * 4 + 4, :, 0:64]
            i_ap = bank[:].rearrange("p (u h e) -> p u h e", u=4, h=2)
            if ug % 2 == 0:
                P.act(lambda e, o_ap=o_ap, i_ap=i_ap: e.copy(out=o_ap, in_=i_ap), r=[bank], w=[vU])
            else:
                P.dve(lambda e, o_ap=o_ap, i_ap=i_ap: e.tensor_copy(out=o_ap, in_=i_ap), r=[bank], w=[vU])
        if st == NST - 1 and hp == 0:
            ntile = GROUPS[g][0] // 128
            for tt in range(16 - ntile, 16):
                kvt = kvtb[tt % 2]
                for half, wsb in enumerate((wk, wv)):
                    bank = pb[half]
                    for c in range(8):
                        mm(bank[:], hT[:, c, tt * 128:(tt + 1) * 128], wsb[:, c, :], c == 0, c == 7, [hT, wsb], [bank])
                    if half == 0:
                        P.act(lambda e, kvt=kvt, bank=bank: e.copy(out=kvt[:, 0:512], in_=bank[:]), r=[bank], w=[kvt])
                    else:
                        P.dve(lambda e, kvt=kvt, bank=bank: e.tensor_copy(out=kvt[:, 512:1024], in_=bank[:]), r=[bank], w=[kvt])
                row0 = (tt - (16 - ntile)) * 128
                P.dma('sp', kvp_d[g][row0:row0 + 128, :], kvt[:], r=[kvt], sem_on=kvt)
                if kvt not in out_bufs:
                    out_bufs.append(kvt)
        if sub < 2:
            return
        def prev_of(u):
            if g == 0:
                return (kU, u - 1) if u > 0 else (kprev, 0)
            if g == 1:
                return (kU, u - 4) if u >= 4 else (kprev, u)
            return (kprev, u)

        ebi = g * 4 + hp
        sbks = ((pb[4], pb[5]), (pb[0], pb[1]))
        obs = (pb[6], pb[2])

        def scores(up):
            for hh in range(2):
                lo = hh * 64
                sbk = sbks[up % 2][hh]
                for uu in range(2):
                    u = up * 2 + uu
                    kpb, ku = prev_of(u)
                    mm(sbk[:, uu * 256:uu * 256 + 128], kU[lo:lo + 64, u, :], qU[lo:lo + 64, u, :], True, True, [kU, qU], [sbk])
                    mm(sbk[:, uu * 256 + 128:uu * 256 + 256], kpb[lo:lo + 64, ku, :], qU[lo:lo + 64, u, :], True, True, [kpb, qU], [sbk])

        def softmax(up):
            for hh in range(2):
                e_ = esb[up % 2][hh]
                sbk = sbks[up % 2][hh]
                P.act(lambda e, e_=e_, sbk=sbk: e.activation(out=e_[:], in_=sbk[:], func=AF.Exp, scale=0.125), r=[sbk], w=[e_])
                for uu in range(2):
                    p_ = pTb[up % 2][hh][uu]
                    fn = lambda e, e_=e_, p_=p_, hh=hh, uu=uu: e.tensor_tensor(out=p_[:], in0=e_[:, uu * 256:(uu + 1) * 256], in1=eb[:, ebi, hh * 256:(hh + 1) * 256], op=ALU.mult)
                    if uu == 0 or os.environ.get("KPOOL", "0") == "0":
                        P.dve(fn, r=[e_, eb], w=[p_])
                    else:
                        P.pool(fn, r=[e_, eb], w=[p_])

        def pv(up):
            for uu in range(2):
                u = up * 2 + uu
                ts = unit_tokens(g, u)
                kpb, ku = prev_of(u)
                vpb = vU if kpb is kU else vprev
                ob = obs[uu]
                for hh in range(2):
                    p_ = pTb[up % 2][hh][uu]
                    mm(ob[0:65, hh * 128:(hh + 1) * 128], vU[:, u, hh, 0:65], p_[:, 0:128], True, False, [vU, p_], [ob])
                    mm(ob[0:65, hh * 128:(hh + 1) * 128], vpb[:, ku, hh, 0:65], p_[:, 128:256], False, True, [vpb, p_], [ob])
                accv = acc[:, :, ts]
                src = ob[0:65, 0:256].rearrange("p (h q) -> p h q", h=2)
                if g == 0:
                    P.dve(lambda e, accv=accv, src=src: e.tensor_copy(out=accv, in_=src), r=[ob], w=[acc])
                else:
                    P.dve(lambda e, accv=accv, src=src: e.tensor_tensor(out=accv, in0=accv, in1=src, op=ALU.add), r=[ob, acc], w=[acc])

        scores(0)
        softmax(0)
        for up in range(8):
            if up + 1 < 8:
                scores(up + 1)
                softmax(up + 1)
            pv(up)
        if st < nst - 1:
            u0 = 16 - nu
            if g == 2:
                P.dve(lambda e: e.tensor_copy(out=kprev[:].rearrange("p u i -> p (u i)"), in_=kU[:].rearrange("p u i -> p (u i)")), r=[kU], w=[kprev])
                P.dma('sp', htk, kprev[:].rearrange("p u i -> p (u i)"), r=[kprev], w=[hb], sem_on=hb)
                if st == 0:
                    P.pool(lambda e: e.memset(kprev[:].rearrange("p u i -> p (u i)"), 0.0), w=[kprev])
            else:
                P.dma('sp', htk, kU[:, u0:16, :].rearrange("p u i -> p (u i)"), r=[kU], w=[hb], sem_on=hb)
            P.dma('sp', htv, vU[:, u0:16].rearrange("p u h e -> p (u h e)"), r=[vU], w=[hb], sem_on=hb)

    def att_final(hp):
        wg = load_slice(9)
        for hh in range(2):
            h = hp * 2 + hh
            for n in range(4):
                tsl = slice(n * 512, (n + 1) * 512)
                rb = pb[2]
                gbk = pb[3]
                mm(rb[0:64, :], sel[0:65, :], acc[0:65, hh, tsl], True, True, [sel, acc], [rb])
                proj_fm(gbk, wg, h * 64, 64, tsl)
                P.act(lambda e, gbk=gbk: e.activation(out=gsA[:], in_=gbk[0:64, :], func=AF.Silu), r=[gbk], w=[gsA])
                P.dve(lambda e, rb=rb: e.reciprocal(out=rcpA[0:64, :], in_=rb[0:64, :]), r=[rb], w=[rcpA])
                P.dve(lambda e, hh=hh, tsl=tsl: e.tensor_tensor(out=tmpA[:], in0=acc[0:64, hh, tsl], in1=rcpA[0:64, :], op=ALU.mult), r=[acc, rcpA], w=[tmpA])
                dst = attT[0:64, hp, tsl] if hh == 0 else att1[:, tsl]
                dbuf = attT if hh == 0 else att1
                P.dve(lambda e, dst=dst: e.tensor_tensor(out=dst, in0=tmpA[:], in1=gsA[:], op=ALU.mult), r=[tmpA, gsA], w=[dbuf])
            if hh == 1:
                P.dma('sp', attT[64:128, hp, :], att1[:], r=[att1], w=[attT], sem_on=att1)

    def hgrn_sub(st, j):
        c0 = j * 512
        tsl = slice(c0, c0 + 512)
        wi = [load_slice(14), load_slice(15)]
        for tt in range(4):
            for half in range(2):
                bank = pb[half]
                for c in range(8):
                    mm(bank[:], hT[:, c, c0 + tt * 128:c0 + (tt + 1) * 128], wi[half][:, c, :], c == 0, c == 7, [hT, wi[half]], [bank])
                o_ap = vI[:, tt, half * 512:(half + 1) * 512]
                if half == 0:
                    P.act(lambda e, o_ap=o_ap, bank=bank: e.copy(out=o_ap, in_=bank[:]), r=[bank], w=[vI])
                else:
                    P.dve(lambda e, o_ap=o_ap, bank=bank: e.tensor_copy(out=o_ap, in_=bank[:]), r=[bank], w=[vI])
        for h4 in range(2):
            wq_ = load_slice(10 + h4)
            wf_ = load_slice(12 + h4)
            wg_ = load_slice(16 + h4)
            for hq_ in range(4):
                h = h4 * 4 + hq_
                off = hq_ * 128
                bF = pb[0]
                proj_fm(bF, wf_, off, 128, tsl)
                P.act(lambda e, bF=bF: e.activation(out=sig[:], in_=bF[:], func=AF.Sigmoid), r=[bF], w=[sig])
                bQ = pb[1]
                proj_fm(bQ, wq_, off, 128, tsl)
                P.act(lambda e, bQ=bQ: e.activation(out=hq[:], in_=bQ[:], func=AF.Silu), r=[bQ], w=[hq])
                bG = pb[2]
                proj_fm(bG, wg_, off, 128, tsl)
                P.act(lambda e, bG=bG: e.activation(out=gsH[:], in_=bG[:], func=AF.Silu), r=[bG], w=[gsH])
                P.act(lambda e, h=h: e.activation(out=logf[:], in_=sig[:], func=AF.Ln, bias=lbc[:, h:h + 1], scale=lbc[:, 8 + h:9 + h]), r=[sig, lbc], w=[logf])
                P.dve(lambda e, h=h: e.tensor_scalar(out=hk[:], in0=sig[:], scalar1=lbc[:, 16 + h:17 + h], scalar2=lbc[:, 8 + h:9 + h], op0=ALU.mult, op1=ALU.add), r=[sig, lbc], w=[hk])
                P.dve(lambda e: e.tensor_tensor_scan(out=bcs[:], data0=rsm[:], data1=logf[:], initial=0.0, op0=ALU.mult, op1=ALU.add), r=[rsm, logf], w=[bcs])
                P.act(lambda e: e.activation(out=Ep[:], in_=bcs[:], func=AF.Exp), r=[bcs], w=[Ep])
                P.act(lambda e: e.activation(out=En[:], in_=bcs[:], func=AF.Exp, scale=-1.0), r=[bcs], w=[En])
                P.dve(lambda e: e.scalar_tensor_tensor(out=qtl[:], in0=hq[:], scalar=float(128 ** -0.5), in1=Ep[:], op0=ALU.mult, op1=ALU.mult), r=[hq, Ep], w=[qtl])
                P.dve(lambda e: e.tensor_tensor(out=kt32[:], in0=hk[:], in1=En[:], op=ALU.mult), r=[hk, En], w=[kt32])
                P.dve(lambda e: e.tensor_copy(out=ktl[:], in_=kt32[:]), r=[kt32], w=[ktl])
                for cc in range(8):
                    P.dve(lambda e, cc=cc: e.tensor_scalar(out=kend[:, cc * 64:(cc + 1) * 64], in0=kt32[:, cc * 64:(cc + 1) * 64], scalar1=Ep[:, cc * 64 + 63:cc * 64 + 64], scalar2=None, op0=ALU.mult), r=[kt32, Ep], w=[kend])
                for tt in range(4):
                    P.pe(lambda e, tt=tt: e.transpose(out=pbf[:, tt * 128:(tt + 1) * 128], in_=kend[:, tt * 128:(tt + 1) * 128], identity=ident[:]), r=[kend, ident], w=[pbf])
                P.act(lambda e: e.copy(out=kendT[:].rearrange("p a b -> p (a b)"), in_=pbf[:, 0:512]), r=[pbf], w=[kendT])
                bOs = (pb[3], pb[2])
                for cc in range(8):
                    bO = bOs[cc % 2]
                    osl = slice((cc // 2) * 64, (cc // 2) * 64 + 64)
                    tt = cc // 2
                    lo = (cc % 2) * 64
                    csl = slice(cc * 64, (cc + 1) * 64)
                    bA = pb[4]
                    aTm = aTms[cc % 2]
                    mm(bA[lo:lo + 64, 0:64], ktl[:, csl], qtl[:, csl], True, True, [ktl, qtl], [bA])
                    bS = pb[5 + cc % 2]
                    mm(bS[:, 0:128], kendT[lo:lo + 64, tt, :], vI[lo:lo + 64, tt, h * 128:(h + 1) * 128], True, True, [kendT, vI], [bS])
                    P.dve(lambda e, lo=lo, bA=bA, aTm=aTm: e.tensor_tensor(out=aTm[lo:lo + 64, :], in0=bA[lo:lo + 64, 0:64], in1=hmask[lo:lo + 64, :], op=ALU.mult), r=[bA, hmask], w=[aTm])
                    if cc == 0:
                        mm(bO[:, osl], Sbf[:, h, :], qtl[:, csl], True, False, [Sbf, qtl], [bO])
                    else:
                        Sp = Sxs[(cc - 1) % 2]
                        mm(bO[:, osl], Sp[:], qtl[:, csl], True, False, [Sp, qtl], [bO])
                    mm(bO[:, osl], vI[lo:lo + 64, tt, h * 128:(h + 1) * 128], aTm[lo:lo + 64, :], False, True, [vI, aTm], [bO])
                    P.dve(lambda e, h=h, cc=cc, bS=bS: e.scalar_tensor_tensor(out=S32[:, h, :], in0=S32[:, h, :], scalar=Ep[:, cc * 64 + 63:cc * 64 + 64], in1=bS[:, 0:128], op0=ALU.mult, op1=ALU.add), r=[S32, Ep, bS], w=[S32])
                    if cc < 7:
                        Sn_ = Sxs[cc % 2]
                        P.dve(lambda e, h=h, Sn_=Sn_: e.tensor_copy(out=Sn_[:], in_=S32[:, h, :]), r=[S32], w=[Sn_])
                    else:
                        P.dve(lambda e, h=h: e.tensor_copy(out=Sbf[:, h, :], in_=S32[:, h, :]), r=[S32], w=[Sbf])
                for par in range(2):
                    bO = bOs[par]
                    o2v = o2[:].rearrange("p (a b t) -> p a b t", b=2, t=64)[:, :, par, :]
                    o32v = o32[:].rearrange("p (a b t) -> p a b t", b=2, t=64)[:, :, par, :]
                    srcv = bO[:, 0:256].rearrange("p (a t) -> p a t", t=64)
                    P.act(lambda e, o2v=o2v, srcv=srcv: e.activation(out=o2v, in_=srcv, func=AF.Square), r=[bO], w=[o2])
                    P.dve(lambda e, o32v=o32v, srcv=srcv: e.tensor_copy(out=o32v, in_=srcv), r=[bO], w=[o32])
                bN = pb[0]
                mm(bN[:], ones_f[:], o2[:], True, True, [ones_f, o2], [bN])
                P.act(lambda e, bN=bN: e.activation(out=rs_h[:], in_=bN[:], func=AF.Ln, bias=epsc[:, 0:1], scale=1.0 / 128), r=[bN, epsc], w=[rs_h])
                P.act(lambda e: e.activation(out=rs_h[:], in_=rs_h[:], func=AF.Exp, scale=-0.5), r=[rs_h], w=[rs_h])
                P.dve(lambda e: e.scalar_tensor_tensor(out=o32[:], in0=o32[:], scalar=hnw[:, 0:1], in1=rs_h[:], op0=ALU.mult, op1=ALU.mult), r=[o32, hnw, rs_h], w=[o32])
                P.dve(lambda e, h=h: e.tensor_tensor(out=hoT[:, h, :], in0=o32[:], in1=gsH[:], op=ALU.mult), r=[o32, gsH], w=[hoT])

    def out_sub(st, j):
        c0 = j * 512
        tsl = slice(c0, c0 + 512)
        wma = [None, None]
        wmb = [None, None]
        for jj in range(8):
            if jj % 4 == 0:
                wma[jj // 4] = load_slice(18 + jj // 4)
                wmb[jj // 4] = load_slice(20 + jj // 4)
            wa_ = wapj[jj % 2]
            wh_ = whpj[jj % 2]
            P.dma('sp', wa_[:], wapb_t[jj], r=[wapb], w=[wa_], sem_on=wa_)
            P.dma('sp', wh_[:], whpb_t[jj], r=[whpb], w=[wh_], sem_on=wh_)
            off = (jj % 4) * 128
            bA = pb[0]
            proj_fm(bA, wma[jj // 4], off, 128, tsl)
            P.act(lambda e, bA=bA: e.activation(out=ga[:], in_=bA[:], func=AF.Sigmoid), r=[bA], w=[ga])
            bB = pb[1]
            proj_fm(bB, wmb[jj // 4], off, 128, tsl)
            P.act(lambda e, bB=bB: e.activation(out=gbt[:], in_=bB[:], func=AF.Sigmoid), r=[bB], w=[gbt])
            b1 = pb[2]
            for hp in range(4):
                mm(b1[:], wa_[:, hp, :], attT[:, hp, tsl], hp == 0, hp == 3, [wa_, attT], [b1])
            b2 = pb[3]
            for h in range(8):
                mm(b2[:], wh_[:, h, :], hoT[:, h, :], h == 0, h == 7, [wh_, hoT], [b2])
            P.dve(lambda e, b1=b1: e.tensor_tensor(out=m1[:], in0=ga[:], in1=b1[:], op=ALU.mult), r=[ga, b1], w=[m1])
            P.dve(lambda e, b2=b2: e.tensor_tensor(out=m2[:], in0=gbt[:], in1=b2[:], op=ALU.mult), r=[gbt, b2], w=[m2])
            P.dve(lambda e, jj=jj: e.tensor_tensor(out=mergedT[:, jj, :], in0=m1[:], in1=m2[:], op=ALU.add), r=[m1, m2], w=[mergedT])
        wo = []
        for s in range(2):
            b = L['wsl'][L['wsl_i'][0] % 3]
            L['wsl_i'][0] += 1
            P.dma('sp', b[:], woutb_t[s], r=[woutb], w=[b], sem_on=b)
            wo.append(b)
        def ofront(tt):
            t0 = st * ST + c0 + tt * 128
            xb = xt[tt % 2]
            P.dma('sp', xb[:], x_d[t0:t0 + 128, :], w=[xb])
            for half in range(2):
                bank = pb[4 + half]
                for c in range(8):
                    mm(bank[:], mergedT[:, c, tt * 128:(tt + 1) * 128], wo[half][:, c, :], c == 0, c == 7, [mergedT, wo[half]], [bank])
                P.dve(lambda e, half=half, bank=bank, xb=xb: e.tensor_tensor(out=xb[:, half * 512:(half + 1) * 512], in0=bank[:], in1=xb[:, half * 512:(half + 1) * 512], op=ALU.add), r=[bank, xb], w=[xb])

        def oback(tt):
            t0 = st * ST + c0 + tt * 128
            xb = xt[tt % 2]
            yb = yo[tt % 2]
            k = tt % 2
            P.act(lambda e, yb=yb, xb=xb: e.activation(out=yb[:], in_=xb[:], func=AF.Square), r=[xb], w=[yb])
            P.dve(lambda e, yb=yb, k=k: e.tensor_reduce(out=ssf[:, k:k + 1], in_=yb[:], axis=AX.X, op=ALU.add), r=[yb], w=[ssf])
            P.act(lambda e, k=k: e.activation(out=rsf[:, 2 + k:3 + k], in_=ssf[:, k:k + 1], func=AF.Ln, bias=epsc[:, 0:1], scale=1.0 / D), r=[ssf, epsc], w=[rsf])
            P.act(lambda e, k=k: e.activation(out=rsf[:, k:k + 1], in_=rsf[:, 2 + k:3 + k], func=AF.Exp, scale=-0.5), r=[rsf], w=[rsf])
            P.dve(lambda e, yb=yb, xb=xb, k=k: e.scalar_tensor_tensor(out=yb[:], in0=xb[:], scalar=rsf[:, k:k + 1], in1=fnw[:], op0=ALU.mult, op1=ALU.mult), r=[xb, rsf, fnw], w=[yb])
            P.dma('sp', y_d[t0:t0 + 128, :], yb[:], r=[yb], sem_on=yb)
            if yb not in out_bufs:
                out_bufs.append(yb)

        ofront(0)
        for tt in range(4):
            if tt + 1 < 4:
                ofront(tt + 1)
            oback(tt)

    import os
    stage0 = int(os.environ.get("KSTAGE", "9"))
    stage1 = int(os.environ.get("KSTAGE1", "9"))
    for st in range(nst):
        stage = stage0 if st == 0 else stage1
        if st > 0:
            P.barrier()
        if stage >= 1:
            phase_n(st)
        if stage >= 2:
            P.dve(lambda e: e.memset(acc[0:1, 0, 0:1], 0.0), w=tmps + [acc])
            for hp in range(4):
                for g in range(3):
                    att_gh(st, g, hp)
                if stage >= 3:
                    att_final(hp)
        if stage >= 4:
            P.barrier()
            P.dve(lambda e: e.memset(tmps[0][0:1, 0:1], 0.0), w=[acc] + tmps)
            for j in range(4):
                hgrn_sub(st, j)
                if stage >= 5:
                    out_sub(st, j)
    P.dma('sp', stp_d.rearrange("h k v -> k h v"), S32[:], r=[S32], sem_on=S32)
    out_bufs.append(S32)


def sample_path(P, L):
    nc = L['nc']; pb = L['pb']; pbf = L['pbf']; out_bufs = L['out_bufs']
    xs_d = L['xs_d']; c_d = L['c_d']; sh_d = L['sh_d']; ys_d = L['ys_d']; kvs_d = L['kvs_d']; sts_d = L['sts_d']
    ident = L['ident']; ident_f = L['ident_f']; fnw = L['fnw']; epsc = L['epsc']; ones_f = L['ones_f']
    load_slice = L['load_slice']; rstd_from_ss = L['rstd_from_ss']
    wapb_t = L['wapb_t']; whpb_t = L['whpb_t']; woutb_t = L['woutb_t']
    wapb = L['wapb']; whpb = L['whpb']; woutb = L['woutb']
    nwr_d = L['nwr_d']; lblr_d = L['lblr_d']; hnwr_d = L['hnwr_d']; sbias_d = L['sbias_d']
    onehot_d = L['onehot_d']; bdm_d = L['bdm_d']; en_d = L['en_d']
    A = AF
    OFF_G = 4608; OFF_Q = 5120; OFF_F = 6144; OFF_I = 7168; OFF_HG = 8192; OFF_MA = 9216; OFF_MB = 10240

    def sb(name, shape, dt=F32):
        return P.sb("s_" + name, shape, dt)

    class phase:
        def __enter__(self):
            self.outer = P.stack
            self.es = ExitStack()
            self.es.__enter__()
            P.stack = self.es
            return self

        def __exit__(self, *a):
            P.barrier()
            P.stack = self.outer
            return self.es.__exit__(*a)

    att_b = sb("att_b", [NS, 512], BF16)
    ho_b = sb("ho_b", [NS, 1024], BF16)

    nwrs = sb("nwrs", [NS, D]); P.dma('sp', nwrs[:], nwr_d[0:NS, :], w=[nwrs])
    lblr = sb("lblr", [NS, 2048]); P.dma('sp', lblr[:], lblr_d, w=[lblr])
    hnwr = sb("hnwr", [NS, 1024]); P.dma('sp', hnwr[:], hnwr_d, w=[hnwr])
    sbias = sb("sbias", [128, 24]); P.dma('sp', sbias[:], sbias_d, w=[sbias])
    onehot = sb("onehot", [NS, NS * 128]); P.dma('sp', onehot[:], onehot_d, w=[onehot])
    bdm = sb("bdm", [8, 520]); P.dma('sp', bdm[:], bdm_d, w=[bdm])
    en = sb("en", [8, NS * NS]); P.dma('sp', en[:], en_d, w=[en])
    xs = sb("xs", [NS, D]); P.dma('sp', xs[:], xs_d, w=[xs])
    sq = sb("sq", [NS, D])
    st1 = sb("st1", [NS, 16]); st2 = sb("st2", [NS, 16])
    xn = sb("xn", [NS, D], BF16)
    hsT = sb("hsT", [128, 8, NS], BF16)
    projs = sb("projs", [NS, 11264])

    def mm(bank_ap, lhsT, rhs, first, last, r, w):
        P.pe(lambda e: e.matmul(bank_ap, lhsT=lhsT, rhs=rhs, start=first, stop=last), r=r, w=w)

    def rms_rows(dst_bf, src, wrow):
        P.act(lambda e: e.activation(out=sq[:], in_=src[:], func=A.Square), r=[src], w=[sq])
        P.dve(lambda e: e.tensor_reduce(out=st1[:, 0:1], in_=sq[:], axis=AX.X, op=ALU.add), r=[sq], w=[st1])
        P.act(lambda e: e.activation(out=st2[:, 0:1], in_=st1[:, 0:1], func=A.Ln, bias=epsc[0:NS, 0:1], scale=1.0 / D), r=[st1, epsc], w=[st2])
        P.act(lambda e: e.activation(out=st2[:, 0:1], in_=st2[:, 0:1], func=A.Exp, scale=-0.5), r=[st2], w=[st2])
        P.dve(lambda e: e.scalar_tensor_tensor(out=dst_bf[:], in0=src[:], scalar=st2[:, 0:1], in1=wrow[:], op0=ALU.mult, op1=ALU.mult), r=[src, st2, wrow], w=[dst_bf])

    def to_fm_bf(dstT, src_bf, nch):
        for c in range(nch):
            P.pe(lambda e, c=c: e.transpose(out=pbf[:, c * NS:(c + 1) * NS], in_=src_bf[0:NS, c * 128:(c + 1) * 128], identity=ident[0:NS, 0:NS]), r=[src_bf, ident], w=[pbf])
        P.act(lambda e: e.copy(out=dstT[:].rearrange("p c n -> p (c n)"), in_=pbf[:, 0:nch * NS]), r=[pbf], w=[dstT])

    rms_rows(xn, xs, nwrs)
    to_fm_bf(hsT, xn, 8)
    for s in range(NSL):
        wsb = load_slice(s)
        bank = pb[s % 2]
        for c in range(8):
            mm(bank[0:NS, :], hsT[:, c, :], wsb[:, c, :], c == 0, c == 7, [hsT, wsb], [bank])
        if s % 2 == 0:
            P.act(lambda e, s=s, bank=bank: e.copy(out=projs[:, s * 512:(s + 1) * 512], in_=bank[0:NS, :]), r=[bank], w=[projs])
        else:
            P.dve(lambda e, s=s, bank=bank: e.tensor_copy(out=projs[:, s * 512:(s + 1) * 512], in_=bank[0:NS, :]), r=[bank], w=[projs])
    for g in range(3):
        P.dma('sp', kvs_d[g], projs[:, g * 1536 + 512:g * 1536 + 1536], r=[projs], sem_on=projs)
    out_bufs.append(projs)

    phB = phase(); phB.__enter__()
    accn = sb("accn", [NS, 512]); accd = sb("accd", [NS, 8])
    tq = sb("tq", [NS, 512]); ts_ = sb("ts", [NS, 8]); tp = sb("tp", [NS, 8])
    for g in range(3):
        b0 = g * 1536
        P.dve(lambda e, b0=b0: e.tensor_tensor(out=tq[:], in0=projs[:, b0:b0 + 512], in1=projs[:, b0 + 512:b0 + 1024], op=ALU.mult), r=[projs], w=[tq])
        P.dve(lambda e: e.tensor_reduce(out=ts_[:], in_=tq[:].rearrange("p (h e) -> p h e", e=64), axis=AX.X, op=ALU.add), r=[tq], w=[ts_])
        P.act(lambda e: e.activation(out=tp[:], in_=ts_[:], func=A.Exp, scale=0.125), r=[ts_], w=[tp])
        for h in range(8):
            hs = slice(h * 64, (h + 1) * 64)
            if g == 0:
                P.dve(lambda e, h=h, hs=hs, b0=b0: e.tensor_scalar(out=accn[:, hs], in0=projs[:, b0 + 1024 + h * 64:b0 + 1024 + (h + 1) * 64], scalar1=tp[:, h:h + 1], scalar2=None, op0=ALU.mult), r=[projs, tp], w=[accn])
            else:
                P.dve(lambda e, h=h, hs=hs, b0=b0: e.scalar_tensor_tensor(out=accn[:, hs], in0=projs[:, b0 + 1024 + h * 64:b0 + 1024 + (h + 1) * 64], scalar=tp[:, h:h + 1], in1=accn[:, hs], op0=ALU.mult, op1=ALU.add), r=[projs, tp, accn], w=[accn])
        if g == 0:
            P.dve(lambda e: e.tensor_copy(out=accd[:], in_=tp[:]), r=[tp], w=[accd])
        else:
            P.dve(lambda e: e.tensor_tensor(out=accd[:], in0=accd[:], in1=tp[:], op=ALU.add), r=[accd, tp], w=[accd])
    ck = [sb("ck%d" % i, [128, 1024]) for i in range(3)]
    prod = sb("prod", [128, 512])
    s8 = sb("s8", [128, 8]); p8s = [sb("p8_%d" % i, [128, 8]) for i in range(2)]
    msk = sb("msk", [8, 520])
    items = [(g, n) for g in range(3) for n in range(NS)]

    def stage1(it):
        g, n = items[it]
        win, dil = GROUPS[g]
        b0 = g * 1536
        c_ = ck[it % 3]
        p8 = p8s[it % 2]
        P.dma('sp', c_[:], c_d[g][n, 0:win:dil, :], w=[c_])
        qb = pb[2]
        mm(qb[:], onehot[:, n * 128:(n + 1) * 128], projs[:, b0:b0 + 512], True, True, [onehot, projs], [qb])
        P.dve(lambda e, c_=c_, qb=qb: e.tensor_tensor(out=prod[:], in0=c_[:, 0:512], in1=qb[:], op=ALU.mult), r=[c_, qb], w=[prod])
        P.dve(lambda e: e.tensor_reduce(out=s8[:], in_=prod[:].rearrange("p (h e) -> p h e", e=64), axis=AX.X, op=ALU.add), r=[prod], w=[s8])
        P.dve(lambda e, g=g: e.scalar_tensor_tensor(out=s8[:], in0=s8[:], scalar=0.125, in1=sbias[:, g * 8:(g + 1) * 8], op0=ALU.mult, op1=ALU.add), r=[s8, sbias], w=[s8])
        P.act(lambda e, p8=p8: e.activation(out=p8[:], in_=s8[:], func=A.Exp), r=[s8], w=[p8])

    def stage2(it):
        g, n = items[it]
        c_ = ck[it % 3]
        p8 = p8s[it % 2]
        nb = pb[3]; db = pb[4]
        mm(nb[0:8, :], p8[:], c_[:, 512:1024], True, True, [p8, c_], [nb])
        mm(db[0:8, 0:8], p8[:], ones_f[:, 0:8], True, True, [p8, ones_f], [db])
        P.dve(lambda e, nb=nb: e.tensor_tensor(out=msk[:, 0:512], in0=nb[0:8, :], in1=bdm[:, 0:512], op=ALU.mult), r=[nb, bdm], w=[msk])
        P.dve(lambda e, db=db: e.tensor_tensor(out=msk[:, 512:520], in0=db[0:8, 0:8], in1=bdm[:, 512:520], op=ALU.mult), r=[db, bdm], w=[msk])
        rb = pb[5]; rd = pb[6]
        mm(rb[0:NS, :], en[:, n * NS:(n + 1) * NS], msk[:, 0:512], True, True, [en, msk], [rb])
        mm(rd[0:NS, 0:8], en[:, n * NS:(n + 1) * NS], msk[:, 512:520], True, True, [en, msk], [rd])
        P.dve(lambda e, rb=rb: e.tensor_tensor(out=accn[:], in0=accn[:], in1=rb[0:NS, :], op=ALU.add), r=[accn, rb], w=[accn])
        P.dve(lambda e, rd=rd: e.tensor_tensor(out=accd[:], in0=accd[:], in1=rd[0:NS, 0:8], op=ALU.add), r=[accd, rd], w=[accd])

    stage1(0)
    for it in range(len(items)):
        if it + 1 < len(items):
            stage1(it + 1)
        stage2(it)
    att_s = sb("att_s", [NS, 512])
    gsa = sb("gsa", [NS, 512])
    P.dve(lambda e: e.reciprocal(out=accd[:], in_=accd[:]), r=[accd], w=[accd])
    for h in range(8):
        hs = slice(h * 64, (h + 1) * 64)
        P.dve(lambda e, h=h, hs=hs: e.tensor_scalar(out=att_s[:, hs], in0=accn[:, hs], scalar1=accd[:, h:h + 1], scalar2=None, op0=ALU.mult), r=[accn, accd], w=[att_s])
    P.act(lambda e: e.activation(out=gsa[:], in_=projs[:, OFF_G:OFF_G + 512], func=A.Silu), r=[projs], w=[gsa])
    P.dve(lambda e: e.tensor_tensor(out=att_b[:], in0=att_s[:], in1=gsa[:], op=ALU.mult), r=[att_s, gsa], w=[att_b])
    phB.__exit__(None, None, None)
    phC = phase(); phC.__enter__()

    lb = sb("lb", [NS, 1024]); oml = sb("oml", [NS, 1024])
    f_t = sb("f_t", [NS, 1024]); hk_t = sb("hk_t", [NS, 1024]); hq_t = sb("hq_t", [NS, 1024])
    P.dve(lambda e: e.tensor_tensor(out=lb[:], in0=lblr[:, 0:1024], in1=lblr[:, 1024:2048], op=ALU.subtract), r=[lblr], w=[lb])
    P.act(lambda e: e.activation(out=lb[:], in_=lb[:], func=A.Sigmoid), r=[lb], w=[lb])
    P.dve(lambda e: e.tensor_scalar(out=oml[:], in0=lb[:], scalar1=-1.0, scalar2=1.0, op0=ALU.mult, op1=ALU.add), r=[lb], w=[oml])
    P.act(lambda e: e.activation(out=f_t[:], in_=projs[:, OFF_F:OFF_F + 1024], func=A.Sigmoid), r=[projs], w=[f_t])
    P.dve(lambda e: e.tensor_tensor(out=f_t[:], in0=f_t[:], in1=oml[:], op=ALU.mult), r=[f_t, oml], w=[f_t])
    P.dve(lambda e: e.tensor_tensor(out=hk_t[:], in0=oml[:], in1=f_t[:], op=ALU.subtract), r=[oml, f_t], w=[hk_t])
    P.dve(lambda e: e.tensor_tensor(out=f_t[:], in0=f_t[:], in1=lb[:], op=ALU.add), r=[f_t, lb], w=[f_t])
    P.act(lambda e: e.activation(out=hq_t[:], in_=projs[:, OFF_Q:OFF_Q + 1024], func=A.Silu), r=[projs], w=[hq_t])
    P.dve(lambda e: e.tensor_scalar(out=hq_t[:], in0=hq_t[:], scalar1=float(128 ** -0.5), scalar2=None, op0=ALU.mult), r=[hq_t], w=[hq_t])
    fT = sb("fT", [128, 8, NS]); hkT = sb("hkT", [128, 8, NS]); hqT = sb("hqT", [128, 8, NS])
    for src, dst, bank in ((f_t, fT, pb[0]), (hk_t, hkT, pb[1]), (hq_t, hqT, pb[2])):
        for h in range(8):
            P.pe(lambda e, h=h, src=src, bank=bank: e.transpose(out=bank[:, h * NS:(h + 1) * NS], in_=src[0:NS, h * 128:(h + 1) * 128], identity=ident_f[0:NS, 0:NS]), r=[src, ident_f], w=[bank])
        P.dve(lambda e, dst=dst, bank=bank: e.tensor_copy(out=dst[:].rearrange("p h n -> p (h n)"), in_=bank[:, 0:8 * NS]), r=[bank], w=[dst])
    Qm = sb("Qm", [128, 8, NS, NS])
    P.pool(lambda e: e.memset(Qm[:].rearrange("p h a b -> p (h a b)"), 0.0), w=[Qm])
    for n in range(NS):
        P.dve(lambda e, n=n: e.tensor_copy(out=Qm[:, :, n, n], in_=hqT[:, :, n]), r=[hqT], w=[Qm])
    oacc = sb("oacc", [NS, 1024])
    P.pool(lambda e: e.memset(oacc[:], 0.0), w=[oacc])
    S0 = [sb("S0_%d" % i, [128, 8, 128]) for i in range(2)]
    Sn = [sb("Sn_%d" % i, [128, 8, 128]) for i in range(2)]
    tmpk = sb("tmpk", [128, 128])
    def hstage1(n):
        s0 = S0[n % 2]; sn = Sn[n % 2]
        P.dma('sp', s0[:], sh_d[n].rearrange("h k v -> k h v"), w=[s0])
        ib = (pb[3], pb[4])
        for half in range(2):
            mm(ib[half][:], onehot[:, n * 128:(n + 1) * 128], projs[:, OFF_I + half * 512:OFF_I + (half + 1) * 512], True, True, [onehot, projs], [ib[half]])
        for h in range(8):
            src = ib[h // 4][:, (h % 4) * 128:(h % 4 + 1) * 128]
            P.dve(lambda e, src=src, h=h, n=n: e.tensor_scalar(out=tmpk[:], in0=src, scalar1=hkT[:, h, n:n + 1], scalar2=None, op0=ALU.mult), r=[ib[h // 4], hkT], w=[tmpk])
            P.dve(lambda e, h=h, n=n, s0=s0, sn=sn: e.scalar_tensor_tensor(out=sn[:, h, :], in0=s0[:, h, :], scalar=fT[:, h, n:n + 1], in1=tmpk[:], op0=ALU.mult, op1=ALU.add), r=[s0, fT, tmpk], w=[sn])
        P.dma('sp', sts_d[n].rearrange("h k v -> k h v"), sn[:], r=[sn], sem_on=sn)
        if sn not in out_bufs:
            out_bufs.append(sn)

    def hstage2(n):
        sn = Sn[n % 2]
        ob = (pb[5], pb[6])
        for h in range(8):
            mm(ob[h // 4][0:NS, (h % 4) * 128:(h % 4 + 1) * 128], Qm[:, h, n, :], sn[:, h, :], True, True, [Qm, sn], [ob[h // 4]])
        for half in range(2):
            P.dve(lambda e, half=half, ob=ob: e.tensor_tensor(out=oacc[:, half * 512:(half + 1) * 512], in0=oacc[:, half * 512:(half + 1) * 512], in1=ob[half][0:NS, :], op=ALU.add), r=[oacc, ob[half]], w=[oacc])

    hstage1(0)
    for n in range(NS):
        if n + 1 < NS:
            hstage1(n + 1)
        hstage2(n)
    o2s = sb("o2s", [NS, 1024]); ssh = sb("ssh", [NS, 8]); rsh = sb("rsh", [NS, 8])
    gsh = sb("gsh", [NS, 1024])
    P.act(lambda e: e.activation(out=o2s[:], in_=oacc[:], func=A.Square), r=[oacc], w=[o2s])
    P.dve(lambda e: e.tensor_reduce(out=ssh[:], in_=o2s[:].rearrange("p (h v) -> p h v", v=128), axis=AX.X, op=ALU.add), r=[o2s], w=[ssh])
    P.act(lambda e: e.activation(out=rsh[:], in_=ssh[:], func=A.Ln, bias=epsc[0:NS, 0:1], scale=1.0 / 128), r=[ssh, epsc], w=[rsh])
    P.act(lambda e: e.activation(out=rsh[:], in_=rsh[:], func=A.Exp, scale=-0.5), r=[rsh], w=[rsh])
    P.act(lambda e: e.activation(out=gsh[:], in_=projs[:, OFF_HG:OFF_HG + 1024], func=A.Silu), r=[projs], w=[gsh])
    for h in range(8):
        hs = slice(h * 128, (h + 1) * 128)
        P.dve(lambda e, h=h, hs=hs: e.scalar_tensor_tensor(out=o2s[:, hs], in0=oacc[:, hs], scalar=rsh[:, h:h + 1], in1=hnwr[:, hs], op0=ALU.mult, op1=ALU.mult), r=[oacc, rsh, hnwr], w=[o2s])
    P.dve(lambda e: e.tensor_tensor(out=ho_b[:], in0=o2s[:], in1=gsh[:], op=ALU.mult), r=[o2s, gsh], w=[ho_b])
    phC.__exit__(None, None, None)

    attTs = sb("attTs", [128, 4, NS], BF16); hoTs = sb("hoTs", [128, 8, NS], BF16)
    to_fm_bf(attTs, att_b, 4)
    to_fm_bf(hoTs, ho_b, 8)
    gas = sb("gas", [NS, 1024]); gbs = sb("gbs", [NS, 1024])
    P.act(lambda e: e.activation(out=gas[:], in_=projs[:, OFF_MA:OFF_MA + 1024], func=A.Sigmoid), r=[projs], w=[gas])
    P.act(lambda e: e.activation(out=gbs[:], in_=projs[:, OFF_MB:OFF_MB + 1024], func=A.Sigmoid), r=[projs], w=[gbs])
    mg = sb("mg", [NS, 1024]); mg2 = sb("mg2", [NS, 1024]); mgb = sb("mgb", [NS, 1024], BF16)
    wa_s = [sb("wa_s%d" % i, [128, 4, 128], BF16) for i in range(2)]
    wh_s = [sb("wh_s%d" % i, [128, 8, 128], BF16) for i in range(2)]
    for jj in range(8):
        wa_ = wa_s[jj % 2]; wh_ = wh_s[jj % 2]
        P.dma('sp', wa_[:], wapb_t[jj], r=[wapb], w=[wa_], sem_on=wa_)
        P.dma('sp', wh_[:], whpb_t[jj], r=[whpb], w=[wh_], sem_on=wh_)
        js = slice(jj * 128, (jj + 1) * 128)
        b1 = pb[0]; b2 = pb[1]
        for hp in range(4):
            mm(b1[0:NS, 0:128], attTs[:, hp, :], wa_[:, hp, :], hp == 0, hp == 3, [attTs, wa_], [b1])
        for h in range(8):
            mm(b2[0:NS, 0:128], hoTs[:, h, :], wh_[:, h, :], h == 0, h == 7, [hoTs, wh_], [b2])
        P.dve(lambda e, js=js, b1=b1: e.tensor_tensor(out=mg[:, js], in0=gas[:, js], in1=b1[0:NS, 0:128], op=ALU.mult), r=[gas, b1], w=[mg])
        P.dve(lambda e, js=js, b2=b2: e.tensor_tensor(out=mg2[:, js], in0=gbs[:, js], in1=b2[0:NS, 0:128], op=ALU.mult), r=[gbs, b2], w=[mg2])
    P.dve(lambda e: e.tensor_tensor(out=mgb[:], in0=mg[:], in1=mg2[:], op=ALU.add), r=[mg, mg2], w=[mgb])
    mgT = sb("mgT", [128, 8, NS], BF16)
    to_fm_bf(mgT, mgb, 8)
    xr_s = sb("xr_s", [NS, D]); fnws = sb("fnws", [NS, D]); ysb = sb("ysb", [NS, D])
    P.dma('sp', fnws[:], L['fnw_d'][0:NS, :], w=[fnws])
    for half in range(2):
        wsl = L['wsl'][L['wsl_i'][0] % 3]
        L['wsl_i'][0] += 1
        P.dma('sp', wsl[:], woutb_t[half], r=[woutb], w=[wsl], sem_on=wsl)
        bank = pb[2 + half]
        for c in range(8):
            mm(bank[0:NS, :], mgT[:, c, :], wsl[:, c, :], c == 0, c == 7, [mgT, wsl], [bank])
        P.dve(lambda e, half=half, bank=bank: e.tensor_tensor(out=xr_s[:, half * 512:(half + 1) * 512], in0=bank[0:NS, :], in1=xs[:, half * 512:(half + 1) * 512], op=ALU.add), r=[bank, xs], w=[xr_s])
    rms_rows(ysb, xr_s, fnws)
    P.dma('sp', ys_d, ysb[:], r=[ysb], sem_on=ysb)
    out_bufs.append(ysb)


def _consts():
    c = {}
    c["ident"] = np.eye(128, dtype=np.float32)
    n = 24
    slopes = (2.0 ** (-8.0 * np.arange(1, n + 1) / n)).astype(np.float32).reshape(3, 8)
    k = np.arange(128)[:, None].astype(np.float32)
    q = np.arange(128)[None, :].astype(np.float32)
    eb = np.zeros((128, 12, 4, 128), np.float32)
    for g, (win, d) in enumerate(GROUPS):
        for hp in range(4):
            for hh in range(2):
                s = slopes[g, hp * 2 + hh]
                cur = np.where(q >= k, np.exp(-s * d * np.maximum(q - k, 0.0)), 0.0)
                prev = np.where(k >= q, np.exp(-s * d * (q + 128 - k)), 0.0)
                eb[:, g * 4 + hp, 2 * hh, :] = cur
                eb[:, g * 4 + hp, 2 * hh + 1, :] = prev
    c["eb"] = eb.reshape(128, 12 * 512)
    p = np.arange(128)[:, None] % 64
    t = np.arange(64)[None, :]
    c["hmask"] = (t >= p).astype(np.float32)
    rs = np.ones((128, 512), np.float32)
    rs[:, ::64] = 0.0
    c["rsm"] = rs
    sel = np.zeros((128, 64), np.float32)
    sel[64, :] = 1.0
    c["sel"] = sel
    sb = np.zeros((128, 24), np.float32)
    for g, (win, d) in enumerate(GROUPS):
        for h in range(8):
            sb[:, g * 8 + h] = -slopes[g, h] * d * (128 - np.arange(128))
    c["sbias"] = sb
    oh = np.zeros((NS, NS, 128), np.float32)
    for i in range(NS):
        oh[i, i, :] = 1.0
    c["onehot"] = oh.reshape(NS, NS * 128)
    bdm = np.zeros((8, 520), np.float32)
    for h in range(8):
        bdm[h, h * 64:(h + 1) * 64] = 1.0
        bdm[h, 512 + h] = 1.0
    c["bdm"] = bdm
    en = np.zeros((8, NS, NS), np.float32)
    for i in range(NS):
        en[:, i, i] = 1.0
    c["en"] = en.reshape(8, NS * NS)
    return c


_CACHE = {}


def kernel(x_prompt, x_sample, cache_kv_w128, cache_kv_w512, cache_kv_w2048, state_hgrn,
           norm_w, w_in, w_att_proj, w_hg_proj, w_out, hg_norm_w, hg_lb_logits, final_norm_w,
           _nst=NST, _cores=8, _prompt=True, _sample=True):
    f = lambda a: np.ascontiguousarray(np.asarray(a, dtype=np.float32))
    key = (_nst, _prompt, _sample)
    if "nc" not in _CACHE or _CACHE.get("key") != key:
        _CACHE["nc"] = build_program(do_prompt=_prompt, do_sample=_sample, nst=_nst)
        _CACHE["key"] = key
    nc = _CACHE["nc"]
    cst = _consts()
    shared = dict(cst)
    shared["w_in"] = f(w_in[0])
    shared["wap"] = f(w_att_proj[0])
    shared["whp"] = f(w_hg_proj[0])
    shared["wout"] = f(w_out[0])
    shared["nw"] = f(np.asarray(norm_w[0]).reshape(8, 128).T)
    shared["nwr"] = f(np.broadcast_to(np.asarray(norm_w[0])[None, :], (128, D)))
    shared["hnw"] = f(np.asarray(hg_norm_w[0]).reshape(128, 1))
    shared["hnwr"] = f(np.broadcast_to(np.tile(np.asarray(hg_norm_w[0]), 8)[None, :], (NS, 1024)))
    lg = np.asarray(hg_lb_logits)
    shared["lbl"] = f(lg.reshape(2, 8, 128).transpose(2, 0, 1).reshape(128, 16))
    shared["lblr"] = f(np.broadcast_to(lg.reshape(1, 2048), (NS, 2048)))
    shared["fnw"] = f(np.broadcast_to(np.asarray(final_norm_w)[None, :], (128, D)))
    in_maps = []
    for i in range(_cores):
        m = dict(shared)
        m["x"] = f(x_prompt[i])
        sl = slice(NS * i, NS * (i + 1))
        m["xs"] = f(np.asarray(x_sample)[sl, 0, :])
        m["c128"] = f(np.asarray(cache_kv_w128)[0, sl].reshape(NS, 128, 1024))
        m["c512"] = f(np.asarray(cache_kv_w512)[0, sl].reshape(NS, 512, 1024))
        m["c2048"] = f(np.asarray(cache_kv_w2048)[0, sl].reshape(NS, 2048, 1024))
        m["sh"] = f(np.asarray(state_hgrn)[0, sl])
        in_maps.append(m)
    res = run_bass_kernel_spmd(nc, in_maps, core_ids=list(range(_cores)))
    R = res.results
    n = _cores
    y = np.stack([R[i]["y"] for i in range(n)], 0)
    ys = np.concatenate([R[i]["ys"] for i in range(n)], 0).reshape(n * NS, 1, D)
    kvp = [np.stack([R[i][k] for i in range(n)], 0).reshape(1, n, w, 2, 8, 64)
           for k, w in (("kv128", 128), ("kv512", 512), ("kv2048", 2048))]
    stp = np.stack([R[i]["stp"] for i in range(n)], 0).reshape(1, n, 8, 128, 128)
    kvs = [np.concatenate([R[i][k] for i in range(n)], 0).reshape(1, n * NS, 1, 2, 8, 64)
           for k in ("kvs128", "kvs512", "kvs2048")]
    sts = np.concatenate([R[i]["sts"] for i in range(n)], 0).reshape(1, n * NS, 8, 128, 128)
    return (y, ys, kvp[0], kvp[1], kvp[2], stp, kvs[0], kvs[1], kvs[2], sts)
```

```python
import numpy as np
import concourse.bass as bass
import concourse.mybir as mybir
from concourse.bass_utils import run_bass_kernel_spmd

F32 = mybir.dt.float32
BF16 = mybir.dt.bfloat16
AF = mybir.ActivationFunctionType
ALU = mybir.AluOpType
AX = mybir.AxisListType

ENGS = ['pe', 'act', 'dve', 'pool', 'sp']


class Buf:
    __slots__ = ('name', 't', 'lw', 'rd', 'dsem')

    def __init__(self, name, t):
        self.name = name
        self.t = t
        self.lw = None
        self.rd = {}
        self.dsem = None

    def __getitem__(self, idx):
        return self.t[idx]


class DSem:
    __slots__ = ('key', 'val', 'h')

    def __init__(self, key, h):
        self.key = key
        self.val = 0
        self.h = h


class Prog:
    def __init__(self, nc, stack):
        self.nc = nc
        self.stack = stack
        self.sem_stack = stack
        self.lists = {e: [] for e in ENGS}
        self.cnt = {e: 0 for e in ENGS}
        self.seen = {e: {} for e in ENGS}
        self.semh = {}
        for e in ENGS:
            self.semh[e] = stack.enter_context(nc.semaphore('s_' + e))
        self.ndsem = 0
        self.alldsem = []
        self.nbuf = 0

    def sb(self, name, shape, dtype):
        t = self.stack.enter_context(self.nc.sbuf_tensor("sb_" + name, list(shape), dtype))
        return Buf(name, t)

    def ps(self, name, shape, dtype=F32):
        t = self.stack.enter_context(self.nc.psum_tensor("ps_" + name, list(shape), dtype))
        return Buf(name, t)

    def wrap(self, name, t):
        return Buf(name, t)

    def _dsem(self, b):
        if b.dsem is None:
            h = self.sem_stack.enter_context(self.nc.semaphore('d%d' % self.ndsem))
            key = ('dma', self.ndsem)
            self.ndsem += 1
            self.semh[key] = h
            b.dsem = DSem(key, h)
            self.alldsem.append(b.dsem)
        return b.dsem

    def emit(self, eng, fn, reads=(), writes=(), dma=None):
        deps = {}

        def add(dep):
            if dep is None:
                return
            k, v = dep
            if deps.get(k, 0) < v:
                deps[k] = v

        for b in reads:
            add(b.lw)
        for b in writes:
            add(b.lw)
            for k, v in b.rd.items():
                add((k, v))
        ds = None
        if dma is not None:
            ds = self._dsem(dma)
            if ds.val:
                add((ds.key, ds.val))
        waits = []
        seen = self.seen[eng]
        for k, v in deps.items():
            if k == eng and eng == 'pe':
                continue
            if seen.get(k, 0) >= v:
                continue
            seen[k] = v
            waits.append((k, v))
        if ds is None:
            self.cnt[eng] += 1
            tok = (eng, self.cnt[eng])
        else:
            ds.val += 16
            tok = (ds.key, ds.val)
        self.lists[eng].append((waits, fn, ds))
        for b in reads:
            if b.rd.get(tok[0], 0) < tok[1]:
                b.rd[tok[0]] = tok[1]
        for b in writes:
            b.lw = tok
            b.rd = {}
        return tok

    def pe(self, fn, r=(), w=()):
        return self.emit('pe', fn, r, w)

    def act(self, fn, r=(), w=()):
        return self.emit('act', fn, r, w)

    def dve(self, fn, r=(), w=()):
        return self.emit('dve', fn, r, w)

    def pool(self, fn, r=(), w=()):
        return self.emit('pool', fn, r, w)

    def dma(self, q, out_ap, in_ap, r=(), w=(), sem_on=None, **kw):
        if sem_on is None:
            sem_on = (list(w) + list(r))[0]
        return self.emit(q, lambda e: e.dma_start(out=out_ap, in_=in_ap, **kw), r, w, dma=sem_on)

    def final_wait(self, eng, bufs):
        deps = {}
        for b in bufs:
            for dep in [b.lw] + list(b.rd.items()):
                if dep is None:
                    continue
                k, v = dep
                if deps.get(k, 0) < v:
                    deps[k] = v
        waits = [(k, v) for k, v in deps.items() if self.seen[eng].get(k, 0) < v]
        for k, v in waits:
            self.seen[eng][k] = v
        self.lists[eng].append((waits, None, None))

    def barrier(self):
        toks = [(e, self.cnt[e]) for e in ENGS if self.cnt[e] > 0]
        toks += [(ds.key, ds.val) for ds in self.alldsem if ds.val > 0]
        for e in ENGS:
            waits = []
            for k, v in toks:
                if k == e and e == 'pe':
                    continue
                if self.seen[e].get(k, 0) < v:
                    self.seen[e][k] = v
                    waits.append((k, v))
            self.lists[e].append((waits, None, None))

    def build(self):
        nc = self.nc
        semh = self.semh
        lists = self.lists
        needed = {e: set() for e in ENGS}
        for e in ENGS:
            for waits, fn, ds in lists[e]:
                for k, v in waits:
                    if k in needed:
                        needed[k].add(v)
        rank = {}
        for e in ENGS:
            rank[e] = {v: i + 1 for i, v in enumerate(sorted(needed[e]))}

        def run(engname):
            def body(e):
                own = semh[engname]
                seq = 0
                myrank = rank[engname]
                for waits, fn, ds in lists[engname]:
                    for k, v in waits:
                        if k in rank:
                            e.wait_ge(semh[k], rank[k][v])
                        else:
                            e.wait_ge(semh[k], v)
                    if fn is None:
                        continue
                    ins = fn(e)
                    if ds is None:
                        seq += 1
                        if seq in myrank:
                            ins.then_inc(own, 1)
                    else:
                        ins.then_inc(ds.h, 16)
            return body

        with nc.Block() as block:
            block.tensor(run('pe'))
            block.scalar(run('act'))
            block.vector(run('dve'))
            block.gpsimd(run('pool'))
            block.sync(run('sp'))


class View:
    def __init__(self, base, ap):
        self.base = base
        self.t = ap
        self.name = base.name

    def __getitem__(self, idx):
        return self.t[idx]

    lw = property(lambda s: s.base.lw, lambda s, v: setattr(s.base, 'lw', v))
    rd = property(lambda s: s.base.rd, lambda s, v: setattr(s.base, 'rd', v))
    dsem = property(lambda s: s.base.dsem, lambda s, v: setattr(s.base, 'dsem', v))

from contextlib import ExitStack

T = 8192
D = 1024
ST = 2048
NST = T // ST
SUB = 512
EPS = 1e-6
GROUPS = ((128, 1), (512, 4), (2048, 16))
NS = 16
NSL = 22


def unit_tokens(g, u):
    d = GROUPS[g][1]
    if d == 1:
        return slice(128 * u, 128 * u + 128)
    if d == 4:
        b, r = divmod(u, 4)
        return slice(512 * b + r, 512 * b + 512, 4)
    return slice(u, 2048, 16)


def build_program(do_prompt=True, do_sample=True, nst=NST):
    nc = bass.Bass("TRN2", target_bir_lowering=False)

    def din(name, shape):
        return nc.dram_tensor(name, list(shape), F32, kind="ExternalInput").ap()

    def dout(name, shape):
        return nc.dram_tensor(name, list(shape), F32, kind="ExternalOutput").ap()

    x_d = din("x", [T, D])
    xs_d = din("xs", [NS, D])
    c_d = [din("c128", [NS, 128, 1024]), din("c512", [NS, 512, 1024]), din("c2048", [NS, 2048, 1024])]
    sh_d = din("sh", [NS, 8, 128, 128])
    win_d = din("w_in", [D, 11264])
    wap_d = din("wap", [512, D])
    whp_d = din("whp", [D, D])
    wout_d = din("wout", [D, D])
    nw_d = din("nw", [128, 8])
    hnw_d = din("hnw", [128, 1])
    lbl_d = din("lbl", [128, 16])
    fnw_d = din("fnw", [128, D])
    nwr_d = din("nwr", [128, D])
    ident_d = din("ident", [128, 128])
    eb_d = din("eb", [128, 12 * 512])
    hmask_d = din("hmask", [128, 64])
    rsm_d = din("rsm", [128, 512])
    sel_d = din("sel", [128, 64])
    lblr_d = din("lblr", [NS, 2048])
    hnwr_d = din("hnwr", [NS, 1024])
    sbias_d = din("sbias", [128, 24])
    onehot_d = din("onehot", [NS, NS * 128])
    bdm_d = din("bdm", [8, 520])
    en_d = din("en", [8, NS * NS])

    y_d = dout("y", [T, D])
    ys_d = dout("ys", [NS, D])
    kvp_d = [dout("kv128", [128, 1024]), dout("kv512", [512, 1024]), dout("kv2048", [2048, 1024])]
    stp_d = dout("stp", [8, 128, 128])
    kvs_d = [dout("kvs128", [NS, 1024]), dout("kvs512", [NS, 1024]), dout("kvs2048", [NS, 1024])]
    sts_d = dout("sts", [NS, 8, 128, 128])

    wib_t = nc.dram_tensor("wib", [NSL, 128, 8, 512], BF16).ap()
    wapb_t = nc.dram_tensor("wapb", [8, 128, 4, 128], BF16).ap()
    whpb_t = nc.dram_tensor("whpb", [8, 128, 8, 128], BF16).ap()
    woutb_t = nc.dram_tensor("woutb", [2, 128, 8, 512], BF16).ap()
    NU = (1, 4, 16)
    hist_t = [[(nc.dram_tensor("histk%d_%d" % (g, hp), [128, NU[g] * 128], BF16).ap(),
                nc.dram_tensor("histv%d_%d" % (g, hp), [128, NU[g] * 132], BF16).ap())
               for hp in range(4)] for g in range(3)]

    with ExitStack() as stack:
        P = Prog(nc, stack)
        out_bufs = []

        wib = [P.wrap("wib%d" % k, wib_t) for k in range(3)]
        wsrc = win_d.rearrange("(c p) (s n) -> s p c n", p=128, n=512)
        for k, (s0, s1) in enumerate(((0, 8), (8, 16), (16, 22))):
            for s in range(s0, s1):
                P.dma('pool', wib_t[s], wsrc[s], w=[wib[k]], sem_on=wib[k])

        def wib_buf(s):
            return wib[0 if s < 8 else (1 if s < 16 else 2)]

        wapb = P.wrap("wapb", wapb_t)
        whpb = P.wrap("whpb", whpb_t)
        woutb = P.wrap("woutb", woutb_t)
        s_ap = wap_d.rearrange("(hp p) (j n) -> j p hp n", p=128, n=128)
        s_hp = whp_d.rearrange("(h p) (j n) -> j p h n", p=128, n=128)
        for j in range(8):
            P.dma('pool', wapb_t[j], s_ap[j], w=[wapb], sem_on=wapb)
            P.dma('pool', whpb_t[j], s_hp[j], w=[whpb], sem_on=whpb)
        s_wo = wout_d.rearrange("(c p) (s n) -> s p c n", p=128, n=512)
        for s in range(2):
            P.dma('pool', woutb_t[s], s_wo[s], w=[woutb], sem_on=woutb)

        ident_f = P.sb("ident_f", [128, 128], F32)
        ident = P.sb("ident", [128, 128], BF16)
        eb = P.sb("eb", [128, 12, 512], BF16)
        hmask = P.sb("hmask", [128, 64], F32)
        rsm = P.sb("rsm", [128, 512], F32)
        sel = P.sb("sel", [128, 64], F32)
        nw = P.sb("nw", [128, 8], F32)
        hnw = P.sb("hnw", [128, 1], F32)
        lbl = P.sb("lbl", [128, 16], F32)
        fnw = P.sb("fnw", [128, D], F32)
        epsc = P.sb("epsc", [128, 1], F32)
        ones_f = P.sb("ones_f", [128, 128], F32)
        lbc = P.sb("lbc", [128, 32], F32)
        P.dma('sp', ident_f[:], ident_d, w=[ident_f])
        for i in range(12):
            P.dma('pool', eb[:, i, :], eb_d[:, i * 512:(i + 1) * 512], w=[eb], sem_on=eb)
        P.dma('sp', hmask[:], hmask_d, w=[hmask])
        P.dma('sp', rsm[:], rsm_d, w=[rsm])
        P.dma('sp', sel[:], sel_d, w=[sel])
        P.dma('sp', nw[:], nw_d, w=[nw])
        P.dma('sp', hnw[:], hnw_d, w=[hnw])
        P.dma('sp', lbl[:], lbl_d, w=[lbl])
        P.dma('sp', fnw[:], fnw_d, w=[fnw])
        P.dve(lambda e: e.tensor_copy(out=ident[:], in_=ident_f[:]), r=[ident_f], w=[ident])
        P.pool(lambda e: e.memset(epsc[:], EPS), w=[epsc])
        P.pool(lambda e: e.memset(ones_f[:], 1.0), w=[ones_f])
        P.dve(lambda e: e.tensor_tensor(out=lbc[:, 24:32], in0=lbl[:, 0:8], in1=lbl[:, 8:16], op=ALU.subtract), r=[lbl], w=[lbc])
        P.act(lambda e: e.activation(out=lbc[:, 0:8], in_=lbc[:, 24:32], func=AF.Sigmoid), r=[lbc], w=[lbc])
        P.dve(lambda e: e.tensor_scalar(out=lbc[:, 8:16], in0=lbc[:, 0:8], scalar1=-1.0, scalar2=1.0, op0=ALU.mult, op1=ALU.add), r=[lbc], w=[lbc])
        P.dve(lambda e: e.tensor_scalar(out=lbc[:, 16:24], in0=lbc[:, 0:8], scalar1=1.0, scalar2=-1.0, op0=ALU.mult, op1=ALU.add), r=[lbc], w=[lbc])

        pb = [P.ps("pb%d" % i, [128, 512], F32) for i in range(7)]
        pbf = P.ps("pbf", [128, 1024], BF16)

        wsl = [P.sb("wsl%d" % i, [128, 8, 512], BF16) for i in range(3)]
        wsl_i = [0]

        def load_slice(s):
            b = wsl[wsl_i[0] % 3]
            wsl_i[0] += 1
            P.dma('sp', b[:], wib_t[s], r=[wib_buf(s)], w=[b], sem_on=b)
            return b

        def rstd_from_ss(dst, ss, scale, n):
            P.act(lambda e: e.activation(out=dst[:, 0:n], in_=ss[:, 0:n], func=AF.Ln, bias=epsc[:, 0:1], scale=scale), r=[ss, epsc], w=[dst])
            P.act(lambda e: e.activation(out=dst[:, 0:n], in_=dst[:, 0:n], func=AF.Exp, scale=-0.5), r=[dst], w=[dst])

        Lc = locals()
        if do_sample:
            with ExitStack() as sstack:
                P.stack = sstack
                sample_path(P, Lc)
                P.barrier()
            P.stack = stack
        if do_prompt:
            prompt_path(P, Lc)
        P.final_wait('sp', out_bufs)
        P.build()
    return nc


def prompt_path(P, L):
    nc = L['nc']; pb = L['pb']; pbf = L['pbf']; out_bufs = L['out_bufs']
    x_d = L['x_d']; y_d = L['y_d']; kvp_d = L['kvp_d']; stp_d = L['stp_d']
    ident = L['ident']; eb = L['eb']; hmask = L['hmask']; rsm = L['rsm']; sel = L['sel']
    hnw = L['hnw']; fnw = L['fnw']; lbc = L['lbc']; epsc = L['epsc']; ones_f = L['ones_f']
    load_slice = L['load_slice']; rstd_from_ss = L['rstd_from_ss']
    hist_t = L['hist_t']; NU = L['NU']; nst = L['nst']
    wapb_t = L['wapb_t']; whpb_t = L['whpb_t']; woutb_t = L['woutb_t']
    wapb = L['wapb']; whpb = L['whpb']; woutb = L['woutb']
    nwr_d = L['nwr_d']

    nwr = P.sb("nwr", [128, D], F32)
    P.dma('sp', nwr[:], nwr_d, w=[nwr])
    hT = P.sb("hT", [128, 8, ST], BF16)
    xt = [P.sb("xt%d" % i, [128, D], F32) for i in range(2)]
    xn = P.sb("xn", [128, D], BF16)
    ssn = P.sb("ssn", [128, 4], F32)
    rsn = P.sb("rsn", [128, 4], F32)
    qU = P.sb("qU", [128, 16, 128], BF16)
    kU = P.sb("kU", [128, 16, 128], BF16)
    vU = P.sb("vU", [128, 16, 2, 66], BF16)
    kprev = P.sb("kprev", [128, 16, 128], BF16)
    vprev = P.sb("vprev", [128, 16, 2, 66], BF16)
    big = P.sb("big", [128, 9 * 512], F32)
    acc = Buf("acc", big.t[0:65, 0:4096].rearrange("p (h t) -> p h t", h=2))
    tmps = [Buf("tmp%d" % i, big.t[:, i * 512:(i + 1) * 512]) for i in range(9)]
    esb = [[P.sb("esb%d_%d" % (i, hh), [128, 512], BF16) for hh in range(2)] for i in range(2)]
    pTb = [[[P.sb("pTb%d_%d_%d" % (i, hh, uu), [128, 256], BF16) for uu in range(2)] for hh in range(2)] for i in range(2)]
    attT = P.sb("attT", [128, 4, ST], BF16)
    att1 = P.sb("att1", [64, ST], BF16)
    gsA = P.sb("gsA", [64, 512], F32)
    tmpA = P.sb("tmpA", [64, 512], F32)
    rcpA = tmps[8]
    Sck = View(kU, kU.t[:, 0:8, :])
    aTm8 = View(qU, qU.t[:, 0:4, :].rearrange("p a (b t) -> p (a b) t", t=64))
    aTms = [P.sb("aTm%d" % i, [128, 64], BF16) for i in range(2)]
    Sxs = [P.sb("Sx%d" % i, [128, 128], BF16) for i in range(2)]
    histb = [[P.wrap("hist%d_%d" % (g, hp), hist_t[g][hp][0]) for hp in range(4)] for g in range(3)]
    vI = P.sb("vI", [128, 4, 1024], BF16)
    S32 = P.sb("S32", [128, 8, 128], F32)
    Sbf = P.sb("Sbf", [128, 8, 128], BF16)
    sig = tmps[0]
    logf = tmps[1]
    hk = tmps[2]
    bcs = tmps[3]
    Ep = tmps[4]
    En = tmps[5]
    hq = tmps[6]
    kt32 = tmps[7]
    qtl = P.sb("qtl", [128, 512], BF16)
    ktl = P.sb("ktl", [128, 512], BF16)
    kend = P.sb("kend", [128, 512], BF16)
    kendT = P.sb("kendT", [128, 4, 128], BF16)
    o32 = tmps[3]
    o2 = tmps[1]
    rs_h = tmps[5]
    gsH = tmps[8]
    hoT = P.sb("hoT", [128, 8, 512], BF16)
    ga = tmps[0]
    gbt = tmps[1]
    m1 = tmps[2]
    m2 = tmps[3]
    mergedT = View(vI, vI.t[:].rearrange("p a b -> p (a b)").rearrange("p (c t) -> p c t", t=512))
    wapj = [P.sb("wapj%d" % i, [128, 4, 128], BF16) for i in range(2)]
    whpj = [P.sb("whpj%d" % i, [128, 8, 128], BF16) for i in range(2)]
    xr = P.sb("xr", [128, D], F32)
    yo = [P.sb("yo%d" % i, [128, D], F32) for i in range(2)]
    kvtb = yo
    sq = xr
    ssf = P.sb("ssf", [128, 4], F32)
    rsf = P.sb("rsf", [128, 4], F32)

    P.pool(lambda e: e.memset(vU[:].rearrange("p u h e -> p (u h e)"), 1.0), w=[vU])
    P.pool(lambda e: e.memset(kprev[:].rearrange("p u i -> p (u i)"), 0.0), w=[kprev])
    P.pool(lambda e: e.memset(vprev[:].rearrange("p u h e -> p (u h e)"), 0.0), w=[vprev])
    P.pool(lambda e: e.memset(S32[:].rearrange("p h v -> p (h v)"), 0.0), w=[S32])
    P.pool(lambda e: e.memset(Sbf[:].rearrange("p h v -> p (h v)"), 0.0), w=[Sbf])

    def mm(bank_ap, lhsT, rhs, first, last, r, w):
        P.pe(lambda e: e.matmul(bank_ap, lhsT=lhsT, rhs=rhs, start=first, stop=last), r=r, w=w)

    def proj_fm(bank, wsb, off, m, tsl):
        for c in range(8):
            mm(bank[0:m, :], wsb[:, c, off:off + m], hT[:, c, tsl], c == 0, c == 7, [wsb, hT], [bank])

    def phase_n(st):
        def front(tt):
            t0 = st * ST + tt * 128
            xb = xt[tt % 2]
            k = tt % 2
            P.dma('sp', xb[:], x_d[t0:t0 + 128, :], w=[xb])
            P.act(lambda e, xb=xb: e.activation(out=sq[:], in_=xb[:], func=AF.Square), r=[xb], w=[sq])
            P.dve(lambda e, k=k: e.tensor_reduce(out=ssn[:, k:k + 1], in_=sq[:], axis=AX.X, op=ALU.add), r=[sq], w=[ssn])
            P.act(lambda e, k=k: e.activation(out=rsn[:, 2 + k:3 + k], in_=ssn[:, k:k + 1], func=AF.Ln, bias=epsc[:, 0:1], scale=1.0 / D), r=[ssn, epsc], w=[rsn])
            P.act(lambda e, k=k: e.activation(out=rsn[:, k:k + 1], in_=rsn[:, 2 + k:3 + k], func=AF.Exp, scale=-0.5), r=[rsn], w=[rsn])

        def back(tt):
            xb = xt[tt % 2]
            k = tt % 2
            P.dve(lambda e, xb=xb, k=k: e.scalar_tensor_tensor(out=xn[:], in0=xb[:], scalar=rsn[:, k:k + 1], in1=nwr[:], op0=ALU.mult, op1=ALU.mult), r=[xb, rsn, nwr], w=[xn])
            for c in range(8):
                P.pe(lambda e, c=c: e.transpose(out=pbf[:, c * 128:(c + 1) * 128], in_=xn[:, c * 128:(c + 1) * 128], identity=ident[:]), r=[xn, ident], w=[pbf])
            P.act(lambda e, tt=tt: e.copy(out=hT[:, :, tt * 128:(tt + 1) * 128], in_=pbf[:].rearrange("p (c t) -> p c t", t=128)), r=[pbf], w=[hT])

        front(0)
        for tt in range(16):
            if tt + 1 < 16:
                front(tt + 1)
            back(tt)

    def att_gh(st, g, hp):
        d = GROUPS[g][1]
        nu = NU[g]
        off = hp * 128
        wq = load_slice(3 * g)
        wk = load_slice(3 * g + 1)
        wv = load_slice(3 * g + 2)
        hb = histb[g][hp]
        htk, htv = hist_t[g][hp]
        if st > 0:
            P.dma('sp', kprev[:, 0:nu, :].rearrange("p u i -> p (u i)"), htk, r=[hb], w=[kprev], sem_on=kprev)
            P.dma('sp', vprev[:, 0:nu].rearrange("p u h e -> p (u h e)"), htv, r=[hb], w=[vprev], sem_on=vprev)
        for wsb, dst, eng in ((wq, qU, 'act'), (wk, kU, 'dve')):
            for n in range(4):
                bank = pb[n % 2]
                proj_fm(bank, wsb, off, 128, slice(n * 512, (n + 1) * 512))
                if d == 1:
                    o_ap = dst[:, 4 * n:4 * n + 4, :]
                    i_ap = bank[:].rearrange("p (u i) -> p u i", i=128)
                elif d == 4:
                    o_ap = dst[:, 4 * n:4 * n + 4, :]
                    i_ap = bank[:].rearrange("p (i r) -> p r i", r=4)
                else:
                    o_ap = dst[:, :, 32 * n:32 * n + 32]
                    i_ap = bank[:].rearrange("p (i r) -> p r i", r=16)
                if eng == 'act':
                    P.act(lambda e, o_ap=o_ap, i_ap=i_ap: e.copy(out=o_ap, in_=i_ap), r=[bank], w=[dst])
                else:
                    P.dve(lambda e, o_ap=o_ap, i_ap=i_ap: e.tensor_copy(out=o_ap, in_=i_ap), r=[bank], w=[dst])
        import os
        sub = int(os.environ.get("KSUB", "9"))
        if sub < 1:
            return
        for ug in range(4):
            bank = pb[2 + ug % 2]
            for uu in range(4):
                ts = unit_tokens(g, ug * 4 + uu)
                for c in range(8):
                    mm(bank[:, uu * 128:(uu + 1) * 128], hT[:, c, ts], wv[:, c, off:off + 128], c == 0, c == 7, [hT, wv], [bank])
            o_ap = vU[:, ug * 4:ug * 4 + 4, :, 0:64]
            i_ap = bank[:].rearrange("p (u h e) -> p u h e", u=4, h=2)
            if ug % 2 == 0:
                P.act(lambda e, o_ap=o_ap, i_ap=i_ap: e.copy(out=o_ap, in_=i_ap), r=[bank], w=[vU])
            else:
                P.dve(lambda e, o_ap=o_ap, i_ap=i_ap: e.tensor_copy(out=o_ap, in_=i_ap), r=[bank], w=[vU])
        if st == NST - 1 and hp == 0:
            ntile = GROUPS[g][0] // 128
            for tt in range(16 - ntile, 16):
                kvt = kvtb[tt % 2]
                for half, wsb in enumerate((wk, wv)):
                    bank = pb[half]
                    for c in range(8):
                        mm(bank[:], hT[:, c, tt * 128:(tt + 1) * 128], wsb[:, c, :], c == 0, c == 7, [hT, wsb], [bank])
                    if half == 0:
                        P.act(lambda e, kvt=kvt, bank=bank: e.copy(out=kvt[:, 0:512], in_=bank[:]), r=[bank], w=[kvt])
                    else:
                        P.dve(lambda e, kvt=kvt, bank=bank: e.tensor_copy(out=kvt[:, 512:1024], in_=bank[:]), r=[bank], w=[kvt])
                row0 = (tt - (16 - ntile)) * 128
                P.dma('sp', kvp_d[g][row0:row0 + 128, :], kvt[:], r=[kvt], sem_on=kvt)
                if kvt not in out_bufs:
                    out_bufs.append(kvt)
        if sub < 2:
            return
        def prev_of(u):
            if g == 0:
                return (kU, u - 1) if u > 0 else (kprev, 0)
            if g == 1:
                return (kU, u - 4) if u >= 4 else (kprev, u)
            return (kprev, u)

        ebi = g * 4 + hp
        sbks = ((pb[4], pb[5]), (pb[0], pb[1]))
        obs = (pb[6], pb[2])

        def scores(up):
            for hh in range(2):
                lo = hh * 64
                sbk = sbks[up % 2][hh]
                for uu in range(2):
                    u = up * 2 + uu
                    kpb, ku = prev_of(u)
                    mm(sbk[:, uu * 256:uu * 256 + 128], kU[lo:lo + 64, u, :], qU[lo:lo + 64, u, :], True, True, [kU, qU], [sbk])
                    mm(sbk[:, uu * 256 + 128:uu * 256 + 256], kpb[lo:lo + 64, ku, :], qU[lo:lo + 64, u, :], True, True, [kpb, qU], [sbk])

        def softmax(up):
            for hh in range(2):
                e_ = esb[up % 2][hh]
                sbk = sbks[up % 2][hh]
                P.act(lambda e, e_=e_, sbk=sbk: e.activation(out=e_[:], in_=sbk[:], func=AF.Exp, scale=0.125), r=[sbk], w=[e_])
                for uu in range(2):
                    p_ = pTb[up % 2][hh][uu]
                    fn = lambda e, e_=e_, p_=p_, hh=hh, uu=uu: e.tensor_tensor(out=p_[:], in0=e_[:, uu * 256:(uu + 1) * 256], in1=eb[:, ebi, hh * 256:(hh + 1) * 256], op=ALU.mult)
                    if uu == 0 or os.environ.get("KPOOL", "0") == "0":
                        P.dve(fn, r=[e_, eb], w=[p_])
                    else:
                        P.pool(fn, r=[e_, eb], w=[p_])

        def pv(up):
            for uu in range(2):
                u = up * 2 + uu
                ts = unit_tokens(g, u)
                kpb, ku = prev_of(u)
                vpb = vU if kpb is kU else vprev
                ob = obs[uu]
                for hh in range(2):
                    p_ = pTb[up % 2][hh][uu]
                    mm(ob[0:65, hh * 128:(hh + 1) * 128], vU[:, u, hh, 0:65], p_[:, 0:128], True, False, [vU, p_], [ob])
                    mm(ob[0:65, hh * 128:(hh + 1) * 128], vpb[:, ku, hh, 0:65], p_[:, 128:256], False, True, [vpb, p_], [ob])
                accv = acc[:, :, ts]
                src = ob[0:65, 0:256].rearrange("p (h q) -> p h q", h=2)
                if g == 0:
                    P.dve(lambda e, accv=accv, src=src: e.tensor_copy(out=accv, in_=src), r=[ob], w=[acc])
                else:
                    P.dve(lambda e, accv=accv, src=src: e.tensor_tensor(out=accv, in0=accv, in1=src, op=ALU.add), r=[ob, acc], w=[acc])

        scores(0)
        softmax(0)
        for up in range(8):
            if up + 1 < 8:
                scores(up + 1)
                softmax(up + 1)
            pv(up)
        if st < nst - 1:
            u0 = 16 - nu
            if g == 2:
                P.dve(lambda e: e.tensor_copy(out=kprev[:].rearrange("p u i -> p (u i)"), in_=kU[:].rearrange("p u i -> p (u i)")), r=[kU], w=[kprev])
                P.dma('sp', htk, kprev[:].rearrange("p u i -> p (u i)"), r=[kprev], w=[hb], sem_on=hb)
                if st == 0:
                    P.pool(lambda e: e.memset(kprev[:].rearrange("p u i -> p (u i)"), 0.0), w=[kprev])
            else:
                P.dma('sp', htk, kU[:, u0:16, :].rearrange("p u i -> p (u i)"), r=[kU], w=[hb], sem_on=hb)
            P.dma('sp', htv, vU[:, u0:16].rearrange("p u h e -> p (u h e)"), r=[vU], w=[hb], sem_on=hb)

    def att_final(hp):
        wg = load_slice(9)
        for hh in range(2):
            h = hp * 2 + hh
            for n in range(4):
                tsl = slice(n * 512, (n + 1) * 512)
                rb = pb[2]
                gbk = pb[3]
                mm(rb[0:64, :], sel[0:65, :], acc[0:65, hh, tsl], True, True, [sel, acc], [rb])
                proj_fm(gbk, wg, h * 64, 64, tsl)
                P.act(lambda e, gbk=gbk: e.activation(out=gsA[:], in_=gbk[0:64, :], func=AF.Silu), r=[gbk], w=[gsA])
                P.dve(lambda e, rb=rb: e.reciprocal(out=rcpA[0:64, :], in_=rb[0:64, :]), r=[rb], w=[rcpA])
                P.dve(lambda e, hh=hh, tsl=tsl: e.tensor_tensor(out=tmpA[:], in0=acc[0:64, hh, tsl], in1=rcpA[0:64, :], op=ALU.mult), r=[acc, rcpA], w=[tmpA])
                dst = attT[0:64, hp, tsl] if hh == 0 else att1[:, tsl]
                dbuf = attT if hh == 0 else att1
                P.dve(lambda e, dst=dst: e.tensor_tensor(out=dst, in0=tmpA[:], in1=gsA[:], op=ALU.mult), r=[tmpA, gsA], w=[dbuf])
            if hh == 1:
                P.dma('sp', attT[64:128, hp, :], att1[:], r=[att1], w=[attT], sem_on=att1)

    def hgrn_sub(st, j):
        c0 = j * 512
        tsl = slice(c0, c0 + 512)
        wi = [load_slice(14), load_slice(15)]
        for tt in range(4):
            for half in range(2):
                bank = pb[half]
                for c in range(8):
                    mm(bank[:], hT[:, c, c0 + tt * 128:c0 + (tt + 1) * 128], wi[half][:, c, :], c == 0, c == 7, [hT, wi[half]], [bank])
                o_ap = vI[:, tt, half * 512:(half + 1) * 512]
                if half == 0:
                    P.act(lambda e, o_ap=o_ap, bank=bank: e.copy(out=o_ap, in_=bank[:]), r=[bank], w=[vI])
                else:
                    P.dve(lambda e, o_ap=o_ap, bank=bank: e.tensor_copy(out=o_ap, in_=bank[:]), r=[bank], w=[vI])
        wts = {}

        def proj_f_pe(h):
            h4_, hq2 = divmod(h, 4)
            if hq2 == 0:
                wts[h4_] = (load_slice(10 + h4_), load_slice(12 + h4_), load_slice(16 + h4_))
            proj_fm(pb[0], wts[h4_][1], hq2 * 128, 128, tsl)

        def sig_evac():
            P.act(lambda e: e.activation(out=sig[:], in_=pb[0][:], func=AF.Sigmoid), r=[pb[0]], w=[sig])

        proj_f_pe(0)
        sig_evac()
        for h4 in range(2):
            for hq_ in range(4):
                h = h4 * 4 + hq_
                off = hq_ * 128
                wq_, wf_, wg_ = wts[h4]
                bQ = pb[1]
                proj_fm(bQ, wq_, off, 128, tsl)
                P.act(lambda e, bQ=bQ: e.activation(out=hq[:], in_=bQ[:], func=AF.Silu), r=[bQ], w=[hq])
                bG = pb[2]
                proj_fm(bG, wg_, off, 128, tsl)
                P.act(lambda e, bG=bG: e.activation(out=gsH[:], in_=bG[:], func=AF.Silu), r=[bG], w=[gsH])
                P.act(lambda e, h=h: e.activation(out=logf[:], in_=sig[:], func=AF.Ln, bias=lbc[:, h:h + 1], scale=lbc[:, 8 + h:9 + h]), r=[sig, lbc], w=[logf])
                P.dve(lambda e, h=h: e.tensor_scalar(out=hk[:], in0=sig[:], scalar1=lbc[:, 16 + h:17 + h], scalar2=lbc[:, 8 + h:9 + h], op0=ALU.mult, op1=ALU.add), r=[sig, lbc], w=[hk])
                P.dve(lambda e: e.tensor_tensor_scan(out=bcs[:], data0=rsm[:], data1=logf[:], initial=0.0, op0=ALU.mult, op1=ALU.add), r=[rsm, logf], w=[bcs])
                P.act(lambda e: e.activation(out=Ep[:], in_=bcs[:], func=AF.Exp), r=[bcs], w=[Ep])
                P.act(lambda e: e.activation(out=En[:], in_=bcs[:], func=AF.Exp, scale=-1.0), r=[bcs], w=[En])
                P.dve(lambda e: e.scalar_tensor_tensor(out=qtl[:], in0=hq[:], scalar=float(128 ** -0.5), in1=Ep[:], op0=ALU.mult, op1=ALU.mult), r=[hq, Ep], w=[qtl])
                P.dve(lambda e: e.tensor_tensor(out=kt32[:], in0=hk[:], in1=En[:], op=ALU.mult), r=[hk, En], w=[kt32])
                P.dve(lambda e: e.tensor_copy(out=ktl[:], in_=kt32[:]), r=[kt32], w=[ktl])
                for cc in range(8):
                    P.dve(lambda e, cc=cc: e.tensor_scalar(out=kend[:, cc * 64:(cc + 1) * 64], in0=kt32[:, cc * 64:(cc + 1) * 64], scalar1=Ep[:, cc * 64 + 63:cc * 64 + 64], scalar2=None, op0=ALU.mult), r=[kt32, Ep], w=[kend])
                for tt in range(4):
                    P.pe(lambda e, tt=tt: e.transpose(out=pbf[:, tt * 128:(tt + 1) * 128], in_=kend[:, tt * 128:(tt + 1) * 128], identity=ident[:]), r=[kend, ident], w=[pbf])
                P.act(lambda e: e.copy(out=kendT[:].rearrange("p a b -> p (a b)"), in_=pbf[:, 0:512]), r=[pbf], w=[kendT])
                bOs = (pb[3], pb[2])
                for cc in range(8):
                    bO = bOs[cc % 2]
                    osl = slice((cc // 2) * 64, (cc // 2) * 64 + 64)
                    tt = cc // 2
                    lo = (cc % 2) * 64
                    csl = slice(cc * 64, (cc + 1) * 64)
                    bA = pb[4]
                    aTm = aTms[cc % 2]
                    mm(bA[lo:lo + 64, 0:64], ktl[:, csl], qtl[:, csl], True, True, [ktl, qtl], [bA])
                    bS = pb[5 + cc % 2]
                    mm(bS[:, 0:128], kendT[lo:lo + 64, tt, :], vI[lo:lo + 64, tt, h * 128:(h + 1) * 128], True, True, [kendT, vI], [bS])
                    P.dve(lambda e, lo=lo, bA=bA, aTm=aTm: e.tensor_tensor(out=aTm[lo:lo + 64, :], in0=bA[lo:lo + 64, 0:64], in1=hmask[lo:lo + 64, :], op=ALU.mult), r=[bA, hmask], w=[aTm])
                    if cc == 0:
                        mm(bO[:, osl], Sbf[:, h, :], qtl[:, csl], True, False, [Sbf, qtl], [bO])
                    else:
                        Sp = Sxs[(cc - 1) % 2]
                        mm(bO[:, osl], Sp[:], qtl[:, csl], True, False, [Sp, qtl], [bO])
                    mm(bO[:, osl], vI[lo:lo + 64, tt, h * 128:(h + 1) * 128], aTm[lo:lo + 64, :], False, True, [vI, aTm], [bO])
                    P.dve(lambda e, h=h, cc=cc, bS=bS: e.scalar_tensor_tensor(out=S32[:, h, :], in0=S32[:, h, :], scalar=Ep[:, cc * 64 + 63:cc * 64 + 64], in1=bS[:, 0:128], op0=ALU.mult, op1=ALU.add), r=[S32, Ep, bS], w=[S32])
                    if cc < 7:
                        Sn_ = Sxs[cc % 2]
                        P.dve(lambda e, h=h, Sn_=Sn_: e.tensor_copy(out=Sn_[:], in_=S32[:, h, :]), r=[S32], w=[Sn_])
                    else:
                        P.dve(lambda e, h=h: e.tensor_copy(out=Sbf[:, h, :], in_=S32[:, h, :]), r=[S32], w=[Sbf])
                if h + 1 < 8:
                    proj_f_pe(h + 1)
                for par in range(2):
                    bO = bOs[par]
                    o2v = o2[:].rearrange("p (a b t) -> p a b t", b=2, t=64)[:, :, par, :]
                    o32v = o32[:].rearrange("p (a b t) -> p a b t", b=2, t=64)[:, :, par, :]
                    srcv = bO[:, 0:256].rearrange("p (a t) -> p a t", t=64)
                    P.act(lambda e, o2v=o2v, srcv=srcv: e.activation(out=o2v, in_=srcv, func=AF.Square), r=[bO], w=[o2])
                    P.dve(lambda e, o32v=o32v, srcv=srcv: e.tensor_copy(out=o32v, in_=srcv), r=[bO], w=[o32])
                bN = pb[4]
                mm(bN[:], ones_f[:], o2[:], True, True, [ones_f, o2], [bN])
                P.act(lambda e, bN=bN: e.activation(out=rs_h[:], in_=bN[:], func=AF.Ln, bias=epsc[:, 0:1], scale=1.0 / 128), r=[bN, epsc], w=[rs_h])
                P.act(lambda e: e.activation(out=rs_h[:], in_=rs_h[:], func=AF.Exp, scale=-0.5), r=[rs_h], w=[rs_h])
                if h + 1 < 8:
                    sig_evac()
                P.dve(lambda e: e.scalar_tensor_tensor(out=o32[:], in0=o32[:], scalar=hnw[:, 0:1], in1=rs_h[:], op0=ALU.mult, op1=ALU.mult), r=[o32, hnw, rs_h], w=[o32])
                P.dve(lambda e, h=h: e.tensor_tensor(out=hoT[:, h, :], in0=o32[:], in1=gsH[:], op=ALU.mult), r=[o32, gsH], w=[hoT])

    def out_sub(st, j):
        c0 = j * 512
        tsl = slice(c0, c0 + 512)
        wma = [None, None]
        wmb = [None, None]
        for jj in range(8):
            if jj % 4 == 0:
                wma[jj // 4] = load_slice(18 + jj // 4)
                wmb[jj // 4] = load_slice(20 + jj // 4)
            wa_ = wapj[jj % 2]
            wh_ = whpj[jj % 2]
            P.dma('sp', wa_[:], wapb_t[jj], r=[wapb], w=[wa_], sem_on=wa_)
            P.dma('sp', wh_[:], whpb_t[jj], r=[whpb], w=[wh_], sem_on=wh_)
            off = (jj % 4) * 128
            bA = pb[0]
            proj_fm(bA, wma[jj // 4], off, 128, tsl)
            P.act(lambda e, bA=bA: e.activation(out=ga[:], in_=bA[:], func=AF.Sigmoid), r=[bA], w=[ga])
            bB = pb[1]
            proj_fm(bB, wmb[jj // 4], off, 128, tsl)
            P.act(lambda e, bB=bB: e.activation(out=gbt[:], in_=bB[:], func=AF.Sigmoid), r=[bB], w=[gbt])
            b1 = pb[2]
            for hp in range(4):
                mm(b1[:], wa_[:, hp, :], attT[:, hp, tsl], hp == 0, hp == 3, [wa_, attT], [b1])
            b2 = pb[3]
            for h in range(8):
                mm(b2[:], wh_[:, h, :], hoT[:, h, :], h == 0, h == 7, [wh_, hoT], [b2])
            P.dve(lambda e, b1=b1: e.tensor_tensor(out=m1[:], in0=ga[:], in1=b1[:], op=ALU.mult), r=[ga, b1], w=[m1])
            P.dve(lambda e, b2=b2: e.tensor_tensor(out=m2[:], in0=gbt[:], in1=b2[:], op=ALU.mult), r=[gbt, b2], w=[m2])
            P.dve(lambda e, jj=jj: e.tensor_tensor(out=mergedT[:, jj, :], in0=m1[:], in1=m2[:], op=ALU.add), r=[m1, m2], w=[mergedT])
        wo = []
        for s in range(2):
            b = L['wsl'][L['wsl_i'][0] % 3]
            L['wsl_i'][0] += 1
            P.dma('sp', b[:], woutb_t[s], r=[woutb], w=[b], sem_on=b)
            wo.append(b)
        def ofront(tt):
            t0 = st * ST + c0 + tt * 128
            xb = xt[tt % 2]
            P.dma('sp', xb[:], x_d[t0:t0 + 128, :], w=[xb])
            for half in range(2):
                bank = pb[4 + half]
                for c in range(8):
                    mm(bank[:], mergedT[:, c, tt * 128:(tt + 1) * 128], wo[half][:, c, :], c == 0, c == 7, [mergedT, wo[half]], [bank])
                P.dve(lambda e, half=half, bank=bank, xb=xb: e.tensor_tensor(out=xb[:, half * 512:(half + 1) * 512], in0=bank[:], in1=xb[:, half * 512:(half + 1) * 512], op=ALU.add), r=[bank, xb], w=[xb])

        def oback(tt):
            t0 = st * ST + c0 + tt * 128
            xb = xt[tt % 2]
            yb = yo[tt % 2]
            k = tt % 2
            P.act(lambda e, yb=yb, xb=xb: e.activation(out=yb[:], in_=xb[:], func=AF.Square), r=[xb], w=[yb])
            P.dve(lambda e, yb=yb, k=k: e.tensor_reduce(out=ssf[:, k:k + 1], in_=yb[:], axis=AX.X, op=ALU.add), r=[yb], w=[ssf])
            P.act(lambda e, k=k: e.activation(out=rsf[:, 2 + k:3 + k], in_=ssf[:, k:k + 1], func=AF.Ln, bias=epsc[:, 0:1], scale=1.0 / D), r=[ssf, epsc], w=[rsf])
            P.act(lambda e, k=k: e.activation(out=rsf[:, k:k + 1], in_=rsf[:, 2 + k:3 + k], func=AF.Exp, scale=-0.5), r=[rsf], w=[rsf])
            P.dve(lambda e, yb=yb, xb=xb, k=k: e.scalar_tensor_tensor(out=yb[:], in0=xb[:], scalar=rsf[:, k:k + 1], in1=fnw[:], op0=ALU.mult, op1=ALU.mult), r=[xb, rsf, fnw], w=[yb])
            P.dma('sp', y_d[t0:t0 + 128, :], yb[:], r=[yb], sem_on=yb)
            if yb not in out_bufs:
                out_bufs.append(yb)

        ofront(0)
        for tt in range(4):
            if tt + 1 < 4:
                ofront(tt + 1)
            oback(tt)

    import os
    stage0 = int(os.environ.get("KSTAGE", "9"))
    stage1 = int(os.environ.get("KSTAGE1", "9"))
    for st in range(nst):
        stage = stage0 if st == 0 else stage1
        if st > 0:
            P.barrier()
        if stage >= 1:
            phase_n(st)
        if stage >= 2:
            P.dve(lambda e: e.memset(acc[0:1, 0, 0:1], 0.0), w=tmps + [acc])
            for hp in range(4):
                for g in range(3):
                    att_gh(st, g, hp)
                if stage >= 3:
                    att_final(hp)
        if stage >= 4:
            P.barrier()
            P.dve(lambda e: e.memset(tmps[0][0:1, 0:1], 0.0), w=[acc] + tmps)
            for j in range(4):
                hgrn_sub(st, j)
                if stage >= 5:
                    out_sub(st, j)
    P.dma('sp', stp_d.rearrange("h k v -> k h v"), S32[:], r=[S32], sem_on=S32)
    out_bufs.append(S32)


def sample_path(P, L):
    nc = L['nc']; pb = L['pb']; pbf = L['pbf']; out_bufs = L['out_bufs']
    xs_d = L['xs_d']; c_d = L['c_d']; sh_d = L['sh_d']; ys_d = L['ys_d']; kvs_d = L['kvs_d']; sts_d = L['sts_d']
    ident = L['ident']; ident_f = L['ident_f']; fnw = L['fnw']; epsc = L['epsc']; ones_f = L['ones_f']
    load_slice = L['load_slice']; rstd_from_ss = L['rstd_from_ss']
    wapb_t = L['wapb_t']; whpb_t = L['whpb_t']; woutb_t = L['woutb_t']
    wapb = L['wapb']; whpb = L['whpb']; woutb = L['woutb']
    nwr_d = L['nwr_d']; lblr_d = L['lblr_d']; hnwr_d = L['hnwr_d']; sbias_d = L['sbias_d']
    onehot_d = L['onehot_d']; bdm_d = L['bdm_d']; en_d = L['en_d']
    A = AF
    OFF_G = 4608; OFF_Q = 5120; OFF_F = 6144; OFF_I = 7168; OFF_HG = 8192; OFF_MA = 9216; OFF_MB = 10240

    def sb(name, shape, dt=F32):
        return P.sb("s_" + name, shape, dt)

    class phase:
        def __enter__(self):
            self.outer = P.stack
            self.es = ExitStack()
            self.es.__enter__()
            P.stack = self.es
            return self

        def __exit__(self, *a):
            P.barrier()
            P.stack = self.outer
            return self.es.__exit__(*a)

    att_b = sb("att_b", [NS, 512], BF16)
    ho_b = sb("ho_b", [NS, 1024], BF16)

    nwrs = sb("nwrs", [NS, D]); P.dma('sp', nwrs[:], nwr_d[0:NS, :], w=[nwrs])
    lblr = sb("lblr", [NS, 2048]); P.dma('sp', lblr[:], lblr_d, w=[lblr])
    hnwr = sb("hnwr", [NS, 1024]); P.dma('sp', hnwr[:], hnwr_d, w=[hnwr])
    sbias = sb("sbias", [128, 24]); P.dma('sp', sbias[:], sbias_d, w=[sbias])
    onehot = sb("onehot", [NS, NS * 128]); P.dma('sp', onehot[:], onehot_d, w=[onehot])
    bdm = sb("bdm", [8, 520]); P.dma('sp', bdm[:], bdm_d, w=[bdm])
    en = sb("en", [8, NS * NS]); P.dma('sp', en[:], en_d, w=[en])
    xs = sb("xs", [NS, D]); P.dma('sp', xs[:], xs_d, w=[xs])
    sq = sb("sq", [NS, D])
    st1 = sb("st1", [NS, 16]); st2 = sb("st2", [NS, 16])
    xn = sb("xn", [NS, D], BF16)
    hsT = sb("hsT", [128, 8, NS], BF16)
    projs = sb("projs", [NS, 11264])

    def mm(bank_ap, lhsT, rhs, first, last, r, w):
        P.pe(lambda e: e.matmul(bank_ap, lhsT=lhsT, rhs=rhs, start=first, stop=last), r=r, w=w)

    def rms_rows(dst_bf, src, wrow):
        P.act(lambda e: e.activation(out=sq[:], in_=src[:], func=A.Square), r=[src], w=[sq])
        P.dve(lambda e: e.tensor_reduce(out=st1[:, 0:1], in_=sq[:], axis=AX.X, op=ALU.add), r=[sq], w=[st1])
        P.act(lambda e: e.activation(out=st2[:, 0:1], in_=st1[:, 0:1], func=A.Ln, bias=epsc[0:NS, 0:1], scale=1.0 / D), r=[st1, epsc], w=[st2])
        P.act(lambda e: e.activation(out=st2[:, 0:1], in_=st2[:, 0:1], func=A.Exp, scale=-0.5), r=[st2], w=[st2])
        P.dve(lambda e: e.scalar_tensor_tensor(out=dst_bf[:], in0=src[:], scalar=st2[:, 0:1], in1=wrow[:], op0=ALU.mult, op1=ALU.mult), r=[src, st2, wrow], w=[dst_bf])

    def to_fm_bf(dstT, src_bf, nch):
        for c in range(nch):
            P.pe(lambda e, c=c: e.transpose(out=pbf[:, c * NS:(c + 1) * NS], in_=src_bf[0:NS, c * 128:(c + 1) * 128], identity=ident[0:NS, 0:NS]), r=[src_bf, ident], w=[pbf])
        P.act(lambda e: e.copy(out=dstT[:].rearrange("p c n -> p (c n)"), in_=pbf[:, 0:nch * NS]), r=[pbf], w=[dstT])

    rms_rows(xn, xs, nwrs)
    to_fm_bf(hsT, xn, 8)
    for s in range(NSL):
        wsb = load_slice(s)
        bank = pb[s % 2]
        for c in range(8):
            mm(bank[0:NS, :], hsT[:, c, :], wsb[:, c, :], c == 0, c == 7, [hsT, wsb], [bank])
        if s % 2 == 0:
            P.act(lambda e, s=s, bank=bank: e.copy(out=projs[:, s * 512:(s + 1) * 512], in_=bank[0:NS, :]), r=[bank], w=[projs])
        else:
            P.dve(lambda e, s=s, bank=bank: e.tensor_copy(out=projs[:, s * 512:(s + 1) * 512], in_=bank[0:NS, :]), r=[bank], w=[projs])
    for g in range(3):
        P.dma('sp', kvs_d[g], projs[:, g * 1536 + 512:g * 1536 + 1536], r=[projs], sem_on=projs)
    out_bufs.append(projs)

    phB = phase(); phB.__enter__()
    accn = sb("accn", [NS, 512]); accd = sb("accd", [NS, 8])
    tq = sb("tq", [NS, 512]); ts_ = sb("ts", [NS, 8]); tp = sb("tp", [NS, 8])
    for g in range(3):
        b0 = g * 1536
        P.dve(lambda e, b0=b0: e.tensor_tensor(out=tq[:], in0=projs[:, b0:b0 + 512], in1=projs[:, b0 + 512:b0 + 1024], op=ALU.mult), r=[projs], w=[tq])
        P.dve(lambda e: e.tensor_reduce(out=ts_[:], in_=tq[:].rearrange("p (h e) -> p h e", e=64), axis=AX.X, op=ALU.add), r=[tq], w=[ts_])
        P.act(lambda e: e.activation(out=tp[:], in_=ts_[:], func=A.Exp, scale=0.125), r=[ts_], w=[tp])
        for h in range(8):
            hs = slice(h * 64, (h + 1) * 64)
            if g == 0:
                P.dve(lambda e, h=h, hs=hs, b0=b0: e.tensor_scalar(out=accn[:, hs], in0=projs[:, b0 + 1024 + h * 64:b0 + 1024 + (h + 1) * 64], scalar1=tp[:, h:h + 1], scalar2=None, op0=ALU.mult), r=[projs, tp], w=[accn])
            else:
                P.dve(lambda e, h=h, hs=hs, b0=b0: e.scalar_tensor_tensor(out=accn[:, hs], in0=projs[:, b0 + 1024 + h * 64:b0 + 1024 + (h + 1) * 64], scalar=tp[:, h:h + 1], in1=accn[:, hs], op0=ALU.mult, op1=ALU.add), r=[projs, tp, accn], w=[accn])
        if g == 0:
            P.dve(lambda e: e.tensor_copy(out=accd[:], in_=tp[:]), r=[tp], w=[accd])
        else:
            P.dve(lambda e: e.tensor_tensor(out=accd[:], in0=accd[:], in1=tp[:], op=ALU.add), r=[accd, tp], w=[accd])
    ck = [sb("ck%d" % i, [128, 1024]) for i in range(3)]
    prod = sb("prod", [128, 512])
    s8 = sb("s8", [128, 8]); p8s = [sb("p8_%d" % i, [128, 8]) for i in range(2)]
    msk = sb("msk", [8, 520])
    items = [(g, n) for g in range(3) for n in range(NS)]

    def stage1(it):
        g, n = items[it]
        win, dil = GROUPS[g]
        b0 = g * 1536
        c_ = ck[it % 3]
        p8 = p8s[it % 2]
        P.dma('sp', c_[:], c_d[g][n, 0:win:dil, :], w=[c_])
        qb = pb[2]
        mm(qb[:], onehot[:, n * 128:(n + 1) * 128], projs[:, b0:b0 + 512], True, True, [onehot, projs], [qb])
        P.dve(lambda e, c_=c_, qb=qb: e.tensor_tensor(out=prod[:], in0=c_[:, 0:512], in1=qb[:], op=ALU.mult), r=[c_, qb], w=[prod])
        P.dve(lambda e: e.tensor_reduce(out=s8[:], in_=prod[:].rearrange("p (h e) -> p h e", e=64), axis=AX.X, op=ALU.add), r=[prod], w=[s8])
        P.dve(lambda e, g=g: e.scalar_tensor_tensor(out=s8[:], in0=s8[:], scalar=0.125, in1=sbias[:, g * 8:(g + 1) * 8], op0=ALU.mult, op1=ALU.add), r=[s8, sbias], w=[s8])
        P.act(lambda e, p8=p8: e.activation(out=p8[:], in_=s8[:], func=A.Exp), r=[s8], w=[p8])

    def stage2(it):
        g, n = items[it]
        c_ = ck[it % 3]
        p8 = p8s[it % 2]
        nb = pb[3]; db = pb[4]
        mm(nb[0:8, :], p8[:], c_[:, 512:1024], True, True, [p8, c_], [nb])
        mm(db[0:8, 0:8], p8[:], ones_f[:, 0:8], True, True, [p8, ones_f], [db])
        P.dve(lambda e, nb=nb: e.tensor_tensor(out=msk[:, 0:512], in0=nb[0:8, :], in1=bdm[:, 0:512], op=ALU.mult), r=[nb, bdm], w=[msk])
        P.dve(lambda e, db=db: e.tensor_tensor(out=msk[:, 512:520], in0=db[0:8, 0:8], in1=bdm[:, 512:520], op=ALU.mult), r=[db, bdm], w=[msk])
        rb = pb[5]; rd = pb[6]
        mm(rb[0:NS, :], en[:, n * NS:(n + 1) * NS], msk[:, 0:512], True, True, [en, msk], [rb])
        mm(rd[0:NS, 0:8], en[:, n * NS:(n + 1) * NS], msk[:, 512:520], True, True, [en, msk], [rd])
        P.dve(lambda e, rb=rb: e.tensor_tensor(out=accn[:], in0=accn[:], in1=rb[0:NS, :], op=ALU.add), r=[accn, rb], w=[accn])
        P.dve(lambda e, rd=rd: e.tensor_tensor(out=accd[:], in0=accd[:], in1=rd[0:NS, 0:8], op=ALU.add), r=[accd, rd], w=[accd])

    stage1(0)
    for it in range(len(items)):
        if it + 1 < len(items):
            stage1(it + 1)
        stage2(it)
    att_s = sb("att_s", [NS, 512])
    gsa = sb("gsa", [NS, 512])
    P.dve(lambda e: e.reciprocal(out=accd[:], in_=accd[:]), r=[accd], w=[accd])
    for h in range(8):
        hs = slice(h * 64, (h + 1) * 64)
        P.dve(lambda e, h=h, hs=hs: e.tensor_scalar(out=att_s[:, hs], in0=accn[:, hs], scalar1=accd[:, h:h + 1], scalar2=None, op0=ALU.mult), r=[accn, accd], w=[att_s])
    P.act(lambda e: e.activation(out=gsa[:], in_=projs[:, OFF_G:OFF_G + 512], func=A.Silu), r=[projs], w=[gsa])
    P.dve(lambda e: e.tensor_tensor(out=att_b[:], in0=att_s[:], in1=gsa[:], op=ALU.mult), r=[att_s, gsa], w=[att_b])
    phB.__exit__(None, None, None)
    phC = phase(); phC.__enter__()

    lb = sb("lb", [NS, 1024]); oml = sb("oml", [NS, 1024])
    f_t = sb("f_t", [NS, 1024]); hk_t = sb("hk_t", [NS, 1024]); hq_t = sb("hq_t", [NS, 1024])
    P.dve(lambda e: e.tensor_tensor(out=lb[:], in0=lblr[:, 0:1024], in1=lblr[:, 1024:2048], op=ALU.subtract), r=[lblr], w=[lb])
    P.act(lambda e: e.activation(out=lb[:], in_=lb[:], func=A.Sigmoid), r=[lb], w=[lb])
    P.dve(lambda e: e.tensor_scalar(out=oml[:], in0=lb[:], scalar1=-1.0, scalar2=1.0, op0=ALU.mult, op1=ALU.add), r=[lb], w=[oml])
    P.act(lambda e: e.activation(out=f_t[:], in_=projs[:, OFF_F:OFF_F + 1024], func=A.Sigmoid), r=[projs], w=[f_t])
    P.dve(lambda e: e.tensor_tensor(out=f_t[:], in0=f_t[:], in1=oml[:], op=ALU.mult), r=[f_t, oml], w=[f_t])
    P.dve(lambda e: e.tensor_tensor(out=hk_t[:], in0=oml[:], in1=f_t[:], op=ALU.subtract), r=[oml, f_t], w=[hk_t])
    P.dve(lambda e: e.tensor_tensor(out=f_t[:], in0=f_t[:], in1=lb[:], op=ALU.add), r=[f_t, lb], w=[f_t])
    P.act(lambda e: e.activation(out=hq_t[:], in_=projs[:, OFF_Q:OFF_Q + 1024], func=A.Silu), r=[projs], w=[hq_t])
    P.dve(lambda e: e.tensor_scalar(out=hq_t[:], in0=hq_t[:], scalar1=float(128 ** -0.5), scalar2=None, op0=ALU.mult), r=[hq_t], w=[hq_t])
    fT = sb("fT", [128, 8, NS]); hkT = sb("hkT", [128, 8, NS]); hqT = sb("hqT", [128, 8, NS])
    for src, dst, bank in ((f_t, fT, pb[0]), (hk_t, hkT, pb[1]), (hq_t, hqT, pb[2])):
        for h in range(8):
            P.pe(lambda e, h=h, src=src, bank=bank: e.transpose(out=bank[:, h * NS:(h + 1) * NS], in_=src[0:NS, h * 128:(h + 1) * 128], identity=ident_f[0:NS, 0:NS]), r=[src, ident_f], w=[bank])
        P.dve(lambda e, dst=dst, bank=bank: e.tensor_copy(out=dst[:].rearrange("p h n -> p (h n)"), in_=bank[:, 0:8 * NS]), r=[bank], w=[dst])
    Qm = sb("Qm", [128, 8, NS, NS])
    P.pool(lambda e: e.memset(Qm[:].rearrange("p h a b -> p (h a b)"), 0.0), w=[Qm])
    for n in range(NS):
        P.dve(lambda e, n=n: e.tensor_copy(out=Qm[:, :, n, n], in_=hqT[:, :, n]), r=[hqT], w=[Qm])
    oacc = sb("oacc", [NS, 1024])
    P.pool(lambda e: e.memset(oacc[:], 0.0), w=[oacc])
    S0 = [sb("S0_%d" % i, [128, 8, 128]) for i in range(2)]
    Sn = [sb("Sn_%d" % i, [128, 8, 128]) for i in range(2)]
    tmpk = sb("tmpk", [128, 128])
    def hstage1(n):
        s0 = S0[n % 2]; sn = Sn[n % 2]
        P.dma('sp', s0[:], sh_d[n].rearrange("h k v -> k h v"), w=[s0])
        ib = (pb[3], pb[4])
        for half in range(2):
            mm(ib[half][:], onehot[:, n * 128:(n + 1) * 128], projs[:, OFF_I + half * 512:OFF_I + (half + 1) * 512], True, True, [onehot, projs], [ib[half]])
        for h in range(8):
            src = ib[h // 4][:, (h % 4) * 128:(h % 4 + 1) * 128]
            P.dve(lambda e, src=src, h=h, n=n: e.tensor_scalar(out=tmpk[:], in0=src, scalar1=hkT[:, h, n:n + 1], scalar2=None, op0=ALU.mult), r=[ib[h // 4], hkT], w=[tmpk])
            P.dve(lambda e, h=h, n=n, s0=s0, sn=sn: e.scalar_tensor_tensor(out=sn[:, h, :], in0=s0[:, h, :], scalar=fT[:, h, n:n + 1], in1=tmpk[:], op0=ALU.mult, op1=ALU.add), r=[s0, fT, tmpk], w=[sn])
        P.dma('sp', sts_d[n].rearrange("h k v -> k h v"), sn[:], r=[sn], sem_on=sn)
        if sn not in out_bufs:
            out_bufs.append(sn)

    def hstage2(n):
        sn = Sn[n % 2]
        ob = (pb[5], pb[6])
        for h in range(8):
            mm(ob[h // 4][0:NS, (h % 4) * 128:(h % 4 + 1) * 128], Qm[:, h, n, :], sn[:, h, :], True, True, [Qm, sn], [ob[h // 4]])
        for half in range(2):
            P.dve(lambda e, half=half, ob=ob: e.tensor_tensor(out=oacc[:, half * 512:(half + 1) * 512], in0=oacc[:, half * 512:(half + 1) * 512], in1=ob[half][0:NS, :], op=ALU.add), r=[oacc, ob[half]], w=[oacc])

    hstage1(0)
    for n in range(NS):
        if n + 1 < NS:
            hstage1(n + 1)
        hstage2(n)
    o2s = sb("o2s", [NS, 1024]); ssh = sb("ssh", [NS, 8]); rsh = sb("rsh", [NS, 8])
    gsh = sb("gsh", [NS, 1024])
    P.act(lambda e: e.activation(out=o2s[:], in_=oacc[:], func=A.Square), r=[oacc], w=[o2s])
    P.dve(lambda e: e.tensor_reduce(out=ssh[:], in_=o2s[:].rearrange("p (h v) -> p h v", v=128), axis=AX.X, op=ALU.add), r=[o2s], w=[ssh])
    P.act(lambda e: e.activation(out=rsh[:], in_=ssh[:], func=A.Ln, bias=epsc[0:NS, 0:1], scale=1.0 / 128), r=[ssh, epsc], w=[rsh])
    P.act(lambda e: e.activation(out=rsh[:], in_=rsh[:], func=A.Exp, scale=-0.5), r=[rsh], w=[rsh])
    P.act(lambda e: e.activation(out=gsh[:], in_=projs[:, OFF_HG:OFF_HG + 1024], func=A.Silu), r=[projs], w=[gsh])
    for h in range(8):
        hs = slice(h * 128, (h + 1) * 128)
        P.dve(lambda e, h=h, hs=hs: e.scalar_tensor_tensor(out=o2s[:, hs], in0=oacc[:, hs], scalar=rsh[:, h:h + 1], in1=hnwr[:, hs], op0=ALU.mult, op1=ALU.mult), r=[oacc, rsh, hnwr], w=[o2s])
    P.dve(lambda e: e.tensor_tensor(out=ho_b[:], in0=o2s[:], in1=gsh[:], op=ALU.mult), r=[o2s, gsh], w=[ho_b])
    phC.__exit__(None, None, None)

    attTs = sb("attTs", [128, 4, NS], BF16); hoTs = sb("hoTs", [128, 8, NS], BF16)
    to_fm_bf(attTs, att_b, 4)
    to_fm_bf(hoTs, ho_b, 8)
    gas = sb("gas", [NS, 1024]); gbs = sb("gbs", [NS, 1024])
    P.act(lambda e: e.activation(out=gas[:], in_=projs[:, OFF_MA:OFF_MA + 1024], func=A.Sigmoid), r=[projs], w=[gas])
    P.act(lambda e: e.activation(out=gbs[:], in_=projs[:, OFF_MB:OFF_MB + 1024], func=A.Sigmoid), r=[projs], w=[gbs])
    mg = sb("mg", [NS, 1024]); mg2 = sb("mg2", [NS, 1024]); mgb = sb("mgb", [NS, 1024], BF16)
    wa_s = [sb("wa_s%d" % i, [128, 4, 128], BF16) for i in range(2)]
    wh_s = [sb("wh_s%d" % i, [128, 8, 128], BF16) for i in range(2)]
    for jj in range(8):
        wa_ = wa_s[jj % 2]; wh_ = wh_s[jj % 2]
        P.dma('sp', wa_[:], wapb_t[jj], r=[wapb], w=[wa_], sem_on=wa_)
        P.dma('sp', wh_[:], whpb_t[jj], r=[whpb], w=[wh_], sem_on=wh_)
        js = slice(jj * 128, (jj + 1) * 128)
        b1 = pb[0]; b2 = pb[1]
        for hp in range(4):
            mm(b1[0:NS, 0:128], attTs[:, hp, :], wa_[:, hp, :], hp == 0, hp == 3, [attTs, wa_], [b1])
        for h in range(8):
            mm(b2[0:NS, 0:128], hoTs[:, h, :], wh_[:, h, :], h == 0, h == 7, [hoTs, wh_], [b2])
        P.dve(lambda e, js=js, b1=b1: e.tensor_tensor(out=mg[:, js], in0=gas[:, js], in1=b1[0:NS, 0:128], op=ALU.mult), r=[gas, b1], w=[mg])
        P.dve(lambda e, js=js, b2=b2: e.tensor_tensor(out=mg2[:, js], in0=gbs[:, js], in1=b2[0:NS, 0:128], op=ALU.mult), r=[gbs, b2], w=[mg2])
    P.dve(lambda e: e.tensor_tensor(out=mgb[:], in0=mg[:], in1=mg2[:], op=ALU.add), r=[mg, mg2], w=[mgb])
    mgT = sb("mgT", [128, 8, NS], BF16)
    to_fm_bf(mgT, mgb, 8)
    xr_s = sb("xr_s", [NS, D]); fnws = sb("fnws", [NS, D]); ysb = sb("ysb", [NS, D])
    P.dma('sp', fnws[:], L['fnw_d'][0:NS, :], w=[fnws])
    for half in range(2):
        wsl = L['wsl'][L['wsl_i'][0] % 3]
        L['wsl_i'][0] += 1
        P.dma('sp', wsl[:], woutb_t[half], r=[woutb], w=[wsl], sem_on=wsl)
        bank = pb[2 + half]
        for c in range(8):
            mm(bank[0:NS, :], mgT[:, c, :], wsl[:, c, :], c == 0, c == 7, [mgT, wsl], [bank])
        P.dve(lambda e, half=half, bank=bank: e.tensor_tensor(out=xr_s[:, half * 512:(half + 1) * 512], in0=bank[0:NS, :], in1=xs[:, half * 512:(half + 1) * 512], op=ALU.add), r=[bank, xs], w=[xr_s])
    rms_rows(ysb, xr_s, fnws)
    P.dma('sp', ys_d, ysb[:], r=[ysb], sem_on=ysb)
    out_bufs.append(ysb)


def _consts():
    c = {}
    c["ident"] = np.eye(128, dtype=np.float32)
    n = 24
    slopes = (2.0 ** (-8.0 * np.arange(1, n + 1) / n)).astype(np.float32).reshape(3, 8)
    k = np.arange(128)[:, None].astype(np.float32)
    q = np.arange(128)[None, :].astype(np.float32)
    eb = np.zeros((128, 12, 4, 128), np.float32)
    for g, (win, d) in enumerate(GROUPS):
        for hp in range(4):
            for hh in range(2):
                s = slopes[g, hp * 2 + hh]
                cur = np.where(q >= k, np.exp(-s * d * np.maximum(q - k, 0.0)), 0.0)
                prev = np.where(k >= q, np.exp(-s * d * (q + 128 - k)), 0.0)
                eb[:, g * 4 + hp, 2 * hh, :] = cur
                eb[:, g * 4 + hp, 2 * hh + 1, :] = prev
    c["eb"] = eb.reshape(128, 12 * 512)
    p = np.arange(128)[:, None] % 64
    t = np.arange(64)[None, :]
    c["hmask"] = (t >= p).astype(np.float32)
    rs = np.ones((128, 512), np.float32)
    rs[:, ::64] = 0.0
    c["rsm"] = rs
    sel = np.zeros((128, 64), np.float32)
    sel[64, :] = 1.0
    c["sel"] = sel
    sb = np.zeros((128, 24), np.float32)
    for g, (win, d) in enumerate(GROUPS):
        for h in range(8):
            sb[:, g * 8 + h] = -slopes[g, h] * d * (128 - np.arange(128))
    c["sbias"] = sb
    oh = np.zeros((NS, NS, 128), np.float32)
    for i in range(NS):
        oh[i, i, :] = 1.0
    c["onehot"] = oh.reshape(NS, NS * 128)
    bdm = np.zeros((8, 520), np.float32)
    for h in range(8):
        bdm[h, h * 64:(h + 1) * 64] = 1.0
        bdm[h, 512 + h] = 1.0
    c["bdm"] = bdm
    en = np.zeros((8, NS, NS), np.float32)
    for i in range(NS):
        en[:, i, i] = 1.0
    c["en"] = en.reshape(8, NS * NS)
    return c


_CACHE = {}


def kernel(x_prompt, x_sample, cache_kv_w128, cache_kv_w512, cache_kv_w2048, state_hgrn,
           norm_w, w_in, w_att_proj, w_hg_proj, w_out, hg_norm_w, hg_lb_logits, final_norm_w,
           _nst=NST, _cores=8, _prompt=True, _sample=True):
    f = lambda a: np.ascontiguousarray(np.asarray(a, dtype=np.float32))
    key = (_nst, _prompt, _sample)
    if "nc" not in _CACHE or _CACHE.get("key") != key:
        _CACHE["nc"] = build_program(do_prompt=_prompt, do_sample=_sample, nst=_nst)
        _CACHE["key"] = key
    nc = _CACHE["nc"]
    cst = _consts()
    shared = dict(cst)
    shared["w_in"] = f(w_in[0])
    shared["wap"] = f(w_att_proj[0])
    shared["whp"] = f(w_hg_proj[0])
    shared["wout"] = f(w_out[0])
    shared["nw"] = f(np.asarray(norm_w[0]).reshape(8, 128).T)
    shared["nwr"] = f(np.broadcast_to(np.asarray(norm_w[0])[None, :], (128, D)))
    shared["hnw"] = f(np.asarray(hg_norm_w[0]).reshape(128, 1))
    shared["hnwr"] = f(np.broadcast_to(np.tile(np.asarray(hg_norm_w[0]), 8)[None, :], (NS, 1024)))
    lg = np.asarray(hg_lb_logits)
    shared["lbl"] = f(lg.reshape(2, 8, 128).transpose(2, 0, 1).reshape(128, 16))
    shared["lblr"] = f(np.broadcast_to(lg.reshape(1, 2048), (NS, 2048)))
    shared["fnw"] = f(np.broadcast_to(np.asarray(final_norm_w)[None, :], (128, D)))
    in_maps = []
    for i in range(_cores):
        m = dict(shared)
        m["x"] = f(x_prompt[i])
        sl = slice(NS * i, NS * (i + 1))
        m["xs"] = f(np.asarray(x_sample)[sl, 0, :])
        m["c128"] = f(np.asarray(cache_kv_w128)[0, sl].reshape(NS, 128, 1024))
        m["c512"] = f(np.asarray(cache_kv_w512)[0, sl].reshape(NS, 512, 1024))
        m["c2048"] = f(np.asarray(cache_kv_w2048)[0, sl].reshape(NS, 2048, 1024))
        m["sh"] = f(np.asarray(state_hgrn)[0, sl])
        in_maps.append(m)
    res = run_bass_kernel_spmd(nc, in_maps, core_ids=list(range(_cores)))
    R = res.results
    n = _cores
    y = np.stack([R[i]["y"] for i in range(n)], 0)
    ys = np.concatenate([R[i]["ys"] for i in range(n)], 0).reshape(n * NS, 1, D)
    kvp = [np.stack([R[i][k] for i in range(n)], 0).reshape(1, n, w, 2, 8, 64)
           for k, w in (("kv128", 128), ("kv512", 512), ("kv2048", 2048))]
    stp = np.stack([R[i]["stp"] for i in range(n)], 0).reshape(1, n, 8, 128, 128)
    kvs = [np.concatenate([R[i][k] for i in range(n)], 0).reshape(1, n * NS, 1, 2, 8, 64)
           for k in ("kvs128", "kvs512", "kvs2048")]
    sts = np.concatenate([R[i]["sts"] for i in range(n)], 0).reshape(1, n * NS, 8, 128, 128)
    return (y, ys, kvp[0], kvp[1], kvp[2], stp, kvs[0], kvs[1], kvs[2], sts)
```
